# Optimizing a Trainium2 kernel written in Bass

```python
import math
import jax, jax.numpy as jnp
from jax import lax
import numpy as np


D_MODEL = 1024
BATCH = 8
SEQ = 4096
DEPTH = 2

PLE_DIM = 256
MLA_HEADS = 4
MLA_NOPE = 64
MLA_ROPE = 32
MLA_V = 64
MLA_Q_RANK = 192
MLA_KV_RANK = 128
MLA_OUT = MLA_HEADS * MLA_V
FOX_HEADS = 4
FOX_HEAD_DIM = 64
FOX_OUT = FOX_HEADS * FOX_HEAD_DIM
LRU_WIDTH = 512
LRU_BLOCKS = 8
LRU_BLOCK = LRU_WIDTH // LRU_BLOCKS
LRU_CONV = 4
LRU_C = 8.0
D_MIX = MLA_OUT + FOX_OUT + LRU_WIDTH
IN_SIZES = (MLA_Q_RANK, MLA_KV_RANK, MLA_ROPE, FOX_OUT, FOX_OUT, FOX_OUT, FOX_HEADS, LRU_WIDTH, LRU_WIDTH)
D_IN = MLA_Q_RANK + MLA_KV_RANK + MLA_ROPE + 3 * FOX_OUT + FOX_HEADS + 2 * LRU_WIDTH
D_FF = 2816
FFN_CONV = 3
ROPE_THETA = 10000.0
EPS = 1e-6
Q_BLOCK = 128

kernel_name = 'hybrid_mla_fox_rglru_convffn_ple'


def _offsets(sizes):
    out, acc = [], 0
    for s in sizes[:-1]:
        acc += s
        out.append(acc)
    return out


def rms_norm(x, g):
    xf = x.astype(jnp.float32)
    y = xf * lax.rsqrt(jnp.mean(xf * xf, axis=-1, keepdims=True) + EPS)
    return (y * g.astype(jnp.float32)).astype(x.dtype)


def rope(x, positions):
    half = x.shape[-1] // 2
    freqs = ROPE_THETA ** (-jnp.arange(half, dtype=jnp.float32) / half)
    ang = positions.astype(jnp.float32)[..., None] * freqs
    ang = ang.reshape(ang.shape[:2] + (1,) * (x.ndim - 3) + (half,))
    cos, sin = jnp.cos(ang), jnp.sin(ang)
    xf = x.astype(jnp.float32)
    x1, x2 = xf[..., :half], xf[..., half:]
    return jnp.concatenate([x1 * cos - x2 * sin, x2 * cos + x1 * sin], axis=-1).astype(x.dtype)


def causal_dwconv(x, w, b):
    K = w.shape[0]
    S = x.shape[1]
    xp = jnp.pad(x, ((0, 0), (K - 1, 0), (0, 0)))
    out = b + xp[:, 0:S] * w[0]
    for k in range(1, K):
        out = out + xp[:, k:k + S] * w[k]
    return out


def causal_block_attention(q, k, v, scale, decay=None):
    B, S, H, dk = q.shape
    dv = v.shape[-1]
    nb = S // Q_BLOCK
    qb = q.reshape(B, nb, Q_BLOCK, H, dk).transpose(1, 0, 3, 2, 4)
    kh = k.transpose(0, 2, 1, 3)
    vh = v.transpose(0, 2, 1, 3)
    k_pos = jnp.arange(S)
    blk_idx = jnp.arange(nb)
    ck = None if decay is None else decay.transpose(0, 2, 1)

    def block(q_blk, c_blk, idx):
        s = jnp.einsum('bhqd,bhkd->bhqk', q_blk, kh, preferred_element_type=jnp.float32) * scale
        if c_blk is not None:
            s = s + c_blk[..., :, None] - ck[..., None, :]
        q_pos = idx * Q_BLOCK + jnp.arange(Q_BLOCK)
        s = jnp.where(k_pos[None, :] <= q_pos[:, None], s, -jnp.inf)
        pr = jax.nn.softmax(s, axis=-1).astype(vh.dtype)
        return jnp.einsum('bhqk,bhkd->bhqd', pr, vh)

    if decay is None:
        out = lax.map(lambda a: block(a[0], None, a[1]), (qb, blk_idx))
    else:
        cb = ck.reshape(B, H, nb, Q_BLOCK).transpose(2, 0, 1, 3)
        out = lax.map(lambda a: block(a[0], a[1], a[2]), (qb, cb, blk_idx))
    return out.transpose(1, 0, 3, 2, 4).reshape(B, S, H, dv)


def _linear_combine(left, right):
    a_l, b_l = left
    a_r, b_r = right
    return a_l * a_r, a_r * b_l + b_r


def hybrid_mixer(xn, positions, w_in, g_qc, w_uq, g_kvc, w_ukv, b_f, lru_conv_w, lru_conv_b,
                 w_r, b_r, w_i, b_i, lru_lambda, g_out, w_o):
    B, S, _ = xn.shape
    z = xn @ w_in
    q_c, kv_c, k_r, fq, fk, fv, f_logit, lx, lg = jnp.split(z, _offsets(IN_SIZES), axis=-1)

    q = (rms_norm(q_c, g_qc) @ w_uq).reshape(B, S, MLA_HEADS, MLA_NOPE + MLA_ROPE)
    q = jnp.concatenate([q[..., :MLA_NOPE], rope(q[..., MLA_NOPE:], positions)], axis=-1)
    kv = (rms_norm(kv_c, g_kvc) @ w_ukv).reshape(B, S, MLA_HEADS, MLA_NOPE + MLA_V)
    k_nope, v_mla = kv[..., :MLA_NOPE], kv[..., MLA_NOPE:]
    k_rope = rope(k_r, positions)
    k = jnp.concatenate([k_nope, jnp.broadcast_to(k_rope[:, :, None, :], (B, S, MLA_HEADS, MLA_ROPE))], axis=-1)
    o_mla = causal_block_attention(q, k, v_mla, (MLA_NOPE + MLA_ROPE) ** -0.5).reshape(B, S, MLA_OUT)

    log_f = jax.nn.log_sigmoid(f_logit.astype(jnp.float32) + b_f.astype(jnp.float32))
    c = jnp.cumsum(log_f, axis=1)
    o_fox = causal_block_attention(fq.reshape(B, S, FOX_HEADS, FOX_HEAD_DIM),
                                   fk.reshape(B, S, FOX_HEADS, FOX_HEAD_DIM),
                                   fv.reshape(B, S, FOX_HEADS, FOX_HEAD_DIM),
                                   FOX_HEAD_DIM ** -0.5, decay=c).reshape(B, S, FOX_OUT)

    xc = causal_dwconv(lx, lru_conv_w, lru_conv_b)
    xblk = xc.reshape(B, S, LRU_BLOCKS, LRU_BLOCK)
    r = jax.nn.sigmoid(jnp.einsum('bsnc,ncd->bsnd', xblk, w_r).reshape(B, S, LRU_WIDTH) + b_r)
    i = jax.nn.sigmoid(jnp.einsum('bsnc,ncd->bsnd', xblk, w_i).reshape(B, S, LRU_WIDTH) + b_i)
    log_a = -LRU_C * r.astype(jnp.float32) * jax.nn.softplus(-lru_lambda.astype(jnp.float32))
    a_t = jnp.exp(log_a)
    bx = jnp.sqrt(-jnp.expm1(2.0 * log_a)) * (i * xc).astype(jnp.float32)
    _, h = lax.associative_scan(_linear_combine, (a_t, bx), axis=1)
    o_lru = h.astype(xn.dtype) * jax.nn.gelu(lg)

    o = jnp.concatenate([
        rms_norm(o_mla, g_out[:MLA_OUT]),
        rms_norm(o_fox, g_out[MLA_OUT:MLA_OUT + FOX_OUT]),
        rms_norm(o_lru, g_out[MLA_OUT + FOX_OUT:]),
    ], axis=-1)
    return o @ w_o


def conv_ffn(xn, w_up, ffn_conv_w, ffn_conv_b, w_down):
    u = causal_dwconv(xn @ w_up, ffn_conv_w, ffn_conv_b)
    g, v = jnp.split(u, 2, axis=-1)
    return (jax.nn.silu(g) * v) @ w_down


def per_layer_embedding(h, p_i, g_ple, w_ple_gate, w_ple_proj):
    return jax.nn.sigmoid(rms_norm(h, g_ple) @ w_ple_gate) * (p_i @ w_ple_proj)


def setup_inputs(seed: int = 0) -> dict:
    key = jax.random.key(seed)
    ks = iter(jax.random.split(key, 40))

    def nrm(shape, scale):
        return jax.random.normal(next(ks), shape, jnp.float32) * scale

    def gain(shape):
        return 1.0 + nrm(shape, 0.02)

    x = nrm((BATCH, SEQ, D_MODEL), 1.0)
    p = nrm((DEPTH, BATCH, SEQ, PLE_DIM), 1.0)
    offset = jax.random.randint(next(ks), (BATCH, 1), 0, 1024, dtype=jnp.int32)
    positions = (offset + jnp.arange(SEQ, dtype=jnp.int32)[None, :]).astype(jnp.int32)

    u = jax.random.uniform(next(ks), (DEPTH, LRU_WIDTH), jnp.float32, 0.9, 0.999)
    s = u ** (1.0 / LRU_C)
    lru_lambda = jnp.log(s) - jnp.log1p(-s)

    return {
        'x': x,
        'p': p,
        'positions': positions,
        'g_mix': gain((DEPTH, D_MODEL)),
        'w_in': nrm((DEPTH, D_MODEL, D_IN), D_MODEL ** -0.5),
        'g_qc': gain((DEPTH, MLA_Q_RANK)),
        'w_uq': nrm((DEPTH, MLA_Q_RANK, MLA_HEADS * (MLA_NOPE + MLA_ROPE)), MLA_Q_RANK ** -0.5),
        'g_kvc': gain((DEPTH, MLA_KV_RANK)),
        'w_ukv': nrm((DEPTH, MLA_KV_RANK, MLA_HEADS * (MLA_NOPE + MLA_V)), MLA_KV_RANK ** -0.5),
        'b_f': jax.random.uniform(next(ks), (DEPTH, FOX_HEADS), jnp.float32, 1.0, 4.0),
        'lru_conv_w': nrm((DEPTH, LRU_CONV, LRU_WIDTH), LRU_CONV ** -0.5),
        'lru_conv_b': nrm((DEPTH, LRU_WIDTH), 0.02),
        'w_r': nrm((DEPTH, LRU_BLOCKS, LRU_BLOCK, LRU_BLOCK), LRU_BLOCK ** -0.5),
        'b_r': nrm((DEPTH, LRU_WIDTH), 0.02),
        'w_i': nrm((DEPTH, LRU_BLOCKS, LRU_BLOCK, LRU_BLOCK), LRU_BLOCK ** -0.5),
        'b_i': nrm((DEPTH, LRU_WIDTH), 0.02),
        'lru_lambda': lru_lambda,
        'g_out': gain((DEPTH, D_MIX)),
        'w_o': nrm((DEPTH, D_MIX, D_MODEL), D_MIX ** -0.5),
        'g_ffn': gain((DEPTH, D_MODEL)),
        'w_up': nrm((DEPTH, D_MODEL, 2 * D_FF), D_MODEL ** -0.5),
        'ffn_conv_w': nrm((DEPTH, FFN_CONV, 2 * D_FF), FFN_CONV ** -0.5),
        'ffn_conv_b': nrm((DEPTH, 2 * D_FF), 0.02),
        'w_down': nrm((DEPTH, D_FF, D_MODEL), D_FF ** -0.5),
        'g_ple': gain((DEPTH, D_MODEL)),
        'w_ple_gate': nrm((DEPTH, D_MODEL, D_MODEL), D_MODEL ** -0.5),
        'w_ple_proj': nrm((DEPTH, PLE_DIM, D_MODEL), PLE_DIM ** -0.5),
        'g_final': gain((D_MODEL,)),
    }


def reference(x, p, positions, g_mix, w_in, g_qc, w_uq, g_kvc, w_ukv, b_f, lru_conv_w, lru_conv_b,
              w_r, b_r, w_i, b_i, lru_lambda, g_out, w_o, g_ffn, w_up, ffn_conv_w, ffn_conv_b,
              w_down, g_ple, w_ple_gate, w_ple_proj, g_final):
    h = x
    for l in range(DEPTH):
        h = h + hybrid_mixer(rms_norm(h, g_mix[l]), positions, w_in[l], g_qc[l], w_uq[l], g_kvc[l],
                             w_ukv[l], b_f[l], lru_conv_w[l], lru_conv_b[l], w_r[l], b_r[l], w_i[l],
                             b_i[l], lru_lambda[l], g_out[l], w_o[l])
        h = h + conv_ffn(rms_norm(h, g_ffn[l]), w_up[l], ffn_conv_w[l], ffn_conv_b[l], w_down[l])
        h = h + per_layer_embedding(h, p[l], g_ple[l], w_ple_gate[l], w_ple_proj[l])
    return rms_norm(h, g_final)
```

```python
import contextlib
import math
import numpy as np
import concourse.bass as bass
import concourse.mybir as mybir
from concourse.bass_utils import run_bass_kernel_spmd

F32 = mybir.dt.float32
BF16 = mybir.dt.bfloat16
I32 = mybir.dt.int32
AF = mybir.ActivationFunctionType
ALU = mybir.AluOpType

S = 4096
D = 1024
TT = 512
NT = S // TT
DEPTH = 2
EPS = 1e-6
NV = 272

C_GMIX, C_GFFN, C_GPLE = 0, 8, 16
C_GQC, C_GKVC = 24, 26
C_GOUT64 = 27
C_GOUTL = 43
C_BF = 47
C_LCW = 48
C_LCB, C_BR, C_BI, C_LAM = 64, 68, 72, 76
C_FCW = 80
C_FCB = 212
C_GFIN = 256


class _Tile:
    __slots__ = ("name", "w", "r")

    def __init__(self, name):
        self.name = name
        self.w = None
        self.r = {}


class _Eng:
    def __init__(self, name, h, k, is_pe=False):
        self.name = name
        self.h = h
        self.k = k
        self.is_pe = is_pe
        self.cnt = 0
        self.waited = {}


class _DSem:
    def __init__(self, k):
        self.k = k
        self.cnt = 0


class Ctx:
    def __init__(self, nc, stack):
        self.nc = nc
        self.stack = stack
        self.sems = []
        self.dsems = []
        self.pe = _Eng("pe", nc.tensor, self._sem("s_pe"), True)
        self.act = _Eng("act", nc.scalar, self._sem("s_act"))
        self.dve = _Eng("dve", nc.vector, self._sem("s_dve"))
        self.pool = _Eng("pool", nc.gpsimd, self._sem("s_pool"))
        self.sp = _Eng("sp", nc.sync, self._sem("s_sp"))
        self.engs = [self.pe, self.act, self.dve, self.pool, self.sp]

    def _sem(self, name):
        name = f"{name}_{len(self.sems)}"
        h = self.stack.enter_context(self.nc.semaphore(name))
        self.sems.append(h)
        return len(self.sems) - 1

    def dsem(self, name):
        d = _DSem(self._sem("d_" + name))
        self.dsems.append(d)
        return d

    def _add(self, need, eng, st, kind):
        if st is None:
            return
        k, val, src = st
        if src is eng:
            if eng.is_pe or kind != "raw":
                return
        if need.get(k, 0) < val:
            need[k] = val

    def _emit_waits(self, eng, reads, writes):
        need = {}
        for t in reads:
            self._add(need, eng, t.w, "raw")
        for t in writes:
            self._add(need, eng, t.w, "waw")
            for st in t.r.values():
                self._add(need, eng, st, "war")
        for k, v in need.items():
            if eng.waited.get(k, 0) < v:
                eng.h.wait_ge(self.sems[k], v)
                eng.waited[k] = v

    def _stamp(self, st, reads, writes):
        for t in reads:
            old = t.r.get(st[0])
            if old is None or old[1] < st[1]:
                t.r[st[0]] = st
        for t in writes:
            t.w = st
            t.r = {}

    def op(self, eng, fn, reads=(), writes=(), inc=True):
        self._emit_waits(eng, reads, writes)
        ins = fn(eng.h)
        if inc:
            eng.cnt += 1
            ins.then_inc(self.sems[eng.k], 1)
            st = (eng.k, eng.cnt, eng)
        else:
            st = (eng.k, eng.cnt + 1, eng)
        self._stamp(st, reads, writes)
        return ins

    def dma(self, q, out, in_, reads, writes, ds):
        need = {}
        for t in reads:
            if t.w is not None:
                k, val, _ = t.w
                need[k] = max(need.get(k, 0), val)
        for t in writes:
            if t.w is not None:
                k, val, _ = t.w
                need[k] = max(need.get(k, 0), val)
            for (k, val, _) in t.r.values():
                need[k] = max(need.get(k, 0), val)
        for k, v in need.items():
            if q.waited.get(k, 0) < v:
                q.h.wait_ge(self.sems[k], v)
                q.waited[k] = v
        ins = q.h.dma_start(out=out, in_=in_)
        ds.cnt += 16
        ins.then_inc(self.sems[ds.k], 16)
        self._stamp((ds.k, ds.cnt, None), reads, writes)
        return ins

    def barrier(self):
        for e in self.engs:
            for f in self.engs:
                if f is not e and f.cnt > 0 and e.waited.get(f.k, 0) < f.cnt:
                    e.h.wait_ge(self.sems[f.k], f.cnt)
                    e.waited[f.k] = f.cnt
            for d in self.dsems:
                if d.cnt > 0 and e.waited.get(d.k, 0) < d.cnt:
                    e.h.wait_ge(self.sems[d.k], d.cnt)
                    e.waited[d.k] = d.cnt


def build_nc(stop_after=None, debug=False):
    nc = bass.Bass("TRN2", target_bir_lowering=False)
    EI = dict(kind="ExternalInput")
    xT = nc.dram_tensor("xT", [D, S], F32, **EI).ap()
    pT = nc.dram_tensor("pT", [DEPTH, 256, S], F32, **EI).ap()
    posd = nc.dram_tensor("pos", [128, S], I32, **EI).ap()
    w_in = nc.dram_tensor("w_in", [DEPTH, D, 2148], F32, **EI).ap()
    w_uq = nc.dram_tensor("w_uq", [DEPTH, 192, 384], F32, **EI).ap()
    w_ukv = nc.dram_tensor("w_ukv", [DEPTH, 128, 512], F32, **EI).ap()
    w_r = nc.dram_tensor("w_r", [DEPTH, 8, 64, 64], F32, **EI).ap()
    w_i = nc.dram_tensor("w_i", [DEPTH, 8, 64, 64], F32, **EI).ap()
    w_o = nc.dram_tensor("w_o", [DEPTH, D, D], F32, **EI).ap()
    w_up = nc.dram_tensor("w_up", [DEPTH, D, 5632], F32, **EI).ap()
    w_down = nc.dram_tensor("w_down", [DEPTH, 2816, D], F32, **EI).ap()
    w_pg = nc.dram_tensor("w_ple_gate", [DEPTH, D, D], F32, **EI).ap()
    w_pp = nc.dram_tensor("w_ple_proj", [DEPTH, 256, D], F32, **EI).ap()
    vecs = nc.dram_tensor("vecs", [DEPTH, 128, NV], F32, **EI).ap()
    cident = nc.dram_tensor("cident", [128, 128], F32, **EI).ap()
    cmaskd = nc.dram_tensor("cmask", [128, 128], F32, **EI).ap()
    cseld = nc.dram_tensor("csel", [128, 8, 70], F32, **EI).ap()
    cfreq = nc.dram_tensor("cfreq", [128, 2], F32, **EI).ap()
    outT = nc.dram_tensor("outT", [D, S], F32, kind="ExternalOutput").ap()
    dk = dict(kind="ExternalOutput") if debug else {}
    hA = nc.dram_tensor("hA", [D, S], F32, **dk).ap()
    xnT = nc.dram_tensor("xnT", [D, S], BF16, **dk).ap()
    omixT = nc.dram_tensor("omixT", [D, S], BF16, **dk).ap()
    ropeT = nc.dram_tensor("ropeT", [2, 128, S], F32, **dk).ap()

    hv_x = xT.rearrange("(c p) t -> p c t", p=128)
    hv_a = hA.rearrange("(c p) t -> p c t", p=128)
    hv_o = outT.rearrange("(c p) t -> p c t", p=128)
    xnv = xnT.rearrange("(c p) t -> p c t", p=128)
    omv = omixT.rearrange("(c p) t -> p c t", p=128)

    with contextlib.ExitStack() as gstack:
        k = Ctx(nc, gstack)
        pe, act, dve, pool, sp = k.pe, k.act, k.dve, k.pool, k.sp

        uid = [0]

        def sb(stack, name, shape, dt):
            uid[0] += 1
            return stack.enter_context(nc.sbuf_tensor(f"{name}_{uid[0]}", shape, dt))

        def MM(out, lhsT, rhs, start, stop, reads, writes, inc=None):
            if inc is None:
                inc = stop
            k.op(pe, lambda e: e.matmul(out, lhsT=lhsT, rhs=rhs, start=start, stop=stop),
                 reads, writes, inc)

        def ACT(out, in_, func, reads, writes, bias=None, scale=None, eng=None):
            kw = {}
            if bias is not None:
                kw["bias"] = bias
            if scale is not None:
                kw["scale"] = scale
            k.op(act, lambda e: e.activation(out=out, in_=in_, func=func, **kw), reads, writes)

        def TS(eng, out, in0, s1, s2, op0, op1, reads, writes):
            if op1 is None:
                k.op(eng, lambda e: e.tensor_scalar(out=out, in0=in0, scalar1=s1, scalar2=None, op0=op0),
                     reads, writes)
            else:
                k.op(eng, lambda e: e.tensor_scalar(out=out, in0=in0, scalar1=s1, scalar2=s2, op0=op0, op1=op1),
                     reads, writes)

        def STT(eng, out, in0, scalar, in1, op0, op1, reads, writes):
            k.op(eng, lambda e: e.scalar_tensor_tensor(out=out, in0=in0, scalar=scalar, in1=in1, op0=op0, op1=op1),
                 reads, writes)

        def TTn(eng, out, in0, in1, op, reads, writes):
            k.op(eng, lambda e: e.tensor_tensor(out=out, in0=in0, in1=in1, op=op), reads, writes)

        def CP(eng, out, in_, reads, writes):
            if eng is act:
                k.op(act, lambda e: e.activation(out=out, in_=in_, func=AF.Copy), reads, writes)
            else:
                k.op(eng, lambda e: e.tensor_copy(out=out, in_=in_), reads, writes)

        def MEMSET(eng, ap, val, writes):
            k.op(eng, lambda e: e.memset(ap, val), (), writes)

        def RECIP(out, in_, reads, writes):
            k.op(dve, lambda e: e.reciprocal(out=out, in_=in_), reads, writes)

        ONES = sb(gstack, "ONES", [128, 128], BF16)
        IDENT = sb(gstack, "IDENT", [128, 128], BF16)
        CMASK = sb(gstack, "CMASK", [128, 128], BF16)
        CST = sb(gstack, "CST", [128, 4], F32)
        VL = sb(gstack, "VL", [128, NV], F32)
        tONES, tIDENT, tCMASK, tCST, tVL = (_Tile(n) for n in ("ONES", "IDENT", "CMASK", "CST", "VL"))
        PSB = [gstack.enter_context(nc.psum_tensor(f"ps{i}", [128, 512], F32)) for i in range(8)]
        tPS = [_Tile(f"ps{i}") for i in range(8)]
        d_const = k.dsem("const")
        d_vl = k.dsem("vl")
        d_w = [k.dsem(f"w{i}") for i in range(12)]

        MEMSET(dve, ONES[:], 1.0, [tONES])
        MEMSET(dve, CST[:, 0:1], EPS, [tCST])
        MEMSET(dve, CST[:, 1:2], 1.0, [tCST])
        MEMSET(dve, CST[:, 2:3], 0.0, [tCST])
        k.dma(pool, IDENT[:], cident[:, :], [], [tIDENT], d_const)
        k.dma(pool, CMASK[:], cmaskd[:, :], [], [tCMASK], d_const)

        tH = [_Tile(f"h{t}") for t in range(NT)]
        tXN = [_Tile(f"xn{t}") for t in range(NT)]
        tOM = [[_Tile(f"om{g}_{t}") for t in range(NT)] for g in range(3)]
        tROPE = _Tile("rope")
        tNONE = _Tile("ext")

        def cols(t):
            return slice(t * TT, (t + 1) * TT)

        def rmsnorm(srcs, rows, gcols, nfeat, outs, SQ, tSQ, RS, tRS, RSTD, tRSTD, psn, src_tiles, out_tiles,
                    sq_eng=None):
            C = len(srcs)
            for c in range(C):
                ACT(SQ[0:rows[c], c, :], srcs[c], AF.Square, src_tiles, [tSQ])
            for c in range(C):
                MM(PSB[psn][:, :], ONES[0:rows[c], :], SQ[0:rows[c], c, :], c == 0, c == C - 1,
                   [tSQ, tONES], [tPS[psn]])
            ACT(RS[:, :], PSB[psn][:, :], AF.Sqrt, [tPS[psn], tCST], [tRS], bias=CST[:, 0:1], scale=1.0 / nfeat)
            RECIP(RSTD[:, :], RS[:, :], [tRS], [tRSTD])
            for c in range(C):
                STT(dve, outs[c], srcs[c], gcols[c], RSTD[0:rows[c], :], ALU.mult, ALU.mult,
                    src_tiles + [tRSTD, tVL], out_tiles)

        with contextlib.ExitStack() as st:
            POSI = sb(st, "POSI", [128, S], I32)
            POSF = sb(st, "POSF", [128, S], F32)
            ANG = sb(st, "ANG", [128, S], F32)
            YT = sb(st, "YT", [128, S], F32)
            FRQ = sb(st, "FRQ", [128, 2], F32)
            CSB = sb(st, "CSB", [128, S], F32)
            CSB2 = sb(st, "CSB2", [128, S], F32)
            tCSB, tCSB2 = _Tile("CSB"), _Tile("CSB2")
            tPOSI, tPOSF, tANG, tYT, tFRQ = (_Tile(n) for n in ("POSI", "POSF", "ANG", "YT", "FRQ"))
            d_p = k.dsem("pro")
            R = slice(64, 96)
            k.dma(sp, POSI[R, :], posd[R, :], [], [tPOSI], d_p)
            k.dma(sp, FRQ[:], cfreq[:, :], [], [tFRQ], d_p)
            CP(dve, POSF[R, :], POSI[R, :], [tPOSI], [tPOSF])
            TS(dve, ANG[R, :], POSF[R, :], FRQ[R, 0:1], None, ALU.mult, None, [tPOSF, tFRQ], [tANG])
            TS(dve, ANG[R, :], ANG[R, :], 1.0 / (2 * math.pi), None, ALU.mult, None, [tANG], [tANG])

            def reduce_turns(shift):
                TS(dve, YT[R, :], ANG[R, :], shift, None, ALU.add, None, [tANG], [tYT])
                CP(dve, POSI[R, :], YT[R, :], [tYT], [tPOSI])
                CP(dve, POSF[R, :], POSI[R, :], [tPOSI], [tPOSF])
                TTn(dve, YT[R, :], YT[R, :], POSF[R, :], ALU.subtract, [tYT, tPOSF], [tYT])
                TS(dve, POSF[R, :], YT[R, :], 0.5, None, ALU.is_gt, None, [tYT], [tPOSF])
                TTn(dve, YT[R, :], YT[R, :], POSF[R, :], ALU.subtract, [tYT, tPOSF], [tYT])
                TS(dve, POSF[R, :], YT[R, :], -0.5, None, ALU.is_lt, None, [tYT], [tPOSF])
                TTn(dve, YT[R, :], YT[R, :], POSF[R, :], ALU.add, [tYT, tPOSF], [tYT])

            reduce_turns(0.25)
            ACT(CSB[R, :], YT[R, :], AF.Sin, [tYT], [tCSB], scale=2 * math.pi)
            k.dma(sp, ropeT[0, R, :], CSB[R, :], [tCSB], [tROPE], d_p)
            reduce_turns(0.0)
            ACT(CSB2[R, :], YT[R, :], AF.Sin, [tYT, tFRQ], [tCSB2], scale=FRQ[R, 1:2])
            k.dma(sp, ropeT[1, R, :], CSB2[R, :], [tCSB2], [tROPE], d_p)
            k.barrier()
        if stop_after == "rope":
            k.barrier()
            return nc

        def attention(l, QT, tQT, KT, tKT, VA, tVA, kd, scale, grp, gcol0):
            with contextlib.ExitStack() as st:
                PT = [sb(st, f"PT{i}", [128, 512], BF16) for i in range(3)]
                tPT = [_Tile(f"PT{i}") for i in range(3)]
                RC = sb(st, "RC", [128, 512], F32)
                OH = sb(st, "OH", [128, 4, 512], F32)
                SQ = sb(st, "SQa", [128, 4, 512], BF16)
                RS = sb(st, "RSa", [128, 512], F32)
                RSTD = sb(st, "RSTDa", [128, 512], F32)
                OMX = [sb(st, f"OMXa{i}", [128, 4, 512], BF16) for i in range(2)]
                tRC, tOH, tSQ, tRS, tRSTD = (_Tile(n) for n in ("RC", "OH", "SQa", "RSa", "RSTDa"))
                tOMX = [_Tile("OMXa0"), _Tile("OMXa1")]
                d_st = [k.dsem("ast0"), k.dsem("ast1")]
                sidx = 0
                for j in range(NT):
                    for h in range(4):
                        nk = 4 * j + 4
                        ob = 3 + (h % 2)
                        for i in range(nk):
                            r = i - 4 * j
                            c0 = 128 * r if r >= 0 else 0
                            pb = sidx % 3
                            sidx += 1
                            ksl = slice(i * 128, (i + 1) * 128)
                            q0 = j * TT
                            if r >= 0:
                                MM(PSB[pb][:, c0:c0 + 128], KT[0:kd, h, ksl], QT[0:kd, h, q0 + c0:q0 + c0 + 128],
                                   True, False, [tKT, tQT], [tPS[pb]], inc=False)
                                MM(PSB[pb][:, c0:c0 + 128], IDENT[:, :], CMASK[:, :], False, True,
                                   [tIDENT, tCMASK], [tPS[pb]], inc=(c0 + 128 == 512))
                                if c0 + 128 < 512:
                                    MM(PSB[pb][:, c0 + 128:512], KT[0:kd, h, ksl], QT[0:kd, h, q0 + c0 + 128:q0 + 512],
                                       True, True, [tKT, tQT], [tPS[pb]])
                            else:
                                MM(PSB[pb][:, :], KT[0:kd, h, ksl], QT[0:kd, h, q0:q0 + 512], True, True,
                                   [tKT, tQT], [tPS[pb]])
                            ACT(PT[pb][:, c0:512], PSB[pb][:, c0:512], AF.Exp, [tPS[pb]], [tPT[pb]], scale=scale)
                            MM(PSB[ob][:, c0:512], VA[:, i, h, :], PT[pb][:, c0:512], i == 0, i == nk - 1,
                               [tVA, tPT[pb]], [tPS[ob]])
                        RECIP(RC[64:128, :], PSB[ob][64:128, :], [tPS[ob]], [tRC])
                        TTn(dve, OH[0:64, h, :], PSB[ob][0:64, :], RC[64:128, :], ALU.mult, [tPS[ob], tRC], [tOH])
                    om = OMX[j % 2]
                    tom = tOMX[j % 2]
                    rmsnorm([OH[0:64, h, :] for h in range(4)], [64] * 4,
                            [VL[0:64, C_GOUT64 + gcol0 + h:C_GOUT64 + gcol0 + h + 1] for h in range(4)], 256.0,
                            [om[0:64, h, :] for h in range(4)], SQ, tSQ, RS, tRS, RSTD, tRSTD, 5, [tOH], [tom])
                    r0 = 256 * grp
                    k.dma(sp, omixT[r0:r0 + 256, cols(j)].rearrange("(h p) t -> p h t", p=64), om[0:64, :, :],
                          [tom], [tOM[grp][j]], d_st[j % 2])
                k.barrier()

        for l in range(DEPTH):
            hsrc = hv_x if l == 0 else hv_a
            tHs = [tNONE] * NT if l == 0 else tH
            k.dma(sp, VL[:], vecs[l, :, :], [], [tVL], d_vl)
            win = w_in[l].rearrange("(c p) n -> p c n", p=128)

            with contextlib.ExitStack() as st:
                QM = sb(st, "QM", [128, 4, S], BF16)
                KM = sb(st, "KM", [128, 4, S], BF16)
                VA = sb(st, "VA", [128, 32, 4, 128], BF16)
                tQM, tKM, tVA = _Tile("QM"), _Tile("KM"), _Tile("VA")
                MEMSET(pool, VA[:, :, :, 64:128], 1.0, [tVA])
                with contextlib.ExitStack() as s1:
                    HT = sb(s1, "HT", [128, 8, TT], F32)
                    XN = sb(s1, "XN", [128, 8, TT], BF16)
                    SQ = sb(s1, "SQ", [128, 8, TT], BF16)
                    RS = sb(s1, "RS", [128, TT], F32)
                    RSTD = sb(s1, "RSTD", [128, TT], F32)
                    QCN = sb(s1, "QCN", [128, 2, TT], BF16)
                    KVN = sb(s1, "KVN", [128, TT], BF16)
                    T1 = sb(s1, "T1", [128, TT], F32)
                    T2 = sb(s1, "T2", [128, TT], F32)
                    RC_ = sb(s1, "ROPEC", [128, TT], F32)
                    RS_ = sb(s1, "ROPES", [128, TT], F32)
                    WA = sb(s1, "WA", [128, 8, 320], BF16)
                    WKA = sb(s1, "WKA", [128, 8, 96], BF16)
                    WKB = sb(s1, "WKB", [128, 8, 96], BF16)
                    WUQ = sb(s1, "WUQ", [128, 2, 384], BF16)
                    WUQB = sb(s1, "WUQB", [128, 2, 4, 96], BF16)
                    WUKV = sb(s1, "WUKV", [128, 512], BF16)
                    WV = sb(s1, "WV", [128, 4, 64], BF16)
                    (tHT, tXNs, tSQ, tRS, tRSTD, tQCN, tKVN, tT1, tT2, tRC_, tRS_, tW) = (
                        _Tile(n) for n in ("HT", "XN", "SQ", "RS", "RSTD", "QCN", "KVN", "T1", "T2", "RC_", "RS_", "W1"))
                    d_h, d_xs, d_r = k.dsem("p1h"), k.dsem("p1xs"), k.dsem("p1r")
                    MEMSET(pool, WKA[:], 0.0, [tW])
                    MEMSET(pool, WKB[:], 0.0, [tW])
                    MEMSET(pool, WUQB[:], 0.0, [tW])
                    MEMSET(pool, WUQ[:], 0.0, [tW])
                    k.dma(pool, WA[:], win[:, :, 0:320], [], [tW], d_w[0])
                    k.dma(pool, WKA[:, :, 64:96], win[:, :, 320:352], [], [tW], d_w[0])
                    k.dma(pool, WKB[:, :, 64:80], win[:, :, 336:352], [], [tW], d_w[0])
                    k.dma(pool, WKB[:, :, 80:96], win[:, :, 320:336], [], [tW], d_w[0])
                    k.dma(pool, WUQ[:, 0, :], w_uq[l, 0:128, :], [], [tW], d_w[0])
                    k.dma(pool, WUQ[0:64, 1, :], w_uq[l, 128:192, :], [], [tW], d_w[0])
                    uq4 = w_uq[l].rearrange("r (h d) -> r h d", h=4)
                    k.dma(pool, WUQB[:, 0, :, 64:80], uq4[0:128, :, 80:96], [], [tW], d_w[0])
                    k.dma(pool, WUQB[:, 0, :, 80:96], uq4[0:128, :, 64:80], [], [tW], d_w[0])
                    k.dma(pool, WUQB[0:64, 1, :, 64:80], uq4[128:192, :, 80:96], [], [tW], d_w[0])
                    k.dma(pool, WUQB[0:64, 1, :, 80:96], uq4[128:192, :, 64:80], [], [tW], d_w[0])
                    k.dma(pool, WUKV[:], w_ukv[l, :, :], [], [tW], d_w[0])
                    k.dma(pool, WV[:], w_ukv[l].rearrange("r (h d) -> r h d", h=4)[:, :, 64:128], [], [tW], d_w[0])
                    for t in range(NT):
                        cs = cols(t)
                        k.dma(sp, HT[:], hsrc[:, :, cs], [tHs[t]], [tHT], d_h)
                        k.dma(sp, RC_[64:96, :], ropeT[0, 64:96, cs], [tROPE], [tRC_], d_r)
                        k.dma(sp, RS_[64:96, :], ropeT[1, 64:96, cs], [tROPE], [tRS_], d_r)
                        rmsnorm([HT[:, c, :] for c in range(8)], [128] * 8,
                                [VL[:, C_GMIX + c:C_GMIX + c + 1] for c in range(8)], 1024.0,
                                [XN[:, c, :] for c in range(8)], SQ, tSQ, RS, tRS, RSTD, tRSTD, 7, [tHT], [tXNs])
                        k.dma(sp, xnv[:, :, cs], XN[:], [tXNs], [tXN[t]], d_xs)
                        specs = [(0, 128, WA, slice(0, 128)), (1, 64, WA, slice(128, 192)), (2, 128, WA, slice(192, 320)),
                                 (3, 96, WKA, slice(0, 96)), (4, 96, WKB, slice(0, 96))]
                        for (pb, m, Wt, csl) in specs:
                            for c in range(8):
                                MM(PSB[pb][0:m, :], Wt[:, c, csl], XN[:, c, :], c == 0, c == 7, [tW, tXNs], [tPS[pb]])
                        rmsnorm([PSB[0][:, :], PSB[1][0:64, :]], [128, 64],
                                [VL[:, C_GQC:C_GQC + 1], VL[0:64, C_GQC + 1:C_GQC + 2]], 192.0,
                                [QCN[:, 0, :], QCN[0:64, 1, :]], SQ, tSQ, RS, tRS, RSTD, tRSTD, 7,
                                [tPS[0], tPS[1]], [tQCN])
                        rmsnorm([PSB[2][:, :]], [128], [VL[:, C_GKVC:C_GKVC + 1]], 128.0, [KVN[:, :]],
                                SQ, tSQ, RS, tRS, RSTD, tRSTD, 7, [tPS[2]], [tKVN])
                        TTn(dve, T1[64:96, :], PSB[3][64:96, :], RC_[64:96, :], ALU.mult, [tPS[3], tRC_], [tT1])
                        TTn(dve, T2[64:96, :], PSB[4][64:96, :], RS_[64:96, :], ALU.mult, [tPS[4], tRS_], [tT2])
                        TTn(pool, T1[64:96, :], T1[64:96, :], T2[64:96, :], ALU.add, [tT1, tT2], [tT1])
                        for h in range(4):
                            CP(pool, KM[64:96, h, cs], T1[64:96, :], [tT1], [tKM])
                        for h in range(4):
                            pa, pb2 = (0, 1) if h % 2 == 0 else (5, 6)
                            hs = slice(h * 96, (h + 1) * 96)
                            MM(PSB[pa][0:96, :], WUQ[:, 0, hs], QCN[:, 0, :], True, False, [tW, tQCN], [tPS[pa]])
                            MM(PSB[pa][0:96, :], WUQ[0:64, 1, hs], QCN[0:64, 1, :], False, True, [tW, tQCN], [tPS[pa]])
                            MM(PSB[pb2][0:96, :], WUQB[:, 0, h, :], QCN[:, 0, :], True, False, [tW, tQCN], [tPS[pb2]])
                            MM(PSB[pb2][0:96, :], WUQB[0:64, 1, h, :], QCN[0:64, 1, :], False, True, [tW, tQCN], [tPS[pb2]])
                            CP(act, QM[0:64, h, cs], PSB[pa][0:64, :], [tPS[pa]], [tQM])
                            TTn(dve, T1[64:96, :], PSB[pa][64:96, :], RC_[64:96, :], ALU.mult, [tPS[pa], tRC_], [tT1])
                            TTn(dve, T2[64:96, :], PSB[pb2][64:96, :], RS_[64:96, :], ALU.mult, [tPS[pb2], tRS_], [tT2])
                            TTn(pool, QM[64:96, h, cs], T1[64:96, :], T2[64:96, :], ALU.add, [tT1, tT2], [tQM])
                        for h in range(4):
                            pb = 2 + (h % 2)
                            MM(PSB[pb][0:64, :], WUKV[:, h * 128:h * 128 + 64], KVN[:, :], True, True, [tW, tKVN], [tPS[pb]])
                            CP(act, KM[0:64, h, cs], PSB[pb][0:64, :], [tPS[pb]], [tKM])
                        for s_ in range(4):
                            pb = 5 + (s_ % 2)
                            MM(PSB[pb][:, 0:256], KVN[:, s_ * 128:(s_ + 1) * 128], WV[:, :, :].rearrange("p h d -> p (h d)"),
                               True, True, [tKVN, tW], [tPS[pb]])
                            CP(dve, VA[:, t * 4 + s_, :, 0:64], PSB[pb][:, 0:256].rearrange("p (h d) -> p h d", h=4),
                               [tPS[pb]], [tVA])
                    k.barrier()
                if stop_after == f"p1_{l}":
                    return nc
                attention(l, QM, tQM, KM, tKM, VA, tVA, 96, 96.0 ** -0.5, 0, 0)
            if stop_after == f"p2_{l}":
                return nc

            with contextlib.ExitStack() as st:
                FQ = sb(st, "FQ", [128, 4, S], BF16)
                FK = sb(st, "FK", [128, 4, S], BF16)
                FVA = sb(st, "FVA", [128, 32, 4, 128], BF16)
                tFQ, tFK, tFVA = _Tile("FQ"), _Tile("FK"), _Tile("FVA")
                MEMSET(pool, FVA[:, :, :, 64:128], 1.0, [tFVA])
                with contextlib.ExitStack() as s1:
                    XNb = [sb(s1, f"XNf{i}", [128, 8, TT], BF16) for i in range(2)]
                    tXNb = [_Tile("XNf0"), _Tile("XNf1")]
                    d_x = [k.dsem("p3x0"), k.dsem("p3x1")]
                    CC = sb(s1, "CC", [128, TT], BF16)
                    EX = sb(s1, "EX", [4, TT], F32)
                    NL = sb(s1, "NL", [4, TT], F32)
                    ONE4 = sb(s1, "ONE4", [4, TT], F32)
                    CF = sb(s1, "CF", [4, TT], F32)
                    R1 = sb(s1, "R1", [4, TT], F32)
                    CHI = sb(s1, "CHI", [4, TT], BF16)
                    CMID = sb(s1, "CMID", [4, TT], BF16)
                    CLO = sb(s1, "CLO", [4, TT], BF16)
                    CARRY = sb(s1, "CARRY", [4, 2], F32)
                    NEGB = sb(s1, "NEGB", [4, 2], F32)
                    WFQ = sb(s1, "WFQ", [128, 8, 4, 70], BF16)
                    WFK = sb(s1, "WFK", [128, 8, 4, 70], BF16)
                    WFV = sb(s1, "WFV", [128, 8, 256], BF16)
                    WF4 = sb(s1, "WF4", [128, 8, 4], BF16)
                    SEL = sb(s1, "SEL", [128, 8, 70], BF16)
                    (tCC, tEX, tNL, tONE4, tCF, tR1, tCHI, tCMID, tCLO, tCARRY, tNEGB, tW) = (
                        _Tile(n) for n in ("CC", "EX", "NL", "ONE4", "CF", "R1", "CHI", "CMID", "CLO", "CARRY", "NEGB", "W3"))
                    MEMSET(pool, WFQ[:], 0.0, [tW])
                    MEMSET(pool, WFK[:], 0.0, [tW])
                    MEMSET(dve, CC[:], 0.0, [tCC])
                    MEMSET(dve, CC[96:97, :], 1.0, [tCC])
                    MEMSET(dve, ONE4[:], 1.0, [tONE4])
                    MEMSET(dve, CARRY[:], 0.0, [tCARRY])
                    TS(dve, NEGB[0:4, 0:1], VL[0:4, C_BF:C_BF + 1], -1.0, None, ALU.mult, None, [tVL], [tNEGB])
                    for c in range(8):
                        k.dma(pool, WFQ[:, c, :, 0:64], win[:, c, 352:608].rearrange("p (h d) -> p h d", h=4), [], [tW], d_w[1])
                        k.dma(pool, WFK[:, c, :, 0:64], win[:, c, 608:864].rearrange("p (h d) -> p h d", h=4), [], [tW], d_w[1])
                    k.dma(pool, WFV[:], win[:, :, 864:1120], [], [tW], d_w[1])
                    k.dma(pool, WF4[:], win[:, :, 1120:1124], [], [tW], d_w[1])
                    k.dma(pool, SEL[:], cseld[:, :, :], [], [tW], d_w[1])
                    k.dma(sp, XNb[0][:], xnv[:, :, cols(0)], [tXN[0]], [tXNb[0]], d_x[0])
                    for t in range(NT):
                        cs = cols(t)
                        XN, tXNs = XNb[t % 2], tXNb[t % 2]
                        if t + 1 < NT:
                            k.dma(sp, XNb[(t + 1) % 2][:], xnv[:, :, cols(t + 1)], [tXN[t + 1]], [tXNb[(t + 1) % 2]],
                                  d_x[(t + 1) % 2])
                        for c in range(8):
                            MM(PSB[7][0:4, :], WF4[:, c, :], XN[:, c, :], c == 0, c == 7, [tW, tXNs], [tPS[7]])
                        ACT(EX[:, :], PSB[7][0:4, :], AF.Exp, [tPS[7], tNEGB], [tEX], bias=NEGB[0:4, 0:1], scale=-1.0)
                        ACT(NL[:, :], EX[:, :], AF.Ln, [tEX, tCST], [tNL], bias=CST[0:4, 1:2], scale=1.0)
                        TS(dve, NL[:, :], NL[:, :], -1.0, None, ALU.mult, None, [tNL], [tNL])
                        k.op(dve, lambda e: e.tensor_tensor_scan(out=CF[:, :], data0=ONE4[:, :], data1=NL[:, :],
                                                                 initial=CARRY[0:4, 0:1], op0=ALU.mult, op1=ALU.add),
                             [tONE4, tNL, tCARRY], [tCF])
                        CP(dve, CARRY[0:4, 0:1], CF[:, TT - 1:TT], [tCF], [tCARRY])
                        CP(dve, CHI[:, :], CF[:, :], [tCF], [tCHI])
                        TTn(dve, R1[:, :], CF[:, :], CHI[:, :], ALU.subtract, [tCF, tCHI], [tR1])
                        CP(dve, CMID[:, :], R1[:, :], [tR1], [tCMID])
                        TTn(dve, R1[:, :], R1[:, :], CMID[:, :], ALU.subtract, [tR1, tCMID], [tR1])
                        CP(dve, CLO[:, :], R1[:, :], [tR1], [tCLO])
                        CP(pool, CC[0:4, :], CHI[:, :], [tCHI], [tCC])
                        CP(pool, CC[32:36, :], CMID[:, :], [tCMID], [tCC])
                        CP(pool, CC[64:68, :], CLO[:, :], [tCLO], [tCC])
                        for h in range(4):
                            for (qi, Wt, DST, tD) in ((0, WFQ, FQ, tFQ), (1, WFK, FK, tFK)):
                                pb = (2 * h + qi) % 4
                                for c in range(8):
                                    MM(PSB[pb][0:70, :], Wt[:, c, h, :], XN[:, c, :], c == 0, False, [tW, tXNs], [tPS[pb]])
                                MM(PSB[pb][0:70, :], SEL[:, qi * 4 + h, :], CC[:, :], False, True, [tW, tCC], [tPS[pb]])
                                CP(act if qi == 0 else dve, DST[0:70, h, cs], PSB[pb][0:70, :], [tPS[pb]], [tD])
                        for s_ in range(4):
                            pb = 4 + (s_ % 2)
                            for c in range(8):
                                MM(PSB[pb][:, 0:256], XN[:, c, s_ * 128:(s_ + 1) * 128], WFV[:, c, :], c == 0, c == 7,
                                   [tXNs, tW], [tPS[pb]])
                            CP(act if s_ % 2 else dve, FVA[:, t * 4 + s_, :, 0:64],
                               PSB[pb][:, 0:256].rearrange("p (h d) -> p h d", h=4), [tPS[pb]], [tFVA])
                    k.barrier()
                if stop_after == f"p3_{l}":
                    return nc
                attention(l, FQ, tFQ, FK, tFK, FVA, tFVA, 70, 0.125, 1, 4)
            if stop_after == f"p4_{l}":
                return nc

            with contextlib.ExitStack() as s1:
                XNb = [sb(s1, f"XNl{i}", [128, 8, TT], BF16) for i in range(2)]
                tXNb = [_Tile("XNl0"), _Tile("XNl1")]
                d_x = [k.dsem("p5x0"), k.dsem("p5x1")]
                LXB = sb(s1, "LXB", [128, 4, TT + 3], F32)
                XC = sb(s1, "XC", [128, 4, TT], F32)
                XCB = sb(s1, "XCB", [128, 4, TT], BF16)
                RG = sb(s1, "RG", [128, TT], F32)
                IG = sb(s1, "IG", [128, TT], F32)
                AG = sb(s1, "AG", [128, TT], F32)
                A2 = sb(s1, "A2", [128, TT], F32)
                BX = sb(s1, "BX", [128, TT], F32)
                GL = sb(s1, "GL", [128, TT], F32)
                HS = sb(s1, "HS", [128, 4, TT], F32)
                OL = sb(s1, "OL", [128, 4, TT], F32)
                SQ = sb(s1, "SQl", [128, 4, TT], BF16)
                RS = sb(s1, "RSl", [128, TT], F32)
                RSTD = sb(s1, "RSTDl", [128, TT], F32)
                OMX = [sb(s1, f"OMXl{i}", [128, 4, TT], BF16) for i in range(2)]
                tOMX = [_Tile("OMXl0"), _Tile("OMXl1")]
                d_st = [k.dsem("lst0"), k.dsem("lst1")]
                CARH = sb(s1, "CARH", [128, 4], F32)
                COEF = sb(s1, "COEF", [128, 4], F32)
                WL = sb(s1, "WL", [128, 8, 1024], BF16)
                WRB = sb(s1, "WRB", [128, 4, 128], BF16)
                WIB = sb(s1, "WIB", [128, 4, 128], BF16)
                (tLXB, tXC, tXCB, tRG, tIG, tAG, tA2, tBX, tGL, tHS, tOL, tSQ, tRS, tRSTD, tCARH, tCOEF, tW) = (
                    _Tile(n) for n in ("LXB", "XC", "XCB", "RG", "IG", "AG", "A2", "BX", "GL", "HS", "OL", "SQl", "RSl",
                                       "RSTDl", "CARH", "COEF", "W5"))
                MEMSET(pool, WRB[:], 0.0, [tW])
                MEMSET(pool, WIB[:], 0.0, [tW])
                MEMSET(dve, LXB[:], 0.0, [tLXB])
                MEMSET(dve, CARH[:], 0.0, [tCARH])
                ACT(COEF[:, :], VL[:, C_LAM:C_LAM + 4], AF.Exp, [tVL], [tCOEF], scale=-1.0)
                ACT(COEF[:, :], COEF[:, :], AF.Ln, [tCOEF, tCST], [tCOEF], bias=CST[:, 1:2], scale=1.0)
                TS(dve, COEF[:, :], COEF[:, :], -8.0, None, ALU.mult, None, [tCOEF], [tCOEF])
                k.dma(pool, WL[:], win[:, :, 1124:2148], [], [tW], d_w[2])
                for c in range(4):
                    for hb in range(2):
                        ps_ = slice(hb * 64, hb * 64 + 64)
                        k.dma(pool, WRB[ps_, c, ps_], w_r[l, 2 * c + hb, :, :], [], [tW], d_w[2])
                        k.dma(pool, WIB[ps_, c, ps_], w_i[l, 2 * c + hb, :, :], [], [tW], d_w[2])
                k.dma(sp, XNb[0][:], xnv[:, :, cols(0)], [tXN[0]], [tXNb[0]], d_x[0])
                for t in range(NT):
                    cs = cols(t)
                    XN, tXNs = XNb[t % 2], tXNb[t % 2]
                    if t + 1 < NT:
                        k.dma(sp, XNb[(t + 1) % 2][:], xnv[:, :, cols(t + 1)], [tXN[t + 1]], [tXNb[(t + 1) % 2]],
                              d_x[(t + 1) % 2])
                    for c in range(4):
                        pb = c % 2
                        for kc in range(8):
                            MM(PSB[pb][:, :], WL[:, kc, c * 128:(c + 1) * 128], XN[:, kc, :], kc == 0, kc == 7,
                               [tW, tXNs], [tPS[pb]])
                        CP(pool, LXB[:, c, 0:3], LXB[:, c, TT:TT + 3], [tLXB], [tLXB])
                        CP(act, LXB[:, c, 3:TT + 3], PSB[pb][:, :], [tPS[pb]], [tLXB])
                        wcol = lambda kk: VL[:, C_LCW + kk * 4 + c:C_LCW + kk * 4 + c + 1]
                        TS(pool, XC[:, c, :], LXB[:, c, 3:TT + 3], wcol(3), VL[:, C_LCB + c:C_LCB + c + 1], ALU.mult, ALU.add,
                           [tLXB, tVL], [tXC])
                        for kk in range(3):
                            STT(dve, XC[:, c, :], LXB[:, c, kk:kk + TT], wcol(kk), XC[:, c, :], ALU.mult, ALU.add,
                                [tLXB, tXC, tVL], [tXC])
                        CP(pool, XCB[:, c, :], XC[:, c, :], [tXC], [tXCB])
                        MM(PSB[2][:, :], WRB[:, c, :], XCB[:, c, :], True, True, [tW, tXCB], [tPS[2]])
                        MM(PSB[3][:, :], WIB[:, c, :], XCB[:, c, :], True, True, [tW, tXCB], [tPS[3]])
                        ACT(RG[:, :], PSB[2][:, :], AF.Sigmoid, [tPS[2], tVL], [tRG], bias=VL[:, C_BR + c:C_BR + c + 1])
                        ACT(IG[:, :], PSB[3][:, :], AF.Sigmoid, [tPS[3], tVL], [tIG], bias=VL[:, C_BI + c:C_BI + c + 1])
                        ACT(AG[:, :], RG[:, :], AF.Exp, [tRG, tCOEF], [tAG], scale=COEF[:, c:c + 1])
                        TTn(dve, A2[:, :], AG[:, :], AG[:, :], ALU.mult, [tAG], [tA2])
                        TS(dve, A2[:, :], A2[:, :], -1.0, 1.0, ALU.mult, ALU.add, [tA2], [tA2])
                        ACT(A2[:, :], A2[:, :], AF.Sqrt, [tA2], [tA2])
                        TTn(dve, BX[:, :], IG[:, :], XC[:, c, :], ALU.mult, [tIG, tXC], [tBX])
                        TTn(dve, BX[:, :], BX[:, :], A2[:, :], ALU.mult, [tBX, tA2], [tBX])
                        k.op(dve, lambda e: e.tensor_tensor_scan(out=HS[:, c, :], data0=AG[:, :], data1=BX[:, :],
                                                                 initial=CARH[:, c:c + 1], op0=ALU.mult, op1=ALU.add),
                             [tAG, tBX, tCARH], [tHS])
                        CP(dve, CARH[:, c:c + 1], HS[:, c, TT - 1:TT], [tHS], [tCARH])
                        pg = 4 + (c % 2)
                        for kc in range(8):
                            MM(PSB[pg][:, :], WL[:, kc, 512 + c * 128:512 + (c + 1) * 128], XN[:, kc, :], kc == 0, kc == 7,
                               [tW, tXNs], [tPS[pg]])
                        ACT(GL[:, :], PSB[pg][:, :], AF.Gelu_apprx_tanh, [tPS[pg]], [tGL])
                        TTn(dve, OL[:, c, :], HS[:, c, :], GL[:, :], ALU.mult, [tHS, tGL], [tOL])
                    om, tom = OMX[t % 2], tOMX[t % 2]
                    rmsnorm([OL[:, c, :] for c in range(4)], [128] * 4,
                            [VL[:, C_GOUTL + c:C_GOUTL + c + 1] for c in range(4)], 512.0,
                            [om[:, c, :] for c in range(4)], SQ, tSQ, RS, tRS, RSTD, tRSTD, 7, [tOL], [tom])
                    k.dma(sp, omv[:, 4:8, cs], om[:, :, :], [tom], [tOM[2][t]], d_st[t % 2])
                k.barrier()
            if stop_after == f"p5_{l}":
                return nc

            with contextlib.ExitStack() as s1:
                WO = sb(s1, "WO", [128, 8, 1024], BF16)
                OMb = [sb(s1, f"OMt{i}", [128, 8, TT], BF16) for i in range(2)]
                HTb = [sb(s1, f"HT6{i}", [128, 8, TT], F32) for i in range(2)]
                XN2 = sb(s1, "XN2", [128, 8, TT], BF16)
                SQ = sb(s1, "SQ6", [128, 8, TT], BF16)
                RS = sb(s1, "RS6", [128, TT], F32)
                RSTD = sb(s1, "RSTD6", [128, TT], F32)
                tOMb = [_Tile("OMt0"), _Tile("OMt1")]
                tHTb = [_Tile("HT60"), _Tile("HT61")]
                tXN2, tSQ, tRS, tRSTD, tW = (_Tile(n) for n in ("XN2", "SQ6", "RS6", "RSTD6", "W6"))
                d_o = [k.dsem("p6o0"), k.dsem("p6o1")]
                d_h = [k.dsem("p6h0"), k.dsem("p6h1")]
                d_xs = k.dsem("p6xs")
                k.dma(pool, WO[:], w_o[l].rearrange("(c p) n -> p c n", p=128), [], [tW], d_w[3])

                def ld6(t):
                    i = t % 2
                    k.dma(sp, OMb[i][:], omv[:, :, cols(t)], [tOM[0][t], tOM[1][t], tOM[2][t]], [tOMb[i]], d_o[i])
                    k.dma(sp, HTb[i][:], hsrc[:, :, cols(t)], [tHs[t]], [tHTb[i]], d_h[i])
                ld6(0)
                for t in range(NT):
                    i = t % 2
                    cs = cols(t)
                    if t + 1 < NT:
                        ld6(t + 1)
                    HT, tHT = HTb[i], tHTb[i]
                    for m in range(8):
                        pb = m % 4
                        for c in range(8):
                            MM(PSB[pb][:, :], WO[:, c, m * 128:(m + 1) * 128], OMb[i][:, c, :], c == 0, c == 7,
                               [tW, tOMb[i]], [tPS[pb]])
                        TTn(dve, HT[:, m, :], PSB[pb][:, :], HT[:, m, :], ALU.add, [tPS[pb], tHT], [tHT])
                    k.dma(sp, hv_a[:, :, cs], HT[:], [tHT], [tH[t]], d_h[i])
                    rmsnorm([HT[:, c, :] for c in range(8)], [128] * 8,
                            [VL[:, C_GFFN + c:C_GFFN + c + 1] for c in range(8)], 1024.0,
                            [XN2[:, c, :] for c in range(8)], SQ, tSQ, RS, tRS, RSTD, tRSTD, 7, [tHT], [tXN2])
                    k.dma(sp, xnv[:, :, cs], XN2[:], [tXN2], [tXN[t]], d_xs)
                k.barrier()
            if stop_after == f"p6_{l}":
                return nc

            wup = w_up[l].rearrange("(c p) n -> p c n", p=128)
            wdn = w_down[l].rearrange("(j p) n -> p j n", p=128)
            last = (l == DEPTH - 1)
            for half in range(2):
                with contextlib.ExitStack() as s1:
                    WUG = sb(s1, "WUG", [128, 8, 11 * 128], BF16)
                    WUV = sb(s1, "WUV", [128, 8, 11 * 128], BF16)
                    WD = sb(s1, "WD", [128, 11, 1024], BF16)
                    XNb = [sb(s1, f"XN7{i}", [128, 8, TT], BF16) for i in range(2)]
                    tXNb = [_Tile("XN70"), _Tile("XN71")]
                    nhb = 2 if half == 0 else 1
                    HTb = [sb(s1, f"HT7{i}", [128, 8, TT], F32) for i in range(nhb)]
                    tHTb = [_Tile(f"HT7{i}") for i in range(nhb)]
                    AT = sb(s1, "AT", [128, 11, TT], BF16)
                    ACG = [sb(s1, f"ACG{i}", [128, TT], F32) for i in range(2)]
                    ACV = [sb(s1, f"ACV{i}", [128, TT], F32) for i in range(2)]
                    SG = [sb(s1, f"SG{i}", [128, TT], F32) for i in range(2)]
                    TAIL = sb(s1, "TAIL", [128, 22, 2], F32)
                    TB = sb(s1, "TB", [128, 4], F32)
                    tTB = _Tile("TB")
                    tAT, tTAIL, tW = _Tile("AT"), _Tile("TAIL"), _Tile("W7")
                    tACG = [_Tile("ACG0"), _Tile("ACG1")]
                    tACV = [_Tile("ACV0"), _Tile("ACV1")]
                    tSG = [_Tile("SG0"), _Tile("SG1")]
                    d_x = [k.dsem(f"p7x0_{half}"), k.dsem(f"p7x1_{half}")]
                    d_h = [k.dsem(f"p7h{i}_{half}") for i in range(nhb)]
                    j0 = half * 11
                    k.dma(pool, WUG[:], wup[:, :, j0 * 128:(j0 + 11) * 128], [], [tW], d_w[4 + half])
                    k.dma(pool, WUV[:], wup[:, :, 2816 + j0 * 128:2816 + (j0 + 11) * 128], [], [tW], d_w[4 + half])
                    k.dma(pool, WD[:], wdn[:, j0:j0 + 11, :], [], [tW], d_w[4 + half])
                    MEMSET(dve, TAIL[:], 0.0, [tTAIL])
                    if half == 1:
                        WG = sb(s1, "WG", [128, 8, 1024], BF16)
                        WP = sb(s1, "WP", [128, 2, 1024], BF16)
                        XN3 = sb(s1, "XN3", [128, 8, TT], BF16)
                        SQ = sb(s1, "SQ8", [128, 8, TT], BF16)
                        RS = sb(s1, "RS8", [128, TT], F32)
                        RSTD = sb(s1, "RSTD8", [128, TT], F32)
                        PTb = [sb(s1, f"PTp{i}", [128, 2, TT], BF16) for i in range(2)]
                        SGT = sb(s1, "SGT", [128, TT], F32)
                        TMP = sb(s1, "TMP", [128, TT], F32)
                        tXN3, tSQ, tRS, tRSTD, tSGT, tTMP = (_Tile(n) for n in ("XN3", "SQ8", "RS8", "RSTD8", "SGT", "TMP"))
                        tPTb = [_Tile("PTp0"), _Tile("PTp1")]
                        d_p = [k.dsem("p8p0"), k.dsem("p8p1")]
                        k.dma(pool, WG[:], w_pg[l].rearrange("(c p) n -> p c n", p=128), [], [tW], d_w[6])
                        k.dma(pool, WP[:], w_pp[l].rearrange("(c p) n -> p c n", p=128), [], [tW], d_w[6])
                        ptv = pT[l].rearrange("(c p) t -> p c t", p=128)

                    def ld7(t):
                        i = t % 2
                        k.dma(sp, XNb[i][:], xnv[:, :, cols(t)], [tXN[t]], [tXNb[i]], d_x[i])
                        ih = t % nhb
                        k.dma(sp, HTb[ih][:], hv_a[:, :, cols(t)], [tH[t]], [tHTb[ih]], d_h[ih])
                        if half == 1:
                            k.dma(pool, PTb[i][:], ptv[:, :, cols(t)], [], [tPTb[i]], d_p[i])
                    ld7(0)
                    cidx = 0
                    for t in range(NT):
                        i = t % 2
                        cs = cols(t)
                        if t + 1 < NT and nhb == 2:
                            ld7(t + 1)
                        XN, tXNs = XNb[i], tXNb[i]
                        HT, tHT = HTb[t % nhb], tHTb[t % nhb]
                        for jj in range(11):
                            j = j0 + jj
                            bi = cidx % 2
                            cidx += 1
                            for (which, Wt, pb, ACC, tACC, ch) in ((0, WUG, 2 * bi, ACG[bi], tACG[bi], j),
                                                                   (1, WUV, 2 * bi + 1, ACV[bi], tACV[bi], 22 + j)):
                                for c in range(8):
                                    MM(PSB[pb][:, :], Wt[:, c, jj * 128:(jj + 1) * 128], XN[:, c, :], c == 0, c == 7,
                                       [tW, tXNs], [tPS[pb]])
                                w0 = VL[:, C_FCW + ch:C_FCW + ch + 1]
                                w1 = VL[:, C_FCW + 44 + ch:C_FCW + 44 + ch + 1]
                                w2 = VL[:, C_FCW + 88 + ch:C_FCW + 88 + ch + 1]
                                bb = VL[:, C_FCB + ch:C_FCB + ch + 1]
                                tl = jj * 2 + which
                                ACT(ACC[:, :], PSB[pb][:, :], AF.Identity, [tPS[pb], tVL], [tACC], bias=bb, scale=w2)
                                STT(dve, ACC[:, 1:TT], PSB[pb][:, 0:TT - 1], w1, ACC[:, 1:TT], ALU.mult, ALU.add,
                                    [tPS[pb], tACC, tVL], [tACC])
                                STT(dve, ACC[:, 2:TT], PSB[pb][:, 0:TT - 2], w0, ACC[:, 2:TT], ALU.mult, ALU.add,
                                    [tPS[pb], tACC, tVL], [tACC])
                                STT(dve, ACC[:, 0:2], TAIL[:, tl, 0:2], w0, ACC[:, 0:2], ALU.mult, ALU.add,
                                    [tTAIL, tACC, tVL], [tACC])
                                STT(dve, ACC[:, 0:1], TAIL[:, tl, 1:2], w1, ACC[:, 0:1], ALU.mult, ALU.add,
                                    [tTAIL, tACC, tVL], [tACC])
                                CP(act, TAIL[:, tl, 0:2], PSB[pb][:, TT - 2:TT], [tPS[pb]], [tTAIL])
                            ACT(SG[bi][:, :], ACG[bi][:, :], AF.Silu, [tACG[bi]], [tSG[bi]])
                            TTn(dve, AT[:, jj, :], SG[bi][:, :], ACV[bi][:, :], ALU.mult, [tSG[bi], tACV[bi]], [tAT])
                        for m in range(8):
                            pb = 4 + (m % 2)
                            for jj in range(11):
                                MM(PSB[pb][:, :], WD[:, jj, m * 128:(m + 1) * 128], AT[:, jj, :], jj == 0, jj == 10,
                                   [tW, tAT], [tPS[pb]])
                            TTn(dve, HT[:, m, :], PSB[pb][:, :], HT[:, m, :], ALU.add, [tPS[pb], tHT], [tHT])
                        if half == 0:
                            k.dma(sp, hv_a[:, :, cs], HT[:], [tHT], [tH[t]], d_h[t % nhb])
                            continue
                        rmsnorm([HT[:, c, :] for c in range(8)], [128] * 8,
                                [VL[:, C_GPLE + c:C_GPLE + c + 1] for c in range(8)], 1024.0,
                                [XN3[:, c, :] for c in range(8)], SQ, tSQ, RS, tRS, RSTD, tRSTD, 7, [tHT], [tXN3])
                        for m in range(8):
                            pg, pp = (4, 5) if m % 2 == 0 else (6, 7)
                            for c in range(8):
                                MM(PSB[pg][:, :], WG[:, c, m * 128:(m + 1) * 128], XN3[:, c, :], c == 0, c == 7,
                                   [tW, tXN3], [tPS[pg]])
                            for c in range(2):
                                MM(PSB[pp][:, :], WP[:, c, m * 128:(m + 1) * 128], PTb[i][:, c, :], c == 0, c == 1,
                                   [tW, tPTb[i]], [tPS[pp]])
                            ACT(SGT[:, :], PSB[pg][:, :], AF.Sigmoid, [tPS[pg]], [tSGT])
                            TTn(dve, TMP[:, :], PSB[pp][:, :], SGT[:, :], ALU.mult, [tPS[pp], tSGT], [tTMP])
                            TTn(pool, HT[:, m, :], HT[:, m, :], TMP[:, :], ALU.add, [tHT, tTMP], [tHT])
                        if last:
                            rmsnorm([HT[:, c, :] for c in range(8)], [128] * 8,
                                    [VL[:, C_GFIN + c:C_GFIN + c + 1] for c in range(8)], 1024.0,
                                    [HT[:, c, :] for c in range(8)], SQ, tSQ, RS, tRS, RSTD, tRSTD, 7, [tHT], [tHT])
                            k.dma(sp, hv_o[:, :, cs], HT[:], [tHT], [tNONE], d_h[0])
                        else:
                            k.dma(sp, hv_a[:, :, cs], HT[:], [tHT], [tH[t]], d_h[0])
                        if t + 1 < NT:
                            ld7(t + 1)
                    k.barrier()
                if stop_after == f"p{7 + half}_{l}":
                    return nc
        k.barrier()
    return nc


def _vec_table(inp, l):
    V = np.zeros((128, NV), np.float32)

    def chunks(v, n, c0, rows=128):
        v = np.asarray(v, np.float32).reshape(n, rows)
        V[:rows, c0:c0 + n] = v.T

    chunks(inp["g_mix"][l], 8, C_GMIX)
    chunks(inp["g_ffn"][l], 8, C_GFFN)
    chunks(inp["g_ple"][l], 8, C_GPLE)
    gq = np.asarray(inp["g_qc"][l], np.float32)
    V[:, C_GQC] = gq[0:128]
    V[0:64, C_GQC + 1] = gq[128:192]
    V[:, C_GKVC] = np.asarray(inp["g_kvc"][l], np.float32)
    chunks(inp["g_out"][l], 16, C_GOUT64, rows=64)
    chunks(np.asarray(inp["g_out"][l])[512:], 4, C_GOUTL)
    V[0:4, C_BF] = np.asarray(inp["b_f"][l], np.float32)
    lcw = np.asarray(inp["lru_conv_w"][l], np.float32)
    for kk in range(4):
        chunks(lcw[kk], 4, C_LCW + kk * 4)
    chunks(inp["lru_conv_b"][l], 4, C_LCB)
    chunks(inp["b_r"][l], 4, C_BR)
    chunks(inp["b_i"][l], 4, C_BI)
    chunks(inp["lru_lambda"][l], 4, C_LAM)
    fcw = np.asarray(inp["ffn_conv_w"][l], np.float32)
    for kk in range(3):
        chunks(fcw[kk], 44, C_FCW + kk * 44)
    chunks(inp["ffn_conv_b"][l], 44, C_FCB)
    chunks(inp["g_final"], 8, C_GFIN)
    return V


def _consts():
    ident = np.eye(128, dtype=np.float32)
    kk = np.arange(128)[:, None]
    qq = np.arange(128)[None, :]
    cmask = np.where(kk > qq, -30000.0, 0.0).astype(np.float32)
    sel = np.zeros((128, 8, 70), np.float32)
    for h in range(4):
        sel[96, h, 64:67] = 8.0
        sel[0 + h, h, 67] = 8.0
        sel[32 + h, h, 68] = 8.0
        sel[64 + h, h, 69] = 8.0
        sel[0 + h, 4 + h, 64] = -1.0
        sel[32 + h, 4 + h, 65] = -1.0
        sel[64 + h, 4 + h, 66] = -1.0
        sel[96, 4 + h, 67:70] = 1.0
    freq = np.zeros((128, 2), np.float32)
    half = 16
    f = (np.float32(10000.0) ** (-np.arange(half, dtype=np.float32) / np.float32(half))).astype(np.float32)
    freq[64:80, 0] = f
    freq[80:96, 0] = f
    freq[64:80, 1] = -2.0 * math.pi
    freq[80:96, 1] = 2.0 * math.pi
    return ident, cmask, sel, freq


_NC_CACHE = {}


def make_in_maps(inputs, cores):
    f32 = lambda a: np.ascontiguousarray(np.asarray(a), dtype=np.float32)
    ident, cmask, sel, freq = _consts()
    vec = np.stack([_vec_table(inputs, l) for l in range(DEPTH)])
    shared = {
        "w_in": f32(inputs["w_in"]), "w_uq": f32(inputs["w_uq"]), "w_ukv": f32(inputs["w_ukv"]),
        "w_r": f32(inputs["w_r"]), "w_i": f32(inputs["w_i"]), "w_o": f32(inputs["w_o"]),
        "w_up": f32(inputs["w_up"]), "w_down": f32(inputs["w_down"]),
        "w_ple_gate": f32(inputs["w_ple_gate"]), "w_ple_proj": f32(inputs["w_ple_proj"]),
        "vecs": vec, "cident": ident, "cmask": cmask, "csel": sel, "cfreq": freq,
    }
    x = np.asarray(inputs["x"])
    p = np.asarray(inputs["p"])
    pos = np.asarray(inputs["positions"])
    maps = []
    for b in cores:
        m = dict(shared)
        m["xT"] = np.ascontiguousarray(x[b].T, dtype=np.float32)
        m["pT"] = np.ascontiguousarray(np.transpose(p[:, b], (0, 2, 1)), dtype=np.float32)
        m["pos"] = np.ascontiguousarray(np.broadcast_to(pos[b].astype(np.int32)[None, :], (128, S)))
        maps.append(m)
    return maps


def kernel(**inputs):
    if "nc" not in _NC_CACHE:
        _NC_CACHE["nc"] = build_nc()
    nc = _NC_CACHE["nc"]
    B = np.asarray(inputs["x"]).shape[0]
    maps = make_in_maps(inputs, list(range(B)))
    res = run_bass_kernel_spmd(nc, maps, core_ids=list(range(B)))
    out = np.stack([np.ascontiguousarray(r["outT"].T) for r in res.results], axis=0)
    return out.astype(np.float32)
```

```python
import contextlib
import math
import numpy as np
import concourse.bass as bass
import concourse.mybir as mybir
from concourse.bass_utils import run_bass_kernel_spmd

F32 = mybir.dt.float32
BF16 = mybir.dt.bfloat16
I32 = mybir.dt.int32
AF = mybir.ActivationFunctionType
ALU = mybir.AluOpType

S = 4096
D = 1024
TT = 512
NT = S // TT
DEPTH = 2
EPS = 1e-6
NV = 272

C_GMIX, C_GFFN, C_GPLE = 0, 8, 16
C_GQC, C_GKVC = 24, 26
C_GOUT64 = 27
C_GOUTL = 43
C_BF = 47
C_LCW = 48
C_LCB, C_BR, C_BI, C_LAM = 64, 68, 72, 76
C_FCW = 80
C_FCB = 212
C_GFIN = 256


class _Tile:
    __slots__ = ("name", "w", "r")

    def __init__(self, name):
        self.name = name
        self.w = None
        self.r = {}


class _Eng:
    def __init__(self, name, h, k, is_pe=False):
        self.name = name
        self.h = h
        self.k = k
        self.is_pe = is_pe
        self.cnt = 0
        self.waited = {}


class _DSem:
    def __init__(self, k):
        self.k = k
        self.cnt = 0


class Ctx:
    def __init__(self, nc, stack):
        self.nc = nc
        self.stack = stack
        self.sems = []
        self.dsems = []
        self.pe = _Eng("pe", nc.tensor, self._sem("s_pe"), True)
        self.act = _Eng("act", nc.scalar, self._sem("s_act"))
        self.dve = _Eng("dve", nc.vector, self._sem("s_dve"))
        self.pool = _Eng("pool", nc.gpsimd, self._sem("s_pool"))
        self.sp = _Eng("sp", nc.sync, self._sem("s_sp"))
        self.engs = [self.pe, self.act, self.dve, self.pool, self.sp]

    def _sem(self, name):
        name = f"{name}_{len(self.sems)}"
        h = self.stack.enter_context(self.nc.semaphore(name))
        self.sems.append(h)
        return len(self.sems) - 1

    def dsem(self, name):
        d = _DSem(self._sem("d_" + name))
        self.dsems.append(d)
        return d

    def _add(self, need, eng, st, kind):
        if st is None:
            return
        k, val, src = st
        if src is eng:
            if eng.is_pe or kind != "raw":
                return
        if need.get(k, 0) < val:
            need[k] = val

    def _emit_waits(self, eng, reads, writes):
        need = {}
        for t in reads:
            self._add(need, eng, t.w, "raw")
        for t in writes:
            self._add(need, eng, t.w, "waw")
            for st in t.r.values():
                self._add(need, eng, st, "war")
        for k, v in need.items():
            if eng.waited.get(k, 0) < v:
                eng.h.wait_ge(self.sems[k], v)
                eng.waited[k] = v

    def _stamp(self, st, reads, writes):
        for t in reads:
            old = t.r.get(st[0])
            if old is None or old[1] < st[1]:
                t.r[st[0]] = st
        for t in writes:
            t.w = st
            t.r = {}

    def op(self, eng, fn, reads=(), writes=(), inc=True):
        self._emit_waits(eng, reads, writes)
        ins = fn(eng.h)
        if inc:
            eng.cnt += 1
            ins.then_inc(self.sems[eng.k], 1)
            st = (eng.k, eng.cnt, eng)
        else:
            st = (eng.k, eng.cnt + 1, eng)
        self._stamp(st, reads, writes)
        return ins

    def dma(self, q, out, in_, reads, writes, ds):
        need = {}
        for t in reads:
            if t.w is not None:
                k, val, _ = t.w
                need[k] = max(need.get(k, 0), val)
        for t in writes:
            if t.w is not None:
                k, val, _ = t.w
                need[k] = max(need.get(k, 0), val)
            for (k, val, _) in t.r.values():
                need[k] = max(need.get(k, 0), val)
        for k, v in need.items():
            if q.waited.get(k, 0) < v:
                q.h.wait_ge(self.sems[k], v)
                q.waited[k] = v
        ins = q.h.dma_start(out=out, in_=in_)
        ds.cnt += 16
        ins.then_inc(self.sems[ds.k], 16)
        self._stamp((ds.k, ds.cnt, None), reads, writes)
        return ins

    def barrier(self):
        for e in self.engs:
            for f in self.engs:
                if f is not e and f.cnt > 0 and e.waited.get(f.k, 0) < f.cnt:
                    e.h.wait_ge(self.sems[f.k], f.cnt)
                    e.waited[f.k] = f.cnt
            for d in self.dsems:
                if d.cnt > 0 and e.waited.get(d.k, 0) < d.cnt:
                    e.h.wait_ge(self.sems[d.k], d.cnt)
                    e.waited[d.k] = d.cnt


def build_nc(stop_after=None, debug=False):
    nc = bass.Bass("TRN2", target_bir_lowering=False)
    EI = dict(kind="ExternalInput")
    xT = nc.dram_tensor("xT", [D, S], F32, **EI).ap()
    pT = nc.dram_tensor("pT", [DEPTH, 256, S], F32, **EI).ap()
    posd = nc.dram_tensor("pos", [128, S], I32, **EI).ap()
    w_in = nc.dram_tensor("w_in", [DEPTH, D, 2148], F32, **EI).ap()
    w_uq = nc.dram_tensor("w_uq", [DEPTH, 192, 384], F32, **EI).ap()
    w_ukv = nc.dram_tensor("w_ukv", [DEPTH, 128, 512], F32, **EI).ap()
    w_r = nc.dram_tensor("w_r", [DEPTH, 8, 64, 64], F32, **EI).ap()
    w_i = nc.dram_tensor("w_i", [DEPTH, 8, 64, 64], F32, **EI).ap()
    w_o = nc.dram_tensor("w_o", [DEPTH, D, D], F32, **EI).ap()
    w_up = nc.dram_tensor("w_up", [DEPTH, D, 5632], F32, **EI).ap()
    w_down = nc.dram_tensor("w_down", [DEPTH, 2816, D], F32, **EI).ap()
    w_pg = nc.dram_tensor("w_ple_gate", [DEPTH, D, D], F32, **EI).ap()
    w_pp = nc.dram_tensor("w_ple_proj", [DEPTH, 256, D], F32, **EI).ap()
    vecs = nc.dram_tensor("vecs", [DEPTH, 128, NV], F32, **EI).ap()
    cident = nc.dram_tensor("cident", [128, 128], F32, **EI).ap()
    cmaskd = nc.dram_tensor("cmask", [128, 128], F32, **EI).ap()
    cseld = nc.dram_tensor("csel", [128, 8, 70], F32, **EI).ap()
    cfreq = nc.dram_tensor("cfreq", [128, 2], F32, **EI).ap()
    outT = nc.dram_tensor("outT", [D, S], F32, kind="ExternalOutput").ap()
    dk = dict(kind="ExternalOutput") if debug else {}
    hA = nc.dram_tensor("hA", [D, S], F32, **dk).ap()
    xnT = nc.dram_tensor("xnT", [D, S], BF16, **dk).ap()
    omixT = nc.dram_tensor("omixT", [D, S], BF16, **dk).ap()
    ropeT = nc.dram_tensor("ropeT", [2, 128, S], F32, **dk).ap()

    hv_x = xT.rearrange("(c p) t -> p c t", p=128)
    hv_a = hA.rearrange("(c p) t -> p c t", p=128)
    hv_o = outT.rearrange("(c p) t -> p c t", p=128)
    xnv = xnT.rearrange("(c p) t -> p c t", p=128)
    omv = omixT.rearrange("(c p) t -> p c t", p=128)

    with contextlib.ExitStack() as gstack:
        k = Ctx(nc, gstack)
        pe, act, dve, pool, sp = k.pe, k.act, k.dve, k.pool, k.sp

        uid = [0]

        def sb(stack, name, shape, dt):
            uid[0] += 1
            return stack.enter_context(nc.sbuf_tensor(f"{name}_{uid[0]}", shape, dt))

        def MM(out, lhsT, rhs, start, stop, reads, writes, inc=None):
            if inc is None:
                inc = stop
            k.op(pe, lambda e: e.matmul(out, lhsT=lhsT, rhs=rhs, start=start, stop=stop),
                 reads, writes, inc)

        def ACT(out, in_, func, reads, writes, bias=None, scale=None, eng=None):
            kw = {}
            if bias is not None:
                kw["bias"] = bias
            if scale is not None:
                kw["scale"] = scale
            k.op(act, lambda e: e.activation(out=out, in_=in_, func=func, **kw), reads, writes)

        def TS(eng, out, in0, s1, s2, op0, op1, reads, writes):
            if op1 is None:
                k.op(eng, lambda e: e.tensor_scalar(out=out, in0=in0, scalar1=s1, scalar2=None, op0=op0),
                     reads, writes)
            else:
                k.op(eng, lambda e: e.tensor_scalar(out=out, in0=in0, scalar1=s1, scalar2=s2, op0=op0, op1=op1),
                     reads, writes)

        def STT(eng, out, in0, scalar, in1, op0, op1, reads, writes):
            k.op(eng, lambda e: e.scalar_tensor_tensor(out=out, in0=in0, scalar=scalar, in1=in1, op0=op0, op1=op1),
                 reads, writes)

        def TTn(eng, out, in0, in1, op, reads, writes):
            k.op(eng, lambda e: e.tensor_tensor(out=out, in0=in0, in1=in1, op=op), reads, writes)

        def CP(eng, out, in_, reads, writes):
            if eng is act:
                k.op(act, lambda e: e.activation(out=out, in_=in_, func=AF.Copy), reads, writes)
            else:
                k.op(eng, lambda e: e.tensor_copy(out=out, in_=in_), reads, writes)

        def MEMSET(eng, ap, val, writes):
            k.op(eng, lambda e: e.memset(ap, val), (), writes)

        def RECIP(out, in_, reads, writes):
            k.op(dve, lambda e: e.reciprocal(out=out, in_=in_), reads, writes)

        ONES = sb(gstack, "ONES", [128, 128], BF16)
        IDENT = sb(gstack, "IDENT", [128, 128], BF16)
        CMASK = sb(gstack, "CMASK", [128, 128], BF16)
        CST = sb(gstack, "CST", [128, 4], F32)
        VL = sb(gstack, "VL", [128, NV], F32)
        tONES, tIDENT, tCMASK, tCST, tVL = (_Tile(n) for n in ("ONES", "IDENT", "CMASK", "CST", "VL"))
        PSB = [gstack.enter_context(nc.psum_tensor(f"ps{i}", [128, 512], F32)) for i in range(8)]
        tPS = [_Tile(f"ps{i}") for i in range(8)]
        d_const = k.dsem("const")
        d_const2 = k.dsem("const2")
        d_vl = k.dsem("vl")
        d_w = [k.dsem(f"w{i}") for i in range(12)]

        MEMSET(dve, ONES[:], 1.0, [tONES])
        MEMSET(dve, CST[:, 0:1], EPS, [tCST])
        MEMSET(dve, CST[:, 1:2], 1.0, [tCST])
        MEMSET(dve, CST[:, 2:3], 0.0, [tCST])
        k.dma(pool, IDENT[:], cident[:, :], [], [tIDENT], d_const)
        k.dma(pool, CMASK[:], cmaskd[:, :], [], [tCMASK], d_const2)

        tH = [_Tile(f"h{t}") for t in range(NT)]
        tXN = [_Tile(f"xn{t}") for t in range(NT)]
        tOM = [[_Tile(f"om{g}_{t}") for t in range(NT)] for g in range(3)]
        tROPE = _Tile("rope")
        tNONE = _Tile("ext")

        def cols(t):
            return slice(t * TT, (t + 1) * TT)

        def rmsnorm(srcs, rows, gcols, nfeat, outs, SQ, tSQ, RS, tRS, RSTD, tRSTD, psn, src_tiles, out_tiles,
                    sq_eng=None):
            C = len(srcs)
            for c in range(C):
                ACT(SQ[0:rows[c], c, :], srcs[c], AF.Square, src_tiles, [tSQ])
            for c in range(C):
                MM(PSB[psn][:, :], ONES[0:rows[c], :], SQ[0:rows[c], c, :], c == 0, c == C - 1,
                   [tSQ, tONES], [tPS[psn]])
            ACT(RS[:, :], PSB[psn][:, :], AF.Sqrt, [tPS[psn], tCST], [tRS], bias=CST[:, 0:1], scale=1.0 / nfeat)
            RECIP(RSTD[:, :], RS[:, :], [tRS], [tRSTD])
            for c in range(C):
                STT(dve, outs[c], srcs[c], gcols[c], RSTD[0:rows[c], :], ALU.mult, ALU.mult,
                    src_tiles + [tRSTD, tVL], out_tiles)

        with contextlib.ExitStack() as st:
            POSI = sb(st, "POSI", [128, S], I32)
            POSF = sb(st, "POSF", [128, S], F32)
            ANG = sb(st, "ANG", [128, S], F32)
            YT = sb(st, "YT", [128, S], F32)
            FRQ = sb(st, "FRQ", [128, 2], F32)
            CSB = sb(st, "CSB", [128, S], F32)
            CSB2 = sb(st, "CSB2", [128, S], F32)
            tCSB, tCSB2 = _Tile("CSB"), _Tile("CSB2")
            tPOSI, tPOSF, tANG, tYT, tFRQ = (_Tile(n) for n in ("POSI", "POSF", "ANG", "YT", "FRQ"))
            d_p = k.dsem("pro")
            R = slice(64, 96)
            k.dma(sp, POSI[R, :], posd[R, :], [], [tPOSI], d_p)
            d_p2 = k.dsem("pro2")
            k.dma(sp, FRQ[:], cfreq[:, :], [], [tFRQ], d_p2)
            CP(dve, POSF[R, :], POSI[R, :], [tPOSI], [tPOSF])
            TS(dve, ANG[R, :], POSF[R, :], FRQ[R, 0:1], None, ALU.mult, None, [tPOSF, tFRQ], [tANG])
            TS(dve, ANG[R, :], ANG[R, :], 1.0 / (2 * math.pi), None, ALU.mult, None, [tANG], [tANG])

            def reduce_turns(shift):
                TS(dve, YT[R, :], ANG[R, :], shift, None, ALU.add, None, [tANG], [tYT])
                CP(dve, POSI[R, :], YT[R, :], [tYT], [tPOSI])
                CP(dve, POSF[R, :], POSI[R, :], [tPOSI], [tPOSF])
                TTn(dve, YT[R, :], YT[R, :], POSF[R, :], ALU.subtract, [tYT, tPOSF], [tYT])
                TS(dve, POSF[R, :], YT[R, :], 0.5, None, ALU.is_gt, None, [tYT], [tPOSF])
                TTn(dve, YT[R, :], YT[R, :], POSF[R, :], ALU.subtract, [tYT, tPOSF], [tYT])
                TS(dve, POSF[R, :], YT[R, :], -0.5, None, ALU.is_lt, None, [tYT], [tPOSF])
                TTn(dve, YT[R, :], YT[R, :], POSF[R, :], ALU.add, [tYT, tPOSF], [tYT])

            reduce_turns(0.25)
            ACT(CSB[R, :], YT[R, :], AF.Sin, [tYT], [tCSB], scale=2 * math.pi)
            k.dma(sp, ropeT[0, R, :], CSB[R, :], [tCSB], [tROPE], d_p)
            reduce_turns(0.0)
            ACT(CSB2[R, :], YT[R, :], AF.Sin, [tYT, tFRQ], [tCSB2], scale=FRQ[R, 1:2])
            k.dma(sp, ropeT[1, R, :], CSB2[R, :], [tCSB2], [tROPE], d_p)
            k.barrier()
        if stop_after == "rope":
            k.barrier()
            return nc

        def attention(l, QT, tQT, KT, tKT, VA, tVA, kd, scale, grp, gcol0):
            with contextlib.ExitStack() as st:
                PT = [sb(st, f"PT{i}", [128, 512], BF16) for i in range(3)]
                tPT = [_Tile(f"PT{i}") for i in range(3)]
                RC = sb(st, "RC", [128, 512], F32)
                OHb = [sb(st, f"OH{i}", [128, 4, 512], F32) for i in range(2)]
                tOHb = [_Tile("OH0"), _Tile("OH1")]
                SQ = sb(st, "SQa", [128, 4, 512], BF16)
                RS = sb(st, "RSa", [128, 512], F32)
                RSTD = sb(st, "RSTDa", [128, 512], F32)
                OMX = [sb(st, f"OMXa{i}", [128, 4, 512], BF16) for i in range(2)]
                tRC, tSQ, tRS, tRSTD = (_Tile(n) for n in ("RC", "SQa", "RSa", "RSTDa"))
                tOMX = [_Tile("OMXa0"), _Tile("OMXa1")]
                d_st = [k.dsem("ast0"), k.dsem("ast1")]
                tiles = [(j, h, i) for j in range(NT) for h in range(4) for i in range(4 * j + 4)]
                LOOK = 2
                deferred = []

                def emit_score(idx):
                    j, h, i = tiles[idx]
                    r = i - 4 * j
                    c0 = 128 * r if r >= 0 else 0
                    pb = idx % 3
                    ksl = slice(i * 128, (i + 1) * 128)
                    q0 = j * TT
                    if r >= 0:
                        MM(PSB[pb][:, c0:c0 + 128], KT[0:kd, h, ksl], QT[0:kd, h, q0 + c0:q0 + c0 + 128],
                           True, False, [tKT, tQT], [tPS[pb]], inc=False)
                        MM(PSB[pb][:, c0:c0 + 128], IDENT[:, :], CMASK[:, :], False, True,
                           [tIDENT, tCMASK], [tPS[pb]], inc=(c0 + 128 == 512))
                        if c0 + 128 < 512:
                            MM(PSB[pb][:, c0 + 128:512], KT[0:kd, h, ksl], QT[0:kd, h, q0 + c0 + 128:q0 + 512],
                               True, True, [tKT, tQT], [tPS[pb]])
                    else:
                        MM(PSB[pb][:, :], KT[0:kd, h, ksl], QT[0:kd, h, q0:q0 + 512], True, True,
                           [tKT, tQT], [tPS[pb]])

                def emit_norm(j):
                    om = OMX[j % 2]
                    tom = tOMX[j % 2]
                    oh, toh = OHb[j % 2], tOHb[j % 2]
                    rmsnorm([oh[0:64, h, :] for h in range(4)], [64] * 4,
                            [VL[0:64, C_GOUT64 + gcol0 + h:C_GOUT64 + gcol0 + h + 1] for h in range(4)], 256.0,
                            [om[0:64, h, :] for h in range(4)], SQ, tSQ, RS, tRS, RSTD, tRSTD, 5, [toh], [tom])
                    r0 = 256 * grp
                    k.dma(sp, omixT[r0:r0 + 256, cols(j)].rearrange("(h p) t -> p h t", p=64), om[0:64, :, :],
                          [tom], [tOM[grp][j]], d_st[j % 2])

                def emit_exp_pv(idx, step):
                    j, h, i = tiles[idx]
                    nk = 4 * j + 4
                    r = i - 4 * j
                    c0 = 128 * r if r >= 0 else 0
                    pb = idx % 3
                    ob = 3 + (h % 2)
                    ACT(PT[pb][:, c0:512], PSB[pb][:, c0:512], AF.Exp, [tPS[pb]], [tPT[pb]], scale=scale)
                    MM(PSB[ob][:, c0:512], VA[:, i, h, :], PT[pb][:, c0:512], i == 0, i == nk - 1,
                       [tVA, tPT[pb]], [tPS[ob]])
                    if i == nk - 1:
                        oh, toh = OHb[j % 2], tOHb[j % 2]
                        RECIP(RC[64:128, :], PSB[ob][64:128, :], [tPS[ob]], [tRC])
                        TTn(dve, oh[0:64, h, :], PSB[ob][0:64, :], RC[64:128, :], ALU.mult, [tPS[ob], tRC], [toh])
                        if h == 3:
                            deferred.append((step + 6, lambda jj=j: emit_norm(jj)))

                nt_ = len(tiles)
                for step in range(nt_ + LOOK):
                    if step < nt_:
                        emit_score(step)
                    if step >= LOOK:
                        emit_exp_pv(step - LOOK, step)
                    while deferred and deferred[0][0] <= step:
                        deferred.pop(0)[1]()
                while deferred:
                    deferred.pop(0)[1]()
                k.barrier()

        for l in range(DEPTH):
            hsrc = hv_x if l == 0 else hv_a
            tHs = [tNONE] * NT if l == 0 else tH
            k.dma(sp, VL[:], vecs[l, :, :], [], [tVL], d_vl)
            win = w_in[l].rearrange("(c p) n -> p c n", p=128)

            with contextlib.ExitStack() as st:
                QM = sb(st, "QM", [128, 4, S], BF16)
                KM = sb(st, "KM", [128, 4, S], BF16)
                VA = sb(st, "VA", [128, 32, 4, 128], BF16)
                tQM, tKM, tVA = _Tile("QM"), _Tile("KM"), _Tile("VA")
                MEMSET(pool, VA[:, :, :, 64:128], 1.0, [tVA])
                with contextlib.ExitStack() as s1:
                    HTb1 = [sb(s1, f"HT1{i}", [128, 8, TT], F32) for i in range(2)]
                    tHTb1 = [_Tile("HT10"), _Tile("HT11")]
                    d_hb = [k.dsem("p1h0"), k.dsem("p1h1")]
                    XN = sb(s1, "XN", [128, 8, TT], BF16)
                    SQ = sb(s1, "SQ", [128, 8, TT], BF16)
                    RS = sb(s1, "RS", [128, TT], F32)
                    RSTD = sb(s1, "RSTD", [128, TT], F32)
                    QCN = sb(s1, "QCN", [128, 2, TT], BF16)
                    KVN = sb(s1, "KVN", [128, TT], BF16)
                    T1 = sb(s1, "T1", [128, TT], F32)
                    T2 = sb(s1, "T2", [128, TT], F32)
                    RC_ = sb(s1, "ROPEC", [128, TT], F32)
                    RS_ = sb(s1, "ROPES", [128, TT], F32)
                    WA = sb(s1, "WA", [128, 8, 320], BF16)
                    WKA = sb(s1, "WKA", [128, 8, 96], BF16)
                    WKB = sb(s1, "WKB", [128, 8, 96], BF16)
                    WUQ = sb(s1, "WUQ", [128, 2, 384], BF16)
                    WUQB = sb(s1, "WUQB", [128, 2, 4, 96], BF16)
                    WUKV = sb(s1, "WUKV", [128, 512], BF16)
                    WV = sb(s1, "WV", [128, 4, 64], BF16)
                    (tXNs, tSQ, tRS, tRSTD, tQCN, tKVN, tT1, tT2, tRC_, tRS_, tW) = (
                        _Tile(n) for n in ("XN", "SQ", "RS", "RSTD", "QCN", "KVN", "T1", "T2", "RC_", "RS_", "W1"))
                    d_xs, d_r, d_r2 = k.dsem("p1xs"), k.dsem("p1r"), k.dsem("p1r2")
                    MEMSET(pool, WKA[:], 0.0, [tW])
                    MEMSET(pool, WKB[:], 0.0, [tW])
                    MEMSET(pool, WUQB[:], 0.0, [tW])
                    MEMSET(pool, WUQ[:], 0.0, [tW])
                    k.dma(pool, WA[:], win[:, :, 0:320], [], [tW], d_w[0])
                    k.dma(pool, WKA[:, :, 64:96], win[:, :, 320:352], [], [tW], d_w[0])
                    k.dma(pool, WKB[:, :, 64:80], win[:, :, 336:352], [], [tW], d_w[0])
                    k.dma(pool, WKB[:, :, 80:96], win[:, :, 320:336], [], [tW], d_w[0])
                    k.dma(pool, WUQ[:, 0, :], w_uq[l, 0:128, :], [], [tW], d_w[0])
                    k.dma(pool, WUQ[0:64, 1, :], w_uq[l, 128:192, :], [], [tW], d_w[0])
                    uq4 = w_uq[l].rearrange("r (h d) -> r h d", h=4)
                    k.dma(pool, WUQB[:, 0, :, 64:80], uq4[0:128, :, 80:96], [], [tW], d_w[0])
                    k.dma(pool, WUQB[:, 0, :, 80:96], uq4[0:128, :, 64:80], [], [tW], d_w[0])
                    k.dma(pool, WUQB[0:64, 1, :, 64:80], uq4[128:192, :, 80:96], [], [tW], d_w[0])
                    k.dma(pool, WUQB[0:64, 1, :, 80:96], uq4[128:192, :, 64:80], [], [tW], d_w[0])
                    k.dma(pool, WUKV[:], w_ukv[l, :, :], [], [tW], d_w[0])
                    k.dma(pool, WV[:], w_ukv[l].rearrange("r (h d) -> r h d", h=4)[:, :, 64:128], [], [tW], d_w[0])
                    k.dma(sp, HTb1[0][:], hsrc[:, :, cols(0)], [tHs[0]], [tHTb1[0]], d_hb[0])
                    for t in range(NT):
                        cs = cols(t)
                        HT, tHT = HTb1[t % 2], tHTb1[t % 2]
                        if t + 1 < NT:
                            k.dma(sp, HTb1[(t + 1) % 2][:], hsrc[:, :, cols(t + 1)], [tHs[t + 1]], [tHTb1[(t + 1) % 2]],
                                  d_hb[(t + 1) % 2])
                        k.dma(sp, RC_[64:96, :], ropeT[0, 64:96, cs], [tROPE], [tRC_], d_r)
                        k.dma(sp, RS_[64:96, :], ropeT[1, 64:96, cs], [tROPE], [tRS_], d_r2)
                        rmsnorm([HT[:, c, :] for c in range(8)], [128] * 8,
                                [VL[:, C_GMIX + c:C_GMIX + c + 1] for c in range(8)], 1024.0,
                                [XN[:, c, :] for c in range(8)], SQ, tSQ, RS, tRS, RSTD, tRSTD, 7, [tHT], [tXNs])
                        k.dma(sp, xnv[:, :, cs], XN[:], [tXNs], [tXN[t]], d_xs)
                        specs = [(0, 128, WA, slice(0, 128)), (1, 64, WA, slice(128, 192)), (2, 128, WA, slice(192, 320)),
                                 (3, 96, WKA, slice(0, 96)), (4, 96, WKB, slice(0, 96))]
                        for (pb, m, Wt, csl) in specs:
                            for c in range(8):
                                MM(PSB[pb][0:m, :], Wt[:, c, csl], XN[:, c, :], c == 0, c == 7, [tW, tXNs], [tPS[pb]])
                        rmsnorm([PSB[0][:, :], PSB[1][0:64, :]], [128, 64],
                                [VL[:, C_GQC:C_GQC + 1], VL[0:64, C_GQC + 1:C_GQC + 2]], 192.0,
                                [QCN[:, 0, :], QCN[0:64, 1, :]], SQ, tSQ, RS, tRS, RSTD, tRSTD, 7,
                                [tPS[0], tPS[1]], [tQCN])
                        rmsnorm([PSB[2][:, :]], [128], [VL[:, C_GKVC:C_GKVC + 1]], 128.0, [KVN[:, :]],
                                SQ, tSQ, RS, tRS, RSTD, tRSTD, 7, [tPS[2]], [tKVN])
                        TTn(dve, T1[64:96, :], PSB[3][64:96, :], RC_[64:96, :], ALU.mult, [tPS[3], tRC_], [tT1])
                        TTn(dve, T2[64:96, :], PSB[4][64:96, :], RS_[64:96, :], ALU.mult, [tPS[4], tRS_], [tT2])
                        TTn(pool, T1[64:96, :], T1[64:96, :], T2[64:96, :], ALU.add, [tT1, tT2], [tT1])
                        for h in range(4):
                            CP(pool, KM[64:96, h, cs], T1[64:96, :], [tT1], [tKM])
                        for h in range(4):
                            pa, pb2 = (0, 1) if h % 2 == 0 else (5, 6)
                            hs = slice(h * 96, (h + 1) * 96)
                            MM(PSB[pa][0:96, :], WUQ[:, 0, hs], QCN[:, 0, :], True, False, [tW, tQCN], [tPS[pa]])
                            MM(PSB[pa][0:96, :], WUQ[0:64, 1, hs], QCN[0:64, 1, :], False, True, [tW, tQCN], [tPS[pa]])
                            MM(PSB[pb2][0:96, :], WUQB[:, 0, h, :], QCN[:, 0, :], True, False, [tW, tQCN], [tPS[pb2]])
                            MM(PSB[pb2][0:96, :], WUQB[0:64, 1, h, :], QCN[0:64, 1, :], False, True, [tW, tQCN], [tPS[pb2]])
                            CP(act, QM[0:64, h, cs], PSB[pa][0:64, :], [tPS[pa]], [tQM])
                            TTn(dve, T1[64:96, :], PSB[pa][64:96, :], RC_[64:96, :], ALU.mult, [tPS[pa], tRC_], [tT1])
                            TTn(dve, T2[64:96, :], PSB[pb2][64:96, :], RS_[64:96, :], ALU.mult, [tPS[pb2], tRS_], [tT2])
                            TTn(pool, QM[64:96, h, cs], T1[64:96, :], T2[64:96, :], ALU.add, [tT1, tT2], [tQM])
                        for h in range(4):
                            pb = 2 + (h % 2)
                            MM(PSB[pb][0:64, :], WUKV[:, h * 128:h * 128 + 64], KVN[:, :], True, True, [tW, tKVN], [tPS[pb]])
                            CP(act, KM[0:64, h, cs], PSB[pb][0:64, :], [tPS[pb]], [tKM])
                        for s_ in range(4):
                            pb = 5 + (s_ % 2)
                            MM(PSB[pb][:, 0:256], KVN[:, s_ * 128:(s_ + 1) * 128], WV[:, :, :].rearrange("p h d -> p (h d)"),
                               True, True, [tKVN, tW], [tPS[pb]])
                            CP(dve, VA[:, t * 4 + s_, :, 0:64], PSB[pb][:, 0:256].rearrange("p (h d) -> p h d", h=4),
                               [tPS[pb]], [tVA])
                    k.barrier()
                if stop_after == f"p1_{l}":
                    return nc
                attention(l, QM, tQM, KM, tKM, VA, tVA, 96, 96.0 ** -0.5, 0, 0)
            if stop_after == f"p2_{l}":
                return nc

            with contextlib.ExitStack() as st:
                FQ = sb(st, "FQ", [128, 4, S], BF16)
                FK = sb(st, "FK", [128, 4, S], BF16)
                FVA = sb(st, "FVA", [128, 32, 4, 128], BF16)
                tFQ, tFK, tFVA = _Tile("FQ"), _Tile("FK"), _Tile("FVA")
                MEMSET(pool, FVA[:, :, :, 64:128], 1.0, [tFVA])
                with contextlib.ExitStack() as s1:
                    XNb = [sb(s1, f"XNf{i}", [128, 8, TT], BF16) for i in range(2)]
                    tXNb = [_Tile("XNf0"), _Tile("XNf1")]
                    d_x = [k.dsem("p3x0"), k.dsem("p3x1")]
                    CC = sb(s1, "CC", [128, TT], BF16)
                    EX = sb(s1, "EX", [4, TT], F32)
                    NL = sb(s1, "NL", [4, TT], F32)
                    ONE4 = sb(s1, "ONE4", [4, TT], F32)
                    CF = sb(s1, "CF", [4, TT], F32)
                    R1 = sb(s1, "R1", [4, TT], F32)
                    CHI = sb(s1, "CHI", [4, TT], BF16)
                    CMID = sb(s1, "CMID", [4, TT], BF16)
                    CLO = sb(s1, "CLO", [4, TT], BF16)
                    CARRY = sb(s1, "CARRY", [4, 2], F32)
                    NEGB = sb(s1, "NEGB", [4, 2], F32)
                    WFQ = sb(s1, "WFQ", [128, 8, 4, 70], BF16)
                    WFK = sb(s1, "WFK", [128, 8, 4, 70], BF16)
                    WFV = sb(s1, "WFV", [128, 8, 256], BF16)
                    WF4 = sb(s1, "WF4", [128, 8, 4], BF16)
                    SEL = sb(s1, "SEL", [128, 8, 70], BF16)
                    (tCC, tEX, tNL, tONE4, tCF, tR1, tCHI, tCMID, tCLO, tCARRY, tNEGB, tW) = (
                        _Tile(n) for n in ("CC", "EX", "NL", "ONE4", "CF", "R1", "CHI", "CMID", "CLO", "CARRY", "NEGB", "W3"))
                    MEMSET(pool, WFQ[:], 0.0, [tW])
                    MEMSET(pool, WFK[:], 0.0, [tW])
                    MEMSET(dve, CC[:], 0.0, [tCC])
                    MEMSET(dve, CC[96:97, :], 1.0, [tCC])
                    MEMSET(dve, ONE4[:], 1.0, [tONE4])
                    MEMSET(dve, CARRY[:], 0.0, [tCARRY])
                    TS(dve, NEGB[0:4, 0:1], VL[0:4, C_BF:C_BF + 1], -1.0, None, ALU.mult, None, [tVL], [tNEGB])
                    for c in range(8):
                        k.dma(pool, WFQ[:, c, :, 0:64], win[:, c, 352:608].rearrange("p (h d) -> p h d", h=4), [], [tW], d_w[1])
                        k.dma(pool, WFK[:, c, :, 0:64], win[:, c, 608:864].rearrange("p (h d) -> p h d", h=4), [], [tW], d_w[1])
                    k.dma(pool, WFV[:], win[:, :, 864:1120], [], [tW], d_w[1])
                    k.dma(pool, WF4[:], win[:, :, 1120:1124], [], [tW], d_w[1])
                    k.dma(pool, SEL[:], cseld[:, :, :], [], [tW], d_w[1])
                    k.dma(sp, XNb[0][:], xnv[:, :, cols(0)], [tXN[0]], [tXNb[0]], d_x[0])
                    for t in range(NT):
                        cs = cols(t)
                        XN, tXNs = XNb[t % 2], tXNb[t % 2]
                        if t + 1 < NT:
                            k.dma(sp, XNb[(t + 1) % 2][:], xnv[:, :, cols(t + 1)], [tXN[t + 1]], [tXNb[(t + 1) % 2]],
                                  d_x[(t + 1) % 2])
                        for c in range(8):
                            MM(PSB[7][0:4, :], WF4[:, c, :], XN[:, c, :], c == 0, c == 7, [tW, tXNs], [tPS[7]])
                        ACT(EX[:, :], PSB[7][0:4, :], AF.Exp, [tPS[7], tNEGB], [tEX], bias=NEGB[0:4, 0:1], scale=-1.0)
                        ACT(NL[:, :], EX[:, :], AF.Ln, [tEX, tCST], [tNL], bias=CST[0:4, 1:2], scale=1.0)
                        TS(dve, NL[:, :], NL[:, :], -1.0, None, ALU.mult, None, [tNL], [tNL])
                        k.op(dve, lambda e: e.tensor_tensor_scan(out=CF[:, :], data0=ONE4[:, :], data1=NL[:, :],
                                                                 initial=CARRY[0:4, 0:1], op0=ALU.mult, op1=ALU.add),
                             [tONE4, tNL, tCARRY], [tCF])
                        CP(dve, CARRY[0:4, 0:1], CF[:, TT - 1:TT], [tCF], [tCARRY])
                        CP(dve, CHI[:, :], CF[:, :], [tCF], [tCHI])
                        TTn(dve, R1[:, :], CF[:, :], CHI[:, :], ALU.subtract, [tCF, tCHI], [tR1])
                        CP(dve, CMID[:, :], R1[:, :], [tR1], [tCMID])
                        TTn(dve, R1[:, :], R1[:, :], CMID[:, :], ALU.subtract, [tR1, tCMID], [tR1])
                        CP(dve, CLO[:, :], R1[:, :], [tR1], [tCLO])
                        CP(pool, CC[0:4, :], CHI[:, :], [tCHI], [tCC])
                        CP(pool, CC[32:36, :], CMID[:, :], [tCMID], [tCC])
                        CP(pool, CC[64:68, :], CLO[:, :], [tCLO], [tCC])
                        for h in range(4):
                            for (qi, Wt, DST, tD) in ((0, WFQ, FQ, tFQ), (1, WFK, FK, tFK)):
                                pb = (2 * h + qi) % 4
                                for c in range(8):
                                    MM(PSB[pb][0:70, :], Wt[:, c, h, :], XN[:, c, :], c == 0, False, [tW, tXNs], [tPS[pb]])
                                MM(PSB[pb][0:70, :], SEL[:, qi * 4 + h, :], CC[:, :], False, True, [tW, tCC], [tPS[pb]])
                                CP(act if qi == 0 else dve, DST[0:70, h, cs], PSB[pb][0:70, :], [tPS[pb]], [tD])
                        for s_ in range(4):
                            pb = 4 + (s_ % 2)
                            for c in range(8):
                                MM(PSB[pb][:, 0:256], XN[:, c, s_ * 128:(s_ + 1) * 128], WFV[:, c, :], c == 0, c == 7,
                                   [tXNs, tW], [tPS[pb]])
                            CP(act if s_ % 2 else dve, FVA[:, t * 4 + s_, :, 0:64],
                               PSB[pb][:, 0:256].rearrange("p (h d) -> p h d", h=4), [tPS[pb]], [tFVA])
                    k.barrier()
                if stop_after == f"p3_{l}":
                    return nc
                attention(l, FQ, tFQ, FK, tFK, FVA, tFVA, 70, 0.125, 1, 4)
            if stop_after == f"p4_{l}":
                return nc

            with contextlib.ExitStack() as s1:
                XNb = [sb(s1, f"XNl{i}", [128, 8, TT], BF16) for i in range(2)]
                tXNb = [_Tile("XNl0"), _Tile("XNl1")]
                d_x = [k.dsem("p5x0"), k.dsem("p5x1")]
                LXB = sb(s1, "LXB", [128, 4, TT + 3], F32)
                XC = sb(s1, "XC", [128, 4, TT], F32)
                XCB = sb(s1, "XCB", [128, 4, TT], BF16)
                RG = sb(s1, "RG", [128, TT], F32)
                IG = sb(s1, "IG", [128, TT], F32)
                AG = sb(s1, "AG", [128, TT], F32)
                A2 = sb(s1, "A2", [128, TT], F32)
                BX = sb(s1, "BX", [128, TT], F32)
                GL = sb(s1, "GL", [128, TT], F32)
                HS = sb(s1, "HS", [128, 4, TT], F32)
                OL = sb(s1, "OL", [128, 4, TT], F32)
                SQ = sb(s1, "SQl", [128, 4, TT], BF16)
                RS = sb(s1, "RSl", [128, TT], F32)
                RSTD = sb(s1, "RSTDl", [128, TT], F32)
                OMX = [sb(s1, f"OMXl{i}", [128, 4, TT], BF16) for i in range(2)]
                tOMX = [_Tile("OMXl0"), _Tile("OMXl1")]
                d_st = [k.dsem("lst0"), k.dsem("lst1")]
                CARH = sb(s1, "CARH", [128, 4], F32)
                COEF = sb(s1, "COEF", [128, 4], F32)
                WL = sb(s1, "WL", [128, 8, 1024], BF16)
                WRB = sb(s1, "WRB", [128, 4, 128], BF16)
                WIB = sb(s1, "WIB", [128, 4, 128], BF16)
                (tLXB, tXC, tXCB, tRG, tIG, tAG, tA2, tBX, tGL, tHS, tOL, tSQ, tRS, tRSTD, tCARH, tCOEF, tW) = (
                    _Tile(n) for n in ("LXB", "XC", "XCB", "RG", "IG", "AG", "A2", "BX", "GL", "HS", "OL", "SQl", "RSl",
                                       "RSTDl", "CARH", "COEF", "W5"))
                MEMSET(pool, WRB[:], 0.0, [tW])
                MEMSET(pool, WIB[:], 0.0, [tW])
                MEMSET(dve, LXB[:], 0.0, [tLXB])
                MEMSET(dve, CARH[:], 0.0, [tCARH])
                ACT(COEF[:, :], VL[:, C_LAM:C_LAM + 4], AF.Exp, [tVL], [tCOEF], scale=-1.0)
                ACT(COEF[:, :], COEF[:, :], AF.Ln, [tCOEF, tCST], [tCOEF], bias=CST[:, 1:2], scale=1.0)
                TS(dve, COEF[:, :], COEF[:, :], -8.0, None, ALU.mult, None, [tCOEF], [tCOEF])
                k.dma(pool, WL[:], win[:, :, 1124:2148], [], [tW], d_w[2])
                for c in range(4):
                    for hb in range(2):
                        ps_ = slice(hb * 64, hb * 64 + 64)
                        k.dma(pool, WRB[ps_, c, ps_], w_r[l, 2 * c + hb, :, :], [], [tW], d_w[2])
                        k.dma(pool, WIB[ps_, c, ps_], w_i[l, 2 * c + hb, :, :], [], [tW], d_w[2])
                k.dma(sp, XNb[0][:], xnv[:, :, cols(0)], [tXN[0]], [tXNb[0]], d_x[0])
                for t in range(NT):
                    cs = cols(t)
                    XN, tXNs = XNb[t % 2], tXNb[t % 2]
                    if t + 1 < NT:
                        k.dma(sp, XNb[(t + 1) % 2][:], xnv[:, :, cols(t + 1)], [tXN[t + 1]], [tXNb[(t + 1) % 2]],
                              d_x[(t + 1) % 2])
                    for c in range(4):
                        pb = c % 2
                        for kc in range(8):
                            MM(PSB[pb][:, :], WL[:, kc, c * 128:(c + 1) * 128], XN[:, kc, :], kc == 0, kc == 7,
                               [tW, tXNs], [tPS[pb]])
                        CP(pool, LXB[:, c, 0:3], LXB[:, c, TT:TT + 3], [tLXB], [tLXB])
                        CP(act, LXB[:, c, 3:TT + 3], PSB[pb][:, :], [tPS[pb]], [tLXB])
                        wcol = lambda kk: VL[:, C_LCW + kk * 4 + c:C_LCW + kk * 4 + c + 1]
                        TS(pool, XC[:, c, :], LXB[:, c, 3:TT + 3], wcol(3), VL[:, C_LCB + c:C_LCB + c + 1], ALU.mult, ALU.add,
                           [tLXB, tVL], [tXC])
                        for kk in range(3):
                            STT(dve, XC[:, c, :], LXB[:, c, kk:kk + TT], wcol(kk), XC[:, c, :], ALU.mult, ALU.add,
                                [tLXB, tXC, tVL], [tXC])
                        CP(pool, XCB[:, c, :], XC[:, c, :], [tXC], [tXCB])
                        MM(PSB[2][:, :], WRB[:, c, :], XCB[:, c, :], True, True, [tW, tXCB], [tPS[2]])
                        MM(PSB[3][:, :], WIB[:, c, :], XCB[:, c, :], True, True, [tW, tXCB], [tPS[3]])
                        ACT(RG[:, :], PSB[2][:, :], AF.Sigmoid, [tPS[2], tVL], [tRG], bias=VL[:, C_BR + c:C_BR + c + 1])
                        ACT(IG[:, :], PSB[3][:, :], AF.Sigmoid, [tPS[3], tVL], [tIG], bias=VL[:, C_BI + c:C_BI + c + 1])
                        ACT(AG[:, :], RG[:, :], AF.Exp, [tRG, tCOEF], [tAG], scale=COEF[:, c:c + 1])
                        TTn(dve, A2[:, :], AG[:, :], AG[:, :], ALU.mult, [tAG], [tA2])
                        TS(dve, A2[:, :], A2[:, :], -1.0, 1.0, ALU.mult, ALU.add, [tA2], [tA2])
                        ACT(A2[:, :], A2[:, :], AF.Sqrt, [tA2], [tA2])
                        TTn(dve, BX[:, :], IG[:, :], XC[:, c, :], ALU.mult, [tIG, tXC], [tBX])
                        TTn(dve, BX[:, :], BX[:, :], A2[:, :], ALU.mult, [tBX, tA2], [tBX])
                        k.op(dve, lambda e: e.tensor_tensor_scan(out=HS[:, c, :], data0=AG[:, :], data1=BX[:, :],
                                                                 initial=CARH[:, c:c + 1], op0=ALU.mult, op1=ALU.add),
                             [tAG, tBX, tCARH], [tHS])
                        CP(dve, CARH[:, c:c + 1], HS[:, c, TT - 1:TT], [tHS], [tCARH])
                        pg = 4 + (c % 2)
                        for kc in range(8):
                            MM(PSB[pg][:, :], WL[:, kc, 512 + c * 128:512 + (c + 1) * 128], XN[:, kc, :], kc == 0, kc == 7,
                               [tW, tXNs], [tPS[pg]])
                        ACT(GL[:, :], PSB[pg][:, :], AF.Gelu_apprx_tanh, [tPS[pg]], [tGL])
                        TTn(dve, OL[:, c, :], HS[:, c, :], GL[:, :], ALU.mult, [tHS, tGL], [tOL])
                    om, tom = OMX[t % 2], tOMX[t % 2]
                    rmsnorm([OL[:, c, :] for c in range(4)], [128] * 4,
                            [VL[:, C_GOUTL + c:C_GOUTL + c + 1] for c in range(4)], 512.0,
                            [om[:, c, :] for c in range(4)], SQ, tSQ, RS, tRS, RSTD, tRSTD, 7, [tOL], [tom])
                    k.dma(sp, omv[:, 4:8, cs], om[:, :, :], [tom], [tOM[2][t]], d_st[t % 2])
                k.barrier()
            if stop_after == f"p5_{l}":
                return nc

            with contextlib.ExitStack() as s1:
                WO = sb(s1, "WO", [128, 8, 1024], BF16)
                OMb = [sb(s1, f"OMt{i}", [128, 8, TT], BF16) for i in range(2)]
                HTb = [sb(s1, f"HT6{i}", [128, 8, TT], F32) for i in range(2)]
                XN2 = sb(s1, "XN2", [128, 8, TT], BF16)
                SQ = sb(s1, "SQ6", [128, 8, TT], BF16)
                RS = sb(s1, "RS6", [128, TT], F32)
                RSTD = sb(s1, "RSTD6", [128, TT], F32)
                tOMb = [_Tile("OMt0"), _Tile("OMt1")]
                tHTb = [_Tile("HT60"), _Tile("HT61")]
                tXN2, tSQ, tRS, tRSTD, tW = (_Tile(n) for n in ("XN2", "SQ6", "RS6", "RSTD6", "W6"))
                d_o = [k.dsem("p6o0"), k.dsem("p6o1")]
                d_h = [k.dsem("p6h0"), k.dsem("p6h1")]
                d_xs = k.dsem("p6xs")
                k.dma(pool, WO[:], w_o[l].rearrange("(c p) n -> p c n", p=128), [], [tW], d_w[3])

                def ld6(t):
                    i = t % 2
                    k.dma(sp, OMb[i][:], omv[:, :, cols(t)], [tOM[0][t], tOM[1][t], tOM[2][t]], [tOMb[i]], d_o[i])
                    k.dma(sp, HTb[i][:], hsrc[:, :, cols(t)], [tHs[t]], [tHTb[i]], d_h[i])
                ld6(0)
                for t in range(NT):
                    i = t % 2
                    cs = cols(t)
                    if t + 1 < NT:
                        ld6(t + 1)
                    HT, tHT = HTb[i], tHTb[i]
                    for m in range(8):
                        pb = m % 4
                        for c in range(8):
                            MM(PSB[pb][:, :], WO[:, c, m * 128:(m + 1) * 128], OMb[i][:, c, :], c == 0, c == 7,
                               [tW, tOMb[i]], [tPS[pb]])
                        TTn(dve, HT[:, m, :], PSB[pb][:, :], HT[:, m, :], ALU.add, [tPS[pb], tHT], [tHT])
                    k.dma(sp, hv_a[:, :, cs], HT[:], [tHT], [tH[t]], d_h[i])
                    rmsnorm([HT[:, c, :] for c in range(8)], [128] * 8,
                            [VL[:, C_GFFN + c:C_GFFN + c + 1] for c in range(8)], 1024.0,
                            [XN2[:, c, :] for c in range(8)], SQ, tSQ, RS, tRS, RSTD, tRSTD, 7, [tHT], [tXN2])
                    k.dma(sp, xnv[:, :, cs], XN2[:], [tXN2], [tXN[t]], d_xs)
                k.barrier()
            if stop_after == f"p6_{l}":
                return nc

            wup = w_up[l].rearrange("(c p) n -> p c n", p=128)
            wdn = w_down[l].rearrange("(j p) n -> p j n", p=128)
            last = (l == DEPTH - 1)
            for half in range(2):
                with contextlib.ExitStack() as s1:
                    WUG = sb(s1, "WUG", [128, 8, 11 * 128], BF16)
                    WUV = sb(s1, "WUV", [128, 8, 11 * 128], BF16)
                    WD = sb(s1, "WD", [128, 11, 1024], BF16)
                    XNb = [sb(s1, f"XN7{i}", [128, 8, TT], BF16) for i in range(2)]
                    tXNb = [_Tile("XN70"), _Tile("XN71")]
                    nhb = 2
                    HTb = [sb(s1, f"HT7{i}", [128, 8, TT], F32) for i in range(nhb)]
                    tHTb = [_Tile(f"HT7{i}") for i in range(nhb)]
                    ATb = [sb(s1, f"AT{i}", [128, 11, TT], BF16) for i in range(2)]
                    tATb = [_Tile("AT0"), _Tile("AT1")]
                    ACG = [sb(s1, f"ACG{i}", [128, TT], F32) for i in range(2)]
                    ACV = [sb(s1, f"ACV{i}", [128, TT], F32) for i in range(2)]
                    SG = [sb(s1, f"SG{i}", [128, TT], F32) for i in range(2)]
                    TAILb = [sb(s1, f"TAIL{i}", [128, 22, 2], F32) for i in range(2)]
                    tTAILb = [_Tile("TAIL0"), _Tile("TAIL1")]
                    tW = _Tile("W7")
                    tWs = [_Tile("W7a"), _Tile("W7b"), _Tile("W7c"), _Tile("W7d"), _Tile("W7e")]
                    tACG = [_Tile("ACG0"), _Tile("ACG1")]
                    tACV = [_Tile("ACV0"), _Tile("ACV1")]
                    tSG = [_Tile("SG0"), _Tile("SG1")]
                    d_x = [k.dsem(f"p7x0_{half}"), k.dsem(f"p7x1_{half}")]
                    d_h = [k.dsem(f"p7h{i}_{half}") for i in range(nhb)]
                    j0 = half * 11
                    k.dma(pool, WUG[:], wup[:, :, j0 * 128:(j0 + 11) * 128], [], [tWs[0]], d_w[4 + half])
                    k.dma(pool, WUV[:], wup[:, :, 2816 + j0 * 128:2816 + (j0 + 11) * 128], [], [tWs[1]], d_w[7 + half])
                    k.dma(pool, WD[:], wdn[:, j0:j0 + 11, :], [], [tWs[2]], d_w[9 + half])
                    MEMSET(dve, TAILb[0][:], 0.0, [tTAILb[0]])
                    if half == 1:
                        WG = sb(s1, "WG", [128, 8, 1024], BF16)
                        WP = sb(s1, "WP", [128, 2, 1024], BF16)
                        XN3 = sb(s1, "XN3", [128, 8, TT], BF16)
                        SQ = sb(s1, "SQ8", [128, 8, TT], BF16)
                        RS = sb(s1, "RS8", [128, TT], F32)
                        RSTD = sb(s1, "RSTD8", [128, TT], F32)
                        PTb = [sb(s1, f"PTp{i}", [128, 2, TT], BF16) for i in range(2)]
                        SGT = sb(s1, "SGT", [128, TT], F32)
                        TMP = sb(s1, "TMP", [128, TT], F32)
                        tXN3, tSQ, tRS, tRSTD, tSGT, tTMP = (_Tile(n) for n in ("XN3", "SQ8", "RS8", "RSTD8", "SGT", "TMP"))
                        tPTb = [_Tile("PTp0"), _Tile("PTp1")]
                        d_p = [k.dsem("p8p0"), k.dsem("p8p1")]
                        k.dma(pool, WG[:], w_pg[l].rearrange("(c p) n -> p c n", p=128), [], [tWs[3]], d_w[6])
                        k.dma(pool, WP[:], w_pp[l].rearrange("(c p) n -> p c n", p=128), [], [tWs[4]], d_w[11])
                        ptv = pT[l].rearrange("(c p) t -> p c t", p=128)

                    def ld7(t):
                        i = t % 2
                        k.dma(sp, XNb[i][:], xnv[:, :, cols(t)], [tXN[t]], [tXNb[i]], d_x[i])
                        ih = t % nhb
                        k.dma(sp, HTb[ih][:], hv_a[:, :, cols(t)], [tH[t]], [tHTb[ih]], d_h[ih])
                        if half == 1:
                            k.dma(pool, PTb[i][:], ptv[:, :, cols(t)], [], [tPTb[i]], d_p[i])
                    ld7(0)
                    cidx = 0
                    for t in range(NT):
                        i = t % 2
                        cs = cols(t)
                        if t + 1 < NT and nhb == 2:
                            ld7(t + 1)
                        XN, tXNs = XNb[i], tXNb[i]
                        HT, tHT = HTb[t % nhb], tHTb[t % nhb]
                        AT, tAT = ATb[i], tATb[i]
                        TAIL, tTAIL = TAILb[i], tTAILb[i]
                        TAILn, tTAILn = TAILb[1 - i], tTAILb[1 - i]
                        for jj in range(11):
                            j = j0 + jj
                            bi = cidx % 2
                            cidx += 1
                            for (which, Wt, pb, ACC, tACC, ch) in ((0, WUG, 2 * bi, ACG[bi], tACG[bi], j),
                                                                   (1, WUV, 2 * bi + 1, ACV[bi], tACV[bi], 22 + j)):
                                for c in range(8):
                                    MM(PSB[pb][:, :], Wt[:, c, jj * 128:(jj + 1) * 128], XN[:, c, :], c == 0, c == 7,
                                       [tWs[which], tXNs], [tPS[pb]])
                                w0 = VL[:, C_FCW + ch:C_FCW + ch + 1]
                                w1 = VL[:, C_FCW + 44 + ch:C_FCW + 44 + ch + 1]
                                w2 = VL[:, C_FCW + 88 + ch:C_FCW + 88 + ch + 1]
                                bb = VL[:, C_FCB + ch:C_FCB + ch + 1]
                                tl = jj * 2 + which
                                ACT(ACC[:, :], PSB[pb][:, :], AF.Identity, [tPS[pb], tVL], [tACC], bias=bb, scale=w2)
                                STT(dve, ACC[:, 1:TT], PSB[pb][:, 0:TT - 1], w1, ACC[:, 1:TT], ALU.mult, ALU.add,
                                    [tPS[pb], tACC, tVL], [tACC])
                                STT(dve, ACC[:, 2:TT], PSB[pb][:, 0:TT - 2], w0, ACC[:, 2:TT], ALU.mult, ALU.add,
                                    [tPS[pb], tACC, tVL], [tACC])
                                STT(dve, ACC[:, 0:2], TAIL[:, tl, 0:2], w0, ACC[:, 0:2], ALU.mult, ALU.add,
                                    [tTAIL, tACC, tVL], [tACC])
                                STT(dve, ACC[:, 0:1], TAIL[:, tl, 1:2], w1, ACC[:, 0:1], ALU.mult, ALU.add,
                                    [tTAIL, tACC, tVL], [tACC])
                                CP(act, TAILn[:, tl, 0:2], PSB[pb][:, TT - 2:TT], [tPS[pb]], [tTAILn])
                            ACT(SG[bi][:, :], ACG[bi][:, :], AF.Silu, [tACG[bi]], [tSG[bi]])
                            TTn(dve, AT[:, jj, :], SG[bi][:, :], ACV[bi][:, :], ALU.mult, [tSG[bi], tACV[bi]], [tAT])
                        for m in range(8):
                            pb = 4 + (m % 2)
                            for jj in range(11):
                                MM(PSB[pb][:, :], WD[:, jj, m * 128:(m + 1) * 128], AT[:, jj, :], jj == 0, jj == 10,
                                   [tWs[2], tAT], [tPS[pb]])
                            TTn(dve, HT[:, m, :], PSB[pb][:, :], HT[:, m, :], ALU.add, [tPS[pb], tHT], [tHT])
                        if half == 0:
                            k.dma(sp, hv_a[:, :, cs], HT[:], [tHT], [tH[t]], d_h[t % nhb])
                            continue
                        rmsnorm([HT[:, c, :] for c in range(8)], [128] * 8,
                                [VL[:, C_GPLE + c:C_GPLE + c + 1] for c in range(8)], 1024.0,
                                [XN3[:, c, :] for c in range(8)], SQ, tSQ, RS, tRS, RSTD, tRSTD, 7, [tHT], [tXN3])
                        for m in range(8):
                            pg, pp = (4, 5) if m % 2 == 0 else (6, 7)
                            for c in range(8):
                                MM(PSB[pg][:, :], WG[:, c, m * 128:(m + 1) * 128], XN3[:, c, :], c == 0, c == 7,
                                   [tWs[3], tXN3], [tPS[pg]])
                            for c in range(2):
                                MM(PSB[pp][:, :], WP[:, c, m * 128:(m + 1) * 128], PTb[i][:, c, :], c == 0, c == 1,
                                   [tWs[4], tPTb[i]], [tPS[pp]])
                            ACT(SGT[:, :], PSB[pg][:, :], AF.Sigmoid, [tPS[pg]], [tSGT])
                            TTn(dve, TMP[:, :], PSB[pp][:, :], SGT[:, :], ALU.mult, [tPS[pp], tSGT], [tTMP])
                            TTn(pool, HT[:, m, :], HT[:, m, :], TMP[:, :], ALU.add, [tHT, tTMP], [tHT])
                        if last:
                            rmsnorm([HT[:, c, :] for c in range(8)], [128] * 8,
                                    [VL[:, C_GFIN + c:C_GFIN + c + 1] for c in range(8)], 1024.0,
                                    [HT[:, c, :] for c in range(8)], SQ, tSQ, RS, tRS, RSTD, tRSTD, 7, [tHT], [tHT])
                            k.dma(sp, hv_o[:, :, cs], HT[:], [tHT], [tNONE], d_h[t % nhb])
                        else:
                            k.dma(sp, hv_a[:, :, cs], HT[:], [tHT], [tH[t]], d_h[t % nhb])
                    k.barrier()
                if stop_after == f"p{7 + half}_{l}":
                    return nc
        k.barrier()
    return nc


def _vec_table(inp, l):
    V = np.zeros((128, NV), np.float32)

    def chunks(v, n, c0, rows=128):
        v = np.asarray(v, np.float32).reshape(n, rows)
        V[:rows, c0:c0 + n] = v.T

    chunks(inp["g_mix"][l], 8, C_GMIX)
    chunks(inp["g_ffn"][l], 8, C_GFFN)
    chunks(inp["g_ple"][l], 8, C_GPLE)
    gq = np.asarray(inp["g_qc"][l], np.float32)
    V[:, C_GQC] = gq[0:128]
    V[0:64, C_GQC + 1] = gq[128:192]
    V[:, C_GKVC] = np.asarray(inp["g_kvc"][l], np.float32)
    chunks(inp["g_out"][l], 16, C_GOUT64, rows=64)
    chunks(np.asarray(inp["g_out"][l])[512:], 4, C_GOUTL)
    V[0:4, C_BF] = np.asarray(inp["b_f"][l], np.float32)
    lcw = np.asarray(inp["lru_conv_w"][l], np.float32)
    for kk in range(4):
        chunks(lcw[kk], 4, C_LCW + kk * 4)
    chunks(inp["lru_conv_b"][l], 4, C_LCB)
    chunks(inp["b_r"][l], 4, C_BR)
    chunks(inp["b_i"][l], 4, C_BI)
    chunks(inp["lru_lambda"][l], 4, C_LAM)
    fcw = np.asarray(inp["ffn_conv_w"][l], np.float32)
    for kk in range(3):
        chunks(fcw[kk], 44, C_FCW + kk * 44)
    chunks(inp["ffn_conv_b"][l], 44, C_FCB)
    chunks(inp["g_final"], 8, C_GFIN)
    return V


def _consts():
    ident = np.eye(128, dtype=np.float32)
    kk = np.arange(128)[:, None]
    qq = np.arange(128)[None, :]
    cmask = np.where(kk > qq, -30000.0, 0.0).astype(np.float32)
    sel = np.zeros((128, 8, 70), np.float32)
    for h in range(4):
        sel[96, h, 64:67] = 8.0
        sel[0 + h, h, 67] = 8.0
        sel[32 + h, h, 68] = 8.0
        sel[64 + h, h, 69] = 8.0
        sel[0 + h, 4 + h, 64] = -1.0
        sel[32 + h, 4 + h, 65] = -1.0
        sel[64 + h, 4 + h, 66] = -1.0
        sel[96, 4 + h, 67:70] = 1.0
    freq = np.zeros((128, 2), np.float32)
    half = 16
    f = (np.float32(10000.0) ** (-np.arange(half, dtype=np.float32) / np.float32(half))).astype(np.float32)
    freq[64:80, 0] = f
    freq[80:96, 0] = f
    freq[64:80, 1] = -2.0 * math.pi
    freq[80:96, 1] = 2.0 * math.pi
    return ident, cmask, sel, freq


_NC_CACHE = {}


def make_in_maps(inputs, cores):
    f32 = lambda a: np.ascontiguousarray(np.asarray(a), dtype=np.float32)
    ident, cmask, sel, freq = _consts()
    vec = np.stack([_vec_table(inputs, l) for l in range(DEPTH)])
    shared = {
        "w_in": f32(inputs["w_in"]), "w_uq": f32(inputs["w_uq"]), "w_ukv": f32(inputs["w_ukv"]),
        "w_r": f32(inputs["w_r"]), "w_i": f32(inputs["w_i"]), "w_o": f32(inputs["w_o"]),
        "w_up": f32(inputs["w_up"]), "w_down": f32(inputs["w_down"]),
        "w_ple_gate": f32(inputs["w_ple_gate"]), "w_ple_proj": f32(inputs["w_ple_proj"]),
        "vecs": vec, "cident": ident, "cmask": cmask, "csel": sel, "cfreq": freq,
    }
    x = np.asarray(inputs["x"])
    p = np.asarray(inputs["p"])
    pos = np.asarray(inputs["positions"])
    maps = []
    for b in cores:
        m = dict(shared)
        m["xT"] = np.ascontiguousarray(x[b].T, dtype=np.float32)
        m["pT"] = np.ascontiguousarray(np.transpose(p[:, b], (0, 2, 1)), dtype=np.float32)
        m["pos"] = np.ascontiguousarray(np.broadcast_to(pos[b].astype(np.int32)[None, :], (128, S)))
        maps.append(m)
    return maps


def kernel(**inputs):
    if "nc" not in _NC_CACHE:
        _NC_CACHE["nc"] = build_nc()
    nc = _NC_CACHE["nc"]
    B = np.asarray(inputs["x"]).shape[0]
    maps = make_in_maps(inputs, list(range(B)))
    res = run_bass_kernel_spmd(nc, maps, core_ids=list(range(B)))
    out = np.stack([np.ascontiguousarray(r["outT"].T) for r in res.results], axis=0)
    return out.astype(np.float32)
```

```python
import contextlib
import math
import numpy as np
import concourse.bass as bass
import concourse.mybir as mybir
from concourse.bass_utils import run_bass_kernel_spmd

F32 = mybir.dt.float32
BF16 = mybir.dt.bfloat16
I32 = mybir.dt.int32
AF = mybir.ActivationFunctionType
ALU = mybir.AluOpType

S = 4096
D = 1024
TT = 512
NT = S // TT
DEPTH = 2
EPS = 1e-6
NV = 272

C_GMIX, C_GFFN, C_GPLE = 0, 8, 16
C_GQC, C_GKVC = 24, 26
C_GOUT64 = 27
C_GOUTL = 43
C_BF = 47
C_LCW = 48
C_LCB, C_BR, C_BI, C_LAM = 64, 68, 72, 76
C_FCW = 80
C_FCB = 212
C_GFIN = 256


class _Tile:
    __slots__ = ("name", "w", "r")

    def __init__(self, name):
        self.name = name
        self.w = None
        self.r = {}


class _Eng:
    def __init__(self, name, h, k, is_pe=False):
        self.name = name
        self.h = h
        self.k = k
        self.is_pe = is_pe
        self.cnt = 0
        self.waited = {}


class _DSem:
    def __init__(self, k):
        self.k = k
        self.cnt = 0


class Ctx:
    def __init__(self, nc, stack):
        self.nc = nc
        self.stack = stack
        self.sems = []
        self.dsems = []
        self.pe = _Eng("pe", nc.tensor, self._sem("s_pe"), True)
        self.act = _Eng("act", nc.scalar, self._sem("s_act"))
        self.dve = _Eng("dve", nc.vector, self._sem("s_dve"))
        self.pool = _Eng("pool", nc.gpsimd, self._sem("s_pool"))
        self.sp = _Eng("sp", nc.sync, self._sem("s_sp"))
        self.engs = [self.pe, self.act, self.dve, self.pool, self.sp]

    def _sem(self, name):
        name = f"{name}_{len(self.sems)}"
        h = self.stack.enter_context(self.nc.semaphore(name))
        self.sems.append(h)
        return len(self.sems) - 1

    def dsem(self, name):
        d = _DSem(self._sem("d_" + name))
        self.dsems.append(d)
        return d

    def _add(self, need, eng, st, kind):
        if st is None:
            return
        k, val, src = st
        if src is eng:
            if eng.is_pe or kind != "raw":
                return
        if need.get(k, 0) < val:
            need[k] = val

    def _emit_waits(self, eng, reads, writes):
        need = {}
        for t in reads:
            self._add(need, eng, t.w, "raw")
        for t in writes:
            self._add(need, eng, t.w, "waw")
            for st in t.r.values():
                self._add(need, eng, st, "war")
        for k, v in need.items():
            if eng.waited.get(k, 0) < v:
                eng.h.wait_ge(self.sems[k], v)
                eng.waited[k] = v

    def _stamp(self, st, reads, writes):
        for t in reads:
            old = t.r.get(st[0])
            if old is None or old[1] < st[1]:
                t.r[st[0]] = st
        for t in writes:
            t.w = st
            t.r = {}

    def op(self, eng, fn, reads=(), writes=(), inc=True):
        self._emit_waits(eng, reads, writes)
        ins = fn(eng.h)
        if inc:
            eng.cnt += 1
            ins.then_inc(self.sems[eng.k], 1)
            st = (eng.k, eng.cnt, eng)
        else:
            st = (eng.k, eng.cnt + 1, eng)
        self._stamp(st, reads, writes)
        return ins

    def dma(self, q, out, in_, reads, writes, ds):
        need = {}
        for t in reads:
            if t.w is not None:
                k, val, _ = t.w
                need[k] = max(need.get(k, 0), val)
        for t in writes:
            if t.w is not None:
                k, val, _ = t.w
                need[k] = max(need.get(k, 0), val)
            for (k, val, _) in t.r.values():
                need[k] = max(need.get(k, 0), val)
        for k, v in need.items():
            if q.waited.get(k, 0) < v:
                q.h.wait_ge(self.sems[k], v)
                q.waited[k] = v
        ins = q.h.dma_start(out=out, in_=in_)
        ds.cnt += 16
        ins.then_inc(self.sems[ds.k], 16)
        self._stamp((ds.k, ds.cnt, None), reads, writes)
        return ins

    def barrier(self):
        for e in self.engs:
            for f in self.engs:
                if f is not e and f.cnt > 0 and e.waited.get(f.k, 0) < f.cnt:
                    e.h.wait_ge(self.sems[f.k], f.cnt)
                    e.waited[f.k] = f.cnt
            for d in self.dsems:
                if d.cnt > 0 and e.waited.get(d.k, 0) < d.cnt:
                    e.h.wait_ge(self.sems[d.k], d.cnt)
                    e.waited[d.k] = d.cnt


def build_nc(stop_after=None, debug=False):
    nc = bass.Bass("TRN2", target_bir_lowering=False)
    EI = dict(kind="ExternalInput")
    xT = nc.dram_tensor("xT", [D, S], F32, **EI).ap()
    pT = nc.dram_tensor("pT", [DEPTH, 256, S], F32, **EI).ap()
    posd = nc.dram_tensor("pos", [128, S], I32, **EI).ap()
    w_in = nc.dram_tensor("w_in", [DEPTH, D, 2148], F32, **EI).ap()
    w_uq = nc.dram_tensor("w_uq", [DEPTH, 192, 384], F32, **EI).ap()
    w_ukv = nc.dram_tensor("w_ukv", [DEPTH, 128, 512], F32, **EI).ap()
    w_r = nc.dram_tensor("w_r", [DEPTH, 8, 64, 64], F32, **EI).ap()
    w_i = nc.dram_tensor("w_i", [DEPTH, 8, 64, 64], F32, **EI).ap()
    w_o = nc.dram_tensor("w_o", [DEPTH, D, D], F32, **EI).ap()
    w_up = nc.dram_tensor("w_up", [DEPTH, D, 5632], F32, **EI).ap()
    w_down = nc.dram_tensor("w_down", [DEPTH, 2816, D], F32, **EI).ap()
    w_pg = nc.dram_tensor("w_ple_gate", [DEPTH, D, D], F32, **EI).ap()
    w_pp = nc.dram_tensor("w_ple_proj", [DEPTH, 256, D], F32, **EI).ap()
    vecs = nc.dram_tensor("vecs", [DEPTH, 128, NV], F32, **EI).ap()
    cident = nc.dram_tensor("cident", [128, 128], F32, **EI).ap()
    cmaskd = nc.dram_tensor("cmask", [128, 128], F32, **EI).ap()
    cseld = nc.dram_tensor("csel", [128, 8, 70], F32, **EI).ap()
    cfreq = nc.dram_tensor("cfreq", [128, 2], F32, **EI).ap()
    outT = nc.dram_tensor("outT", [D, S], F32, kind="ExternalOutput").ap()
    dk = dict(kind="ExternalOutput") if debug else {}
    hA = nc.dram_tensor("hA", [D, S], F32, **dk).ap()
    xnT = nc.dram_tensor("xnT", [D, S], BF16, **dk).ap()
    omixT = nc.dram_tensor("omixT", [D, S], BF16, **dk).ap()
    ropeT = nc.dram_tensor("ropeT", [2, 128, S], F32, **dk).ap()

    hv_x = xT.rearrange("(c p) t -> p c t", p=128)
    hv_a = hA.rearrange("(c p) t -> p c t", p=128)
    hv_o = outT.rearrange("(c p) t -> p c t", p=128)
    xnv = xnT.rearrange("(c p) t -> p c t", p=128)
    omv = omixT.rearrange("(c p) t -> p c t", p=128)

    with contextlib.ExitStack() as gstack:
        k = Ctx(nc, gstack)
        pe, act, dve, pool, sp = k.pe, k.act, k.dve, k.pool, k.sp

        uid = [0]

        def sb(stack, name, shape, dt):
            uid[0] += 1
            return stack.enter_context(nc.sbuf_tensor(f"{name}_{uid[0]}", shape, dt))

        def MM(out, lhsT, rhs, start, stop, reads, writes, inc=None):
            if inc is None:
                inc = stop
            k.op(pe, lambda e: e.matmul(out, lhsT=lhsT, rhs=rhs, start=start, stop=stop),
                 reads, writes, inc)

        def ACT(out, in_, func, reads, writes, bias=None, scale=None, eng=None):
            kw = {}
            if bias is not None:
                kw["bias"] = bias
            if scale is not None:
                kw["scale"] = scale
            k.op(act, lambda e: e.activation(out=out, in_=in_, func=func, **kw), reads, writes)

        def TS(eng, out, in0, s1, s2, op0, op1, reads, writes):
            if op1 is None:
                k.op(eng, lambda e: e.tensor_scalar(out=out, in0=in0, scalar1=s1, scalar2=None, op0=op0),
                     reads, writes)
            else:
                k.op(eng, lambda e: e.tensor_scalar(out=out, in0=in0, scalar1=s1, scalar2=s2, op0=op0, op1=op1),
                     reads, writes)

        def STT(eng, out, in0, scalar, in1, op0, op1, reads, writes):
            k.op(eng, lambda e: e.scalar_tensor_tensor(out=out, in0=in0, scalar=scalar, in1=in1, op0=op0, op1=op1),
                 reads, writes)

        def TTn(eng, out, in0, in1, op, reads, writes):
            k.op(eng, lambda e: e.tensor_tensor(out=out, in0=in0, in1=in1, op=op), reads, writes)

        def CP(eng, out, in_, reads, writes):
            if eng is act:
                k.op(act, lambda e: e.activation(out=out, in_=in_, func=AF.Copy), reads, writes)
            else:
                k.op(eng, lambda e: e.tensor_copy(out=out, in_=in_), reads, writes)

        def MEMSET(eng, ap, val, writes):
            k.op(eng, lambda e: e.memset(ap, val), (), writes)

        def RECIP(out, in_, reads, writes):
            k.op(dve, lambda e: e.reciprocal(out=out, in_=in_), reads, writes)

        ONES = sb(gstack, "ONES", [128, 128], BF16)
        IDENT = sb(gstack, "IDENT", [128, 128], BF16)
        CMASK = sb(gstack, "CMASK", [128, 128], BF16)
        CST = sb(gstack, "CST", [128, 4], F32)
        VL = sb(gstack, "VL", [128, NV], F32)
        tONES, tIDENT, tCMASK, tCST, tVL = (_Tile(n) for n in ("ONES", "IDENT", "CMASK", "CST", "VL"))
        PSALL = gstack.enter_context(nc.psum_tensor("psall", [128, 8, 512], F32))
        PSB = [PSALL[:, i, :] for i in range(8)]
        tPS = [_Tile(f"ps{i}") for i in range(8)]
        d_const = k.dsem("const")
        d_const2 = k.dsem("const2")
        d_vl = k.dsem("vl")
        d_w = [k.dsem(f"w{i}") for i in range(12)]

        MEMSET(dve, ONES[:], 1.0, [tONES])
        MEMSET(dve, CST[:, 0:1], EPS, [tCST])
        MEMSET(dve, CST[:, 1:2], 1.0, [tCST])
        MEMSET(dve, CST[:, 2:3], 0.0, [tCST])
        k.dma(pool, IDENT[:], cident[:, :], [], [tIDENT], d_const)
        k.dma(pool, CMASK[:], cmaskd[:, :], [], [tCMASK], d_const2)

        tH = [_Tile(f"h{t}") for t in range(NT)]
        tXN = [_Tile(f"xn{t}") for t in range(NT)]
        tOM = [[_Tile(f"om{g}_{t}") for t in range(NT)] for g in range(3)]
        tROPE = _Tile("rope")
        tNONE = _Tile("ext")

        def cols(t):
            return slice(t * TT, (t + 1) * TT)

        def rmsnorm(srcs, rows, gcols, nfeat, outs, SQ, tSQ, RS, tRS, RSTD, tRSTD, psn, src_tiles, out_tiles,
                    sq_eng=None):
            C = len(srcs)
            if sq_eng is not None:
                ACT(sq_eng[1], sq_eng[0], AF.Square, src_tiles, [tSQ])
            else:
                for c in range(C):
                    ACT(SQ[0:rows[c], c, :], srcs[c], AF.Square, src_tiles, [tSQ])
            for c in range(C):
                MM(PSB[psn][:, :], ONES[0:rows[c], :], SQ[0:rows[c], c, :], c == 0, c == C - 1,
                   [tSQ, tONES], [tPS[psn]])
            ACT(RS[:, :], PSB[psn][:, :], AF.Sqrt, [tPS[psn], tCST], [tRS], bias=CST[:, 0:1], scale=1.0 / nfeat)
            RECIP(RSTD[:, :], RS[:, :], [tRS], [tRSTD])
            for c in range(C):
                STT(dve, outs[c], srcs[c], gcols[c], RSTD[0:rows[c], :], ALU.mult, ALU.mult,
                    src_tiles + [tRSTD, tVL], out_tiles)

        with contextlib.ExitStack() as st:
            POSI = sb(st, "POSI", [128, S], I32)
            POSF = sb(st, "POSF", [128, S], F32)
            ANG = sb(st, "ANG", [128, S], F32)
            YT = sb(st, "YT", [128, S], F32)
            FRQ = sb(st, "FRQ", [128, 2], F32)
            CSB = sb(st, "CSB", [128, S], F32)
            CSB2 = sb(st, "CSB2", [128, S], F32)
            tCSB, tCSB2 = _Tile("CSB"), _Tile("CSB2")
            tPOSI, tPOSF, tANG, tYT, tFRQ = (_Tile(n) for n in ("POSI", "POSF", "ANG", "YT", "FRQ"))
            d_p = k.dsem("pro")
            R = slice(64, 96)
            k.dma(sp, POSI[R, :], posd[R, :], [], [tPOSI], d_p)
            d_p2 = k.dsem("pro2")
            k.dma(sp, FRQ[:], cfreq[:, :], [], [tFRQ], d_p2)
            CP(dve, POSF[R, :], POSI[R, :], [tPOSI], [tPOSF])
            TS(dve, ANG[R, :], POSF[R, :], FRQ[R, 0:1], None, ALU.mult, None, [tPOSF, tFRQ], [tANG])
            TS(dve, ANG[R, :], ANG[R, :], 1.0 / (2 * math.pi), None, ALU.mult, None, [tANG], [tANG])

            def reduce_turns(shift):
                TS(dve, YT[R, :], ANG[R, :], shift, None, ALU.add, None, [tANG], [tYT])
                CP(dve, POSI[R, :], YT[R, :], [tYT], [tPOSI])
                CP(dve, POSF[R, :], POSI[R, :], [tPOSI], [tPOSF])
                TTn(dve, YT[R, :], YT[R, :], POSF[R, :], ALU.subtract, [tYT, tPOSF], [tYT])
                TS(dve, POSF[R, :], YT[R, :], 0.5, None, ALU.is_gt, None, [tYT], [tPOSF])
                TTn(dve, YT[R, :], YT[R, :], POSF[R, :], ALU.subtract, [tYT, tPOSF], [tYT])
                TS(dve, POSF[R, :], YT[R, :], -0.5, None, ALU.is_lt, None, [tYT], [tPOSF])
                TTn(dve, YT[R, :], YT[R, :], POSF[R, :], ALU.add, [tYT, tPOSF], [tYT])

            reduce_turns(0.25)
            ACT(CSB[R, :], YT[R, :], AF.Sin, [tYT], [tCSB], scale=2 * math.pi)
            k.dma(sp, ropeT[0, R, :], CSB[R, :], [tCSB], [tROPE], d_p)
            reduce_turns(0.0)
            ACT(CSB2[R, :], YT[R, :], AF.Sin, [tYT, tFRQ], [tCSB2], scale=FRQ[R, 1:2])
            k.dma(sp, ropeT[1, R, :], CSB2[R, :], [tCSB2], [tROPE], d_p)
            k.barrier()
        if stop_after == "rope":
            k.barrier()
            return nc

        def attention(l, QT, tQT, KT, tKT, VA, tVA, kd, scale, grp, gcol0):
            with contextlib.ExitStack() as st:
                PT = [sb(st, f"PT{i}", [128, 512], BF16) for i in range(3)]
                tPT = [_Tile(f"PT{i}") for i in range(3)]
                RC = sb(st, "RC", [128, 512], F32)
                OHb = [sb(st, f"OH{i}", [128, 4, 512], F32) for i in range(2)]
                tOHb = [_Tile("OH0"), _Tile("OH1")]
                SQ = sb(st, "SQa", [128, 4, 512], BF16)
                RS = sb(st, "RSa", [128, 512], F32)
                RSTD = sb(st, "RSTDa", [128, 512], F32)
                OMX = [sb(st, f"OMXa{i}", [128, 4, 512], BF16) for i in range(2)]
                tRC, tSQ, tRS, tRSTD = (_Tile(n) for n in ("RC", "SQa", "RSa", "RSTDa"))
                tOMX = [_Tile("OMXa0"), _Tile("OMXa1")]
                d_st = [k.dsem("ast0"), k.dsem("ast1")]
                tiles = [(j, h, i) for j in range(NT) for h in range(4) for i in range(4 * j + 4)]
                LOOK = 2
                deferred = []

                def emit_score(idx):
                    j, h, i = tiles[idx]
                    r = i - 4 * j
                    c0 = 128 * r if r >= 0 else 0
                    pb = idx % 3
                    ksl = slice(i * 128, (i + 1) * 128)
                    q0 = j * TT
                    if r >= 0:
                        MM(PSB[pb][:, c0:c0 + 128], KT[0:kd, h, ksl], QT[0:kd, h, q0 + c0:q0 + c0 + 128],
                           True, False, [tKT, tQT], [tPS[pb]], inc=False)
                        MM(PSB[pb][:, c0:c0 + 128], IDENT[:, :], CMASK[:, :], False, True,
                           [tIDENT, tCMASK], [tPS[pb]], inc=(c0 + 128 == 512))
                        if c0 + 128 < 512:
                            MM(PSB[pb][:, c0 + 128:512], KT[0:kd, h, ksl], QT[0:kd, h, q0 + c0 + 128:q0 + 512],
                               True, True, [tKT, tQT], [tPS[pb]])
                    else:
                        MM(PSB[pb][:, :], KT[0:kd, h, ksl], QT[0:kd, h, q0:q0 + 512], True, True,
                           [tKT, tQT], [tPS[pb]])

                def emit_norm(j):
                    om = OMX[j % 2]
                    tom = tOMX[j % 2]
                    oh, toh = OHb[j % 2], tOHb[j % 2]
                    rmsnorm([oh[0:64, h, :] for h in range(4)], [64] * 4,
                            [VL[0:64, C_GOUT64 + gcol0 + h:C_GOUT64 + gcol0 + h + 1] for h in range(4)], 256.0,
                            [om[0:64, h, :] for h in range(4)], SQ, tSQ, RS, tRS, RSTD, tRSTD, 5, [toh], [tom])
                    r0 = 256 * grp
                    k.dma(sp, omixT[r0:r0 + 256, cols(j)].rearrange("(h p) t -> p h t", p=64), om[0:64, :, :],
                          [tom], [tOM[grp][j]], d_st[j % 2])

                def emit_exp_pv(idx, step):
                    j, h, i = tiles[idx]
                    nk = 4 * j + 4
                    r = i - 4 * j
                    c0 = 128 * r if r >= 0 else 0
                    pb = idx % 3
                    ob = 3 + (h % 2)
                    ACT(PT[pb][:, c0:512], PSB[pb][:, c0:512], AF.Exp, [tPS[pb]], [tPT[pb]], scale=scale)
                    MM(PSB[ob][:, c0:512], VA[:, i, h, :], PT[pb][:, c0:512], i == 0, i == nk - 1,
                       [tVA, tPT[pb]], [tPS[ob]])
                    if i == nk - 1:
                        oh, toh = OHb[j % 2], tOHb[j % 2]
                        RECIP(RC[64:128, :], PSB[ob][64:128, :], [tPS[ob]], [tRC])
                        TTn(dve, oh[0:64, h, :], PSB[ob][0:64, :], RC[64:128, :], ALU.mult, [tPS[ob], tRC], [toh])
                        if h == 3:
                            deferred.append((step + 6, lambda jj=j: emit_norm(jj)))

                nt_ = len(tiles)
                for step in range(nt_ + LOOK):
                    if step < nt_:
                        emit_score(step)
                    if step >= LOOK:
                        emit_exp_pv(step - LOOK, step)
                    while deferred and deferred[0][0] <= step:
                        deferred.pop(0)[1]()
                while deferred:
                    deferred.pop(0)[1]()
                k.barrier()

        for l in range(DEPTH):
            hsrc = hv_x if l == 0 else hv_a
            tHs = [tNONE] * NT if l == 0 else tH
            k.dma(sp, VL[:], vecs[l, :, :], [], [tVL], d_vl)
            win = w_in[l].rearrange("(c p) n -> p c n", p=128)

            with contextlib.ExitStack() as st:
                QM = sb(st, "QM", [128, 4, S], BF16)
                KM = sb(st, "KM", [128, 4, S], BF16)
                VA = sb(st, "VA", [128, 32, 4, 128], BF16)
                tQM, tKM, tVA = _Tile("QM"), _Tile("KM"), _Tile("VA")
                MEMSET(pool, VA[:, :, :, 64:128], 1.0, [tVA])
                with contextlib.ExitStack() as s1:
                    HTb1 = [sb(s1, f"HT1{i}", [128, 8, TT], F32) for i in range(2)]
                    tHTb1 = [_Tile("HT10"), _Tile("HT11")]
                    d_hb = [k.dsem("p1h0"), k.dsem("p1h1")]
                    XNb1 = [sb(s1, f"XN1{i}", [128, 8, TT], BF16) for i in range(2)]
                    tXNb1 = [_Tile("XN10"), _Tile("XN11")]
                    d_xsb = [k.dsem("p1xs0"), k.dsem("p1xs1")]
                    SQ = sb(s1, "SQ", [128, 8, TT], BF16)
                    SQB = sb(s1, "SQB", [128, 2, TT], BF16)
                    RSB = sb(s1, "RSB", [128, TT], F32)
                    RSTDB = sb(s1, "RSTDB", [128, TT], F32)
                    tSQB, tRSB, tRSTDB = _Tile("SQB"), _Tile("RSB"), _Tile("RSTDB")
                    RS = sb(s1, "RS", [128, TT], F32)
                    RSTD = sb(s1, "RSTD", [128, TT], F32)
                    QCN = sb(s1, "QCN", [128, 2, TT], BF16)
                    KVN = sb(s1, "KVN", [128, TT], BF16)
                    T1 = sb(s1, "T1", [128, TT], F32)
                    T2 = sb(s1, "T2", [128, TT], F32)
                    RC_ = sb(s1, "ROPEC", [128, TT], F32)
                    RS_ = sb(s1, "ROPES", [128, TT], F32)
                    WA = sb(s1, "WA", [128, 8, 320], BF16)
                    WKA = sb(s1, "WKA", [128, 8, 96], BF16)
                    WKB = sb(s1, "WKB", [128, 8, 96], BF16)
                    WUQ = sb(s1, "WUQ", [128, 2, 384], BF16)
                    WUQB = sb(s1, "WUQB", [128, 2, 4, 96], BF16)
                    WUKV = sb(s1, "WUKV", [128, 512], BF16)
                    WV = sb(s1, "WV", [128, 4, 64], BF16)
                    (tSQ, tRS, tRSTD, tQCN, tKVN, tT1, tT2, tRC_, tRS_, tW) = (
                        _Tile(n) for n in ("SQ", "RS", "RSTD", "QCN", "KVN", "T1", "T2", "RC_", "RS_", "W1"))
                    d_r, d_r2 = k.dsem("p1r"), k.dsem("p1r2")
                    MEMSET(pool, WKA[:], 0.0, [tW])
                    MEMSET(pool, WKB[:], 0.0, [tW])
                    MEMSET(pool, WUQB[:], 0.0, [tW])
                    MEMSET(pool, WUQ[:], 0.0, [tW])
                    k.dma(pool, WA[:], win[:, :, 0:320], [], [tW], d_w[0])
                    k.dma(pool, WKA[:, :, 64:96], win[:, :, 320:352], [], [tW], d_w[0])
                    k.dma(pool, WKB[:, :, 64:80], win[:, :, 336:352], [], [tW], d_w[0])
                    k.dma(pool, WKB[:, :, 80:96], win[:, :, 320:336], [], [tW], d_w[0])
                    k.dma(pool, WUQ[:, 0, :], w_uq[l, 0:128, :], [], [tW], d_w[0])
                    k.dma(pool, WUQ[0:64, 1, :], w_uq[l, 128:192, :], [], [tW], d_w[0])
                    uq4 = w_uq[l].rearrange("r (h d) -> r h d", h=4)
                    k.dma(pool, WUQB[:, 0, :, 64:80], uq4[0:128, :, 80:96], [], [tW], d_w[0])
                    k.dma(pool, WUQB[:, 0, :, 80:96], uq4[0:128, :, 64:80], [], [tW], d_w[0])
                    k.dma(pool, WUQB[0:64, 1, :, 64:80], uq4[128:192, :, 80:96], [], [tW], d_w[0])
                    k.dma(pool, WUQB[0:64, 1, :, 80:96], uq4[128:192, :, 64:80], [], [tW], d_w[0])
                    k.dma(pool, WUKV[:], w_ukv[l, :, :], [], [tW], d_w[0])
                    k.dma(pool, WV[:], w_ukv[l].rearrange("r (h d) -> r h d", h=4)[:, :, 64:128], [], [tW], d_w[0])
                    def ldh1(t):
                        k.dma(sp, HTb1[t % 2][:], hsrc[:, :, cols(t)], [tHs[t]], [tHTb1[t % 2]], d_hb[t % 2])

                    def normA(t):
                        HT, tHT = HTb1[t % 2], tHTb1[t % 2]
                        rmsnorm([HT[:, c, :] for c in range(8)], [128] * 8,
                                [VL[:, C_GMIX + c:C_GMIX + c + 1] for c in range(8)], 1024.0,
                                [XNb1[t % 2][:, c, :] for c in range(8)], SQ, tSQ, RS, tRS, RSTD, tRSTD, 7, [tHT],
                                [tXNb1[t % 2]], sq_eng=(HT[:, :, :], SQ[:, :, :]))
                        k.dma(sp, xnv[:, :, cols(t)], XNb1[t % 2][:], [tXNb1[t % 2]], [tXN[t]], d_xsb[t % 2])
                    ldh1(0)
                    ldh1(1)
                    normA(0)
                    for t in range(NT):
                        cs = cols(t)
                        XN, tXNs = XNb1[t % 2], tXNb1[t % 2]
                        if t >= 1 and t + 1 < NT:
                            ldh1(t + 1)
                        k.dma(sp, RC_[64:96, :], ropeT[0, 64:96, cs], [tROPE], [tRC_], d_r)
                        k.dma(sp, RS_[64:96, :], ropeT[1, 64:96, cs], [tROPE], [tRS_], d_r2)
                        specs = [(0, 128, WA, slice(0, 128)), (1, 64, WA, slice(128, 192)), (2, 128, WA, slice(192, 320)),
                                 (3, 96, WKA, slice(0, 96)), (4, 96, WKB, slice(0, 96))]
                        for (pb, m, Wt, csl) in specs:
                            for c in range(8):
                                MM(PSB[pb][0:m, :], Wt[:, c, csl], XN[:, c, :], c == 0, c == 7, [tW, tXNs], [tPS[pb]])
                        if t + 1 < NT:
                            normA(t + 1)
                        rmsnorm([PSB[0][:, :], PSB[1][0:64, :]], [128, 64],
                                [VL[:, C_GQC:C_GQC + 1], VL[0:64, C_GQC + 1:C_GQC + 2]], 192.0,
                                [QCN[:, 0, :], QCN[0:64, 1, :]], SQB, tSQB, RSB, tRSB, RSTDB, tRSTDB, 7,
                                [tPS[0], tPS[1]], [tQCN])
                        rmsnorm([PSB[2][:, :]], [128], [VL[:, C_GKVC:C_GKVC + 1]], 128.0, [KVN[:, :]],
                                SQB, tSQB, RSB, tRSB, RSTDB, tRSTDB, 7, [tPS[2]], [tKVN])
                        TTn(dve, T1[64:96, :], PSB[3][64:96, :], RC_[64:96, :], ALU.mult, [tPS[3], tRC_], [tT1])
                        TTn(dve, T2[64:96, :], PSB[4][64:96, :], RS_[64:96, :], ALU.mult, [tPS[4], tRS_], [tT2])
                        TTn(pool, T1[64:96, :], T1[64:96, :], T2[64:96, :], ALU.add, [tT1, tT2], [tT1])
                        for h in range(4):
                            CP(pool, KM[64:96, h, cs], T1[64:96, :], [tT1], [tKM])
                        for h in range(4):
                            pa, pb2 = (0, 1) if h % 2 == 0 else (5, 6)
                            hs = slice(h * 96, (h + 1) * 96)
                            MM(PSB[pa][0:96, :], WUQ[:, 0, hs], QCN[:, 0, :], True, False, [tW, tQCN], [tPS[pa]])
                            MM(PSB[pa][0:96, :], WUQ[0:64, 1, hs], QCN[0:64, 1, :], False, True, [tW, tQCN], [tPS[pa]])
                            MM(PSB[pb2][0:96, :], WUQB[:, 0, h, :], QCN[:, 0, :], True, False, [tW, tQCN], [tPS[pb2]])
                            MM(PSB[pb2][0:96, :], WUQB[0:64, 1, h, :], QCN[0:64, 1, :], False, True, [tW, tQCN], [tPS[pb2]])
                            CP(act, QM[0:64, h, cs], PSB[pa][0:64, :], [tPS[pa]], [tQM])
                            TTn(dve, T1[64:96, :], PSB[pa][64:96, :], RC_[64:96, :], ALU.mult, [tPS[pa], tRC_], [tT1])
                            TTn(dve, T2[64:96, :], PSB[pb2][64:96, :], RS_[64:96, :], ALU.mult, [tPS[pb2], tRS_], [tT2])
                            TTn(pool, QM[64:96, h, cs], T1[64:96, :], T2[64:96, :], ALU.add, [tT1, tT2], [tQM])
                        for h in range(4):
                            pb = 2 + (h % 2)
                            MM(PSB[pb][0:64, :], WUKV[:, h * 128:h * 128 + 64], KVN[:, :], True, True, [tW, tKVN], [tPS[pb]])
                            CP(act, KM[0:64, h, cs], PSB[pb][0:64, :], [tPS[pb]], [tKM])
                        for s_ in range(4):
                            pb = 5 + (s_ % 2)
                            MM(PSB[pb][:, 0:256], KVN[:, s_ * 128:(s_ + 1) * 128], WV[:, :, :].rearrange("p h d -> p (h d)"),
                               True, True, [tKVN, tW], [tPS[pb]])
                            CP(dve, VA[:, t * 4 + s_, :, 0:64], PSB[pb][:, 0:256].rearrange("p (h d) -> p h d", h=4),
                               [tPS[pb]], [tVA])
                    k.barrier()
                if stop_after == f"p1_{l}":
                    return nc
                attention(l, QM, tQM, KM, tKM, VA, tVA, 96, 96.0 ** -0.5, 0, 0)
            if stop_after == f"p2_{l}":
                return nc

            with contextlib.ExitStack() as st:
                FQ = sb(st, "FQ", [128, 4, S], BF16)
                FK = sb(st, "FK", [128, 4, S], BF16)
                FVA = sb(st, "FVA", [128, 32, 4, 128], BF16)
                tFQ, tFK, tFVA = _Tile("FQ"), _Tile("FK"), _Tile("FVA")
                MEMSET(pool, FVA[:, :, :, 64:128], 1.0, [tFVA])
                with contextlib.ExitStack() as s1:
                    XNb = [sb(s1, f"XNf{i}", [128, 8, TT], BF16) for i in range(2)]
                    tXNb = [_Tile("XNf0"), _Tile("XNf1")]
                    d_x = [k.dsem("p3x0"), k.dsem("p3x1")]
                    CC = sb(s1, "CC", [128, TT], BF16)
                    EX = sb(s1, "EX", [4, TT], F32)
                    NL = sb(s1, "NL", [4, TT], F32)
                    ONE4 = sb(s1, "ONE4", [4, TT], F32)
                    CF = sb(s1, "CF", [4, TT], F32)
                    R1 = sb(s1, "R1", [4, TT], F32)
                    CHI = sb(s1, "CHI", [4, TT], BF16)
                    CMID = sb(s1, "CMID", [4, TT], BF16)
                    CLO = sb(s1, "CLO", [4, TT], BF16)
                    CARRY = sb(s1, "CARRY", [4, 2], F32)
                    NEGB = sb(s1, "NEGB", [4, 2], F32)
                    WFQ = sb(s1, "WFQ", [128, 8, 4, 70], BF16)
                    WFK = sb(s1, "WFK", [128, 8, 4, 70], BF16)
                    WFV = sb(s1, "WFV", [128, 8, 256], BF16)
                    WF4 = sb(s1, "WF4", [128, 8, 4], BF16)
                    SEL = sb(s1, "SEL", [128, 8, 70], BF16)
                    (tCC, tEX, tNL, tONE4, tCF, tR1, tCHI, tCMID, tCLO, tCARRY, tNEGB, tW) = (
                        _Tile(n) for n in ("CC", "EX", "NL", "ONE4", "CF", "R1", "CHI", "CMID", "CLO", "CARRY", "NEGB", "W3"))
                    MEMSET(pool, WFQ[:], 0.0, [tW])
                    MEMSET(pool, WFK[:], 0.0, [tW])
                    MEMSET(dve, CC[:], 0.0, [tCC])
                    MEMSET(dve, CC[96:97, :], 1.0, [tCC])
                    MEMSET(dve, ONE4[:], 1.0, [tONE4])
                    MEMSET(dve, CARRY[:], 0.0, [tCARRY])
                    TS(dve, NEGB[0:4, 0:1], VL[0:4, C_BF:C_BF + 1], -1.0, None, ALU.mult, None, [tVL], [tNEGB])
                    for c in range(8):
                        k.dma(pool, WFQ[:, c, :, 0:64], win[:, c, 352:608].rearrange("p (h d) -> p h d", h=4), [], [tW], d_w[1])
                        k.dma(pool, WFK[:, c, :, 0:64], win[:, c, 608:864].rearrange("p (h d) -> p h d", h=4), [], [tW], d_w[1])
                    k.dma(pool, WFV[:], win[:, :, 864:1120], [], [tW], d_w[1])
                    k.dma(pool, WF4[:], win[:, :, 1120:1124], [], [tW], d_w[1])
                    k.dma(pool, SEL[:], cseld[:, :, :], [], [tW], d_w[1])
                    k.dma(sp, XNb[0][:], xnv[:, :, cols(0)], [tXN[0]], [tXNb[0]], d_x[0])
                    for t in range(NT):
                        cs = cols(t)
                        XN, tXNs = XNb[t % 2], tXNb[t % 2]
                        if t + 1 < NT:
                            k.dma(sp, XNb[(t + 1) % 2][:], xnv[:, :, cols(t + 1)], [tXN[t + 1]], [tXNb[(t + 1) % 2]],
                                  d_x[(t + 1) % 2])
                        for c in range(8):
                            MM(PSB[7][0:4, :], WF4[:, c, :], XN[:, c, :], c == 0, c == 7, [tW, tXNs], [tPS[7]])
                        ACT(EX[:, :], PSB[7][0:4, :], AF.Exp, [tPS[7], tNEGB], [tEX], bias=NEGB[0:4, 0:1], scale=-1.0)
                        ACT(NL[:, :], EX[:, :], AF.Ln, [tEX, tCST], [tNL], bias=CST[0:4, 1:2], scale=1.0)
                        TS(dve, NL[:, :], NL[:, :], -1.0, None, ALU.mult, None, [tNL], [tNL])
                        k.op(dve, lambda e: e.tensor_tensor_scan(out=CF[:, :], data0=ONE4[:, :], data1=NL[:, :],
                                                                 initial=CARRY[0:4, 0:1], op0=ALU.mult, op1=ALU.add),
                             [tONE4, tNL, tCARRY], [tCF])
                        CP(dve, CARRY[0:4, 0:1], CF[:, TT - 1:TT], [tCF], [tCARRY])
                        CP(dve, CHI[:, :], CF[:, :], [tCF], [tCHI])
                        TTn(dve, R1[:, :], CF[:, :], CHI[:, :], ALU.subtract, [tCF, tCHI], [tR1])
                        CP(dve, CMID[:, :], R1[:, :], [tR1], [tCMID])
                        TTn(dve, R1[:, :], R1[:, :], CMID[:, :], ALU.subtract, [tR1, tCMID], [tR1])
                        CP(dve, CLO[:, :], R1[:, :], [tR1], [tCLO])
                        CP(pool, CC[0:4, :], CHI[:, :], [tCHI], [tCC])
                        CP(pool, CC[32:36, :], CMID[:, :], [tCMID], [tCC])
                        CP(pool, CC[64:68, :], CLO[:, :], [tCLO], [tCC])
                        for h in range(4):
                            for (qi, Wt, DST, tD) in ((0, WFQ, FQ, tFQ), (1, WFK, FK, tFK)):
                                pb = (2 * h + qi) % 4
                                for c in range(8):
                                    MM(PSB[pb][0:70, :], Wt[:, c, h, :], XN[:, c, :], c == 0, False, [tW, tXNs], [tPS[pb]])
                                MM(PSB[pb][0:70, :], SEL[:, qi * 4 + h, :], CC[:, :], False, True, [tW, tCC], [tPS[pb]])
                                CP(act if qi == 0 else dve, DST[0:70, h, cs], PSB[pb][0:70, :], [tPS[pb]], [tD])
                        for s_ in range(4):
                            pb = 4 + (s_ % 2)
                            for c in range(8):
                                MM(PSB[pb][:, 0:256], XN[:, c, s_ * 128:(s_ + 1) * 128], WFV[:, c, :], c == 0, c == 7,
                                   [tXNs, tW], [tPS[pb]])
                            CP(act if s_ % 2 else dve, FVA[:, t * 4 + s_, :, 0:64],
                               PSB[pb][:, 0:256].rearrange("p (h d) -> p h d", h=4), [tPS[pb]], [tFVA])
                    k.barrier()
                if stop_after == f"p3_{l}":
                    return nc
                attention(l, FQ, tFQ, FK, tFK, FVA, tFVA, 70, 0.125, 1, 4)
            if stop_after == f"p4_{l}":
                return nc

            with contextlib.ExitStack() as s1:
                XNb = [sb(s1, f"XNl{i}", [128, 8, TT], BF16) for i in range(2)]
                tXNb = [_Tile("XNl0"), _Tile("XNl1")]
                d_x = [k.dsem("p5x0"), k.dsem("p5x1")]
                LXB = sb(s1, "LXB", [128, 4, TT + 3], F32)
                XC = sb(s1, "XC", [128, 4, TT], F32)
                XCB = sb(s1, "XCB", [128, 4, TT], BF16)
                RG = sb(s1, "RG", [128, 4, TT], F32)
                IG = sb(s1, "IG", [128, 4, TT], F32)
                AG = sb(s1, "AG", [128, 4, TT], F32)
                A2 = sb(s1, "A2", [128, 4, TT], F32)
                BX = sb(s1, "BX", [128, 4, TT], F32)
                GL = sb(s1, "GL", [128, 4, TT], F32)
                HS = sb(s1, "HS", [128, 4, TT], F32)
                OL = sb(s1, "OL", [128, 4, TT], F32)
                SQ = sb(s1, "SQl", [128, 4, TT], BF16)
                RS = sb(s1, "RSl", [128, TT], F32)
                RSTD = sb(s1, "RSTDl", [128, TT], F32)
                OMX = [sb(s1, f"OMXl{i}", [128, 4, TT], BF16) for i in range(2)]
                tOMX = [_Tile("OMXl0"), _Tile("OMXl1")]
                d_st = [k.dsem("lst0"), k.dsem("lst1")]
                CARH = sb(s1, "CARH", [128, 4], F32)
                COEF = sb(s1, "COEF", [128, 4], F32)
                WL = sb(s1, "WL", [128, 8, 1024], BF16)
                WRB = sb(s1, "WRB", [128, 4, 128], BF16)
                WIB = sb(s1, "WIB", [128, 4, 128], BF16)
                (tLXB, tXC, tXCB, tRG, tIG, tAG, tA2, tBX, tGL, tHS, tOL, tSQ, tRS, tRSTD, tCARH, tCOEF, tW) = (
                    _Tile(n) for n in ("LXB", "XC", "XCB", "RG", "IG", "AG", "A2", "BX", "GL", "HS", "OL", "SQl", "RSl",
                                       "RSTDl", "CARH", "COEF", "W5"))
                MEMSET(pool, WRB[:], 0.0, [tW])
                MEMSET(pool, WIB[:], 0.0, [tW])
                MEMSET(dve, LXB[:], 0.0, [tLXB])
                MEMSET(dve, CARH[:], 0.0, [tCARH])
                ACT(COEF[:, :], VL[:, C_LAM:C_LAM + 4], AF.Exp, [tVL], [tCOEF], scale=-1.0)
                ACT(COEF[:, :], COEF[:, :], AF.Ln, [tCOEF, tCST], [tCOEF], bias=CST[:, 1:2], scale=1.0)
                TS(dve, COEF[:, :], COEF[:, :], -8.0, None, ALU.mult, None, [tCOEF], [tCOEF])
                k.dma(pool, WL[:], win[:, :, 1124:2148], [], [tW], d_w[2])
                for c in range(4):
                    for hb in range(2):
                        ps_ = slice(hb * 64, hb * 64 + 64)
                        k.dma(pool, WRB[ps_, c, ps_], w_r[l, 2 * c + hb, :, :], [], [tW], d_w[2])
                        k.dma(pool, WIB[ps_, c, ps_], w_i[l, 2 * c + hb, :, :], [], [tW], d_w[2])
                k.dma(sp, XNb[0][:], xnv[:, :, cols(0)], [tXN[0]], [tXNb[0]], d_x[0])
                for t in range(NT):
                    cs = cols(t)
                    XN, tXNs = XNb[t % 2], tXNb[t % 2]
                    if t + 1 < NT:
                        k.dma(sp, XNb[(t + 1) % 2][:], xnv[:, :, cols(t + 1)], [tXN[t + 1]], [tXNb[(t + 1) % 2]],
                              d_x[(t + 1) % 2])
                    tP03 = [tPS[0], tPS[1], tPS[2], tPS[3]]
                    tP47 = [tPS[4], tPS[5], tPS[6], tPS[7]]
                    for c in range(4):
                        for kc in range(8):
                            MM(PSB[c][:, :], WL[:, kc, c * 128:(c + 1) * 128], XN[:, kc, :], kc == 0, kc == 7,
                               [tW, tXNs], [tPS[c]])
                    CP(pool, LXB[:, :, 0:3], LXB[:, :, TT:TT + 3], [tLXB], [tLXB])
                    CP(act, LXB[:, :, 3:TT + 3], PSALL[:, 0:4, :], tP03, [tLXB])
                    for c in range(4):
                        for kc in range(8):
                            MM(PSB[4 + c][:, :], WL[:, kc, 512 + c * 128:512 + (c + 1) * 128], XN[:, kc, :], kc == 0, kc == 7,
                               [tW, tXNs], [tPS[4 + c]])
                    for c in range(4):
                        wcol = lambda kk: VL[:, C_LCW + kk * 4 + c:C_LCW + kk * 4 + c + 1]
                        TS(pool, XC[:, c, :], LXB[:, c, 3:TT + 3], wcol(3), VL[:, C_LCB + c:C_LCB + c + 1], ALU.mult, ALU.add,
                           [tLXB, tVL], [tXC])
                        for kk in range(3):
                            STT(dve, XC[:, c, :], LXB[:, c, kk:kk + TT], wcol(kk), XC[:, c, :], ALU.mult, ALU.add,
                                [tLXB, tXC, tVL], [tXC])
                    CP(pool, XCB[:, :, :], XC[:, :, :], [tXC], [tXCB])
                    for c in range(4):
                        MM(PSB[c][:, :], WRB[:, c, :], XCB[:, c, :], True, True, [tW, tXCB], [tPS[c]])
                    for c in range(4):
                        ACT(RG[:, c, :], PSB[c][:, :], AF.Sigmoid, [tPS[c], tVL], [tRG], bias=VL[:, C_BR + c:C_BR + c + 1])
                    for c in range(4):
                        MM(PSB[c][:, :], WIB[:, c, :], XCB[:, c, :], True, True, [tW, tXCB], [tPS[c]])
                    for c in range(4):
                        ACT(IG[:, c, :], PSB[c][:, :], AF.Sigmoid, [tPS[c], tVL], [tIG], bias=VL[:, C_BI + c:C_BI + c + 1])
                    for c in range(4):
                        ACT(AG[:, c, :], RG[:, c, :], AF.Exp, [tRG, tCOEF], [tAG], scale=COEF[:, c:c + 1])
                    TTn(dve, A2[:, :, :], AG[:, :, :], AG[:, :, :], ALU.mult, [tAG], [tA2])
                    TS(dve, A2[:, :, :], A2[:, :, :], -1.0, 1.0, ALU.mult, ALU.add, [tA2], [tA2])
                    ACT(A2[:, :, :], A2[:, :, :], AF.Sqrt, [tA2], [tA2])
                    TTn(dve, BX[:, :, :], IG[:, :, :], XC[:, :, :], ALU.mult, [tIG, tXC], [tBX])
                    TTn(dve, BX[:, :, :], BX[:, :, :], A2[:, :, :], ALU.mult, [tBX, tA2], [tBX])
                    for c in range(4):
                        k.op(dve, lambda e: e.tensor_tensor_scan(out=HS[:, c, :], data0=AG[:, c, :], data1=BX[:, c, :],
                                                                 initial=CARH[:, c:c + 1], op0=ALU.mult, op1=ALU.add),
                             [tAG, tBX, tCARH], [tHS])
                        CP(dve, CARH[:, c:c + 1], HS[:, c, TT - 1:TT], [tHS], [tCARH])
                    ACT(GL[:, :, :], PSALL[:, 4:8, :], AF.Gelu_apprx_tanh, tP47, [tGL])
                    TTn(dve, OL[:, :, :], HS[:, :, :], GL[:, :, :], ALU.mult, [tHS, tGL], [tOL])
                    om, tom = OMX[t % 2], tOMX[t % 2]
                    rmsnorm([OL[:, c, :] for c in range(4)], [128] * 4,
                            [VL[:, C_GOUTL + c:C_GOUTL + c + 1] for c in range(4)], 512.0,
                            [om[:, c, :] for c in range(4)], SQ, tSQ, RS, tRS, RSTD, tRSTD, 0, [tOL], [tom],
                            sq_eng=(OL[:, :, :], SQ[:, :, :]))
                    k.dma(sp, omv[:, 4:8, cs], om[:, :, :], [tom], [tOM[2][t]], d_st[t % 2])
                k.barrier()
            if stop_after == f"p5_{l}":
                return nc

            with contextlib.ExitStack() as s1:
                WO = sb(s1, "WO", [128, 8, 1024], BF16)
                OMb = [sb(s1, f"OMt{i}", [128, 8, TT], BF16) for i in range(2)]
                HTb = [sb(s1, f"HT6{i}", [128, 8, TT], F32) for i in range(3)]
                XN2 = sb(s1, "XN2", [128, 8, TT], BF16)
                SQ = sb(s1, "SQ6", [128, 8, TT], BF16)
                RS = sb(s1, "RS6", [128, TT], F32)
                RSTD = sb(s1, "RSTD6", [128, TT], F32)
                tOMb = [_Tile("OMt0"), _Tile("OMt1")]
                tHTb = [_Tile("HT60"), _Tile("HT61"), _Tile("HT62")]
                tXN2, tSQ, tRS, tRSTD, tW = (_Tile(n) for n in ("XN2", "SQ6", "RS6", "RSTD6", "W6"))
                d_o = [k.dsem("p6o0"), k.dsem("p6o1")]
                d_h = [k.dsem("p6h0"), k.dsem("p6h1"), k.dsem("p6h2")]
                d_xs = k.dsem("p6xs")
                k.dma(pool, WO[:], w_o[l].rearrange("(c p) n -> p c n", p=128), [], [tW], d_w[3])

                def ld6(t):
                    i = t % 2
                    ih = t % 3
                    k.dma(sp, OMb[i][:], omv[:, :, cols(t)], [tOM[0][t], tOM[1][t], tOM[2][t]], [tOMb[i]], d_o[i])
                    k.dma(sp, HTb[ih][:], hsrc[:, :, cols(t)], [tHs[t]], [tHTb[ih]], d_h[ih])

                def norm6(t):
                    HT, tHT = HTb[t % 3], tHTb[t % 3]
                    rmsnorm([HT[:, c, :] for c in range(8)], [128] * 8,
                            [VL[:, C_GFFN + c:C_GFFN + c + 1] for c in range(8)], 1024.0,
                            [XN2[:, c, :] for c in range(8)], SQ, tSQ, RS, tRS, RSTD, tRSTD, 7, [tHT], [tXN2],
                            sq_eng=(HT[:, :, :], SQ[:, :, :]))
                    k.dma(sp, xnv[:, :, cols(t)], XN2[:], [tXN2], [tXN[t]], d_xs)
                ld6(0)
                for t in range(NT + 1):
                    if t < NT:
                        i = t % 2
                        cs = cols(t)
                        if t + 1 < NT:
                            ld6(t + 1)
                        HT, tHT = HTb[t % 3], tHTb[t % 3]
                        for m in range(8):
                            pb = m % 4
                            for c in range(8):
                                MM(PSB[pb][:, :], WO[:, c, m * 128:(m + 1) * 128], OMb[i][:, c, :], c == 0, c == 7,
                                   [tW, tOMb[i]], [tPS[pb]])
                            TTn(dve, HT[:, m, :], PSB[pb][:, :], HT[:, m, :], ALU.add, [tPS[pb], tHT], [tHT])
                        k.dma(sp, hv_a[:, :, cs], HT[:], [tHT], [tH[t]], d_h[t % 3])
                    if t >= 1:
                        norm6(t - 1)
                k.barrier()
            if stop_after == f"p6_{l}":
                return nc

            wup = w_up[l].rearrange("(c p) n -> p c n", p=128)
            wdn = w_down[l].rearrange("(j p) n -> p j n", p=128)
            last = (l == DEPTH - 1)
            for half in range(2):
                with contextlib.ExitStack() as s1:
                    WUG = sb(s1, "WUG", [128, 8, 11 * 128], BF16)
                    WUV = sb(s1, "WUV", [128, 8, 11 * 128], BF16)
                    WD = sb(s1, "WD", [128, 11, 1024], BF16)
                    XNb = [sb(s1, f"XN7{i}", [128, 8, TT], BF16) for i in range(2)]
                    tXNb = [_Tile("XN70"), _Tile("XN71")]
                    nhb = 2
                    HTb = [sb(s1, f"HT7{i}", [128, 8, TT], F32) for i in range(nhb)]
                    tHTb = [_Tile(f"HT7{i}") for i in range(nhb)]
                    ATb = [sb(s1, f"AT{i}", [128, 11, TT], BF16) for i in range(2)]
                    tATb = [_Tile("AT0"), _Tile("AT1")]
                    ACG = [sb(s1, f"ACG{i}", [128, TT], F32) for i in range(2)]
                    ACV = [sb(s1, f"ACV{i}", [128, TT], F32) for i in range(2)]
                    SG = [sb(s1, f"SG{i}", [128, TT], F32) for i in range(2)]
                    TAILb = [sb(s1, f"TAIL{i}", [128, 22, 2], F32) for i in range(2)]
                    tTAILb = [_Tile("TAIL0"), _Tile("TAIL1")]
                    tW = _Tile("W7")
                    tWs = [_Tile("W7a"), _Tile("W7b"), _Tile("W7c"), _Tile("W7d"), _Tile("W7e")]
                    tACG = [_Tile("ACG0"), _Tile("ACG1")]
                    tACV = [_Tile("ACV0"), _Tile("ACV1")]
                    tSG = [_Tile("SG0"), _Tile("SG1")]
                    d_x = [k.dsem(f"p7x0_{half}"), k.dsem(f"p7x1_{half}")]
                    d_h = [k.dsem(f"p7h{i}_{half}") for i in range(nhb)]
                    j0 = half * 11
                    k.dma(pool, WUG[:], wup[:, :, j0 * 128:(j0 + 11) * 128], [], [tWs[0]], d_w[4 + half])
                    k.dma(pool, WUV[:], wup[:, :, 2816 + j0 * 128:2816 + (j0 + 11) * 128], [], [tWs[1]], d_w[7 + half])
                    k.dma(pool, WD[:], wdn[:, j0:j0 + 11, :], [], [tWs[2]], d_w[9 + half])
                    MEMSET(dve, TAILb[0][:], 0.0, [tTAILb[0]])
                    if half == 1:
                        WG = sb(s1, "WG", [128, 8, 1024], BF16)
                        WP = sb(s1, "WP", [128, 2, 1024], BF16)
                        XN3 = sb(s1, "XN3", [128, 8, TT], BF16)
                        SQ = sb(s1, "SQ8", [128, 8, TT], BF16)
                        RS = sb(s1, "RS8", [128, TT], F32)
                        RSTD = sb(s1, "RSTD8", [128, TT], F32)
                        PTb = [sb(s1, f"PTp{i}", [128, 2, TT], BF16) for i in range(2)]
                        SGT = sb(s1, "SGT", [128, TT], F32)
                        TMP = sb(s1, "TMP", [128, TT], F32)
                        tXN3, tSQ, tRS, tRSTD, tSGT, tTMP = (_Tile(n) for n in ("XN3", "SQ8", "RS8", "RSTD8", "SGT", "TMP"))
                        tPTb = [_Tile("PTp0"), _Tile("PTp1")]
                        d_p = [k.dsem("p8p0"), k.dsem("p8p1")]
                        k.dma(pool, WG[:], w_pg[l].rearrange("(c p) n -> p c n", p=128), [], [tWs[3]], d_w[6])
                        k.dma(pool, WP[:], w_pp[l].rearrange("(c p) n -> p c n", p=128), [], [tWs[4]], d_w[11])
                        ptv = pT[l].rearrange("(c p) t -> p c t", p=128)

                    def ld7(t):
                        i = t % 2
                        k.dma(sp, XNb[i][:], xnv[:, :, cols(t)], [tXN[t]], [tXNb[i]], d_x[i])
                        ih = t % nhb
                        k.dma(sp, HTb[ih][:], hv_a[:, :, cols(t)], [tH[t]], [tHTb[ih]], d_h[ih])
                        if half == 1:
                            k.dma(pool, PTb[i][:], ptv[:, :, cols(t)], [], [tPTb[i]], d_p[i])
                    ld7(0)
                    cidx = 0
                    for t in range(NT):
                        i = t % 2
                        cs = cols(t)
                        if t + 1 < NT and nhb == 2:
                            ld7(t + 1)
                        XN, tXNs = XNb[i], tXNb[i]
                        HT, tHT = HTb[t % nhb], tHTb[t % nhb]
                        AT, tAT = ATb[i], tATb[i]
                        TAIL, tTAIL = TAILb[i], tTAILb[i]
                        TAILn, tTAILn = TAILb[1 - i], tTAILb[1 - i]
                        for jj in range(11):
                            j = j0 + jj
                            bi = cidx % 2
                            cidx += 1
                            for (which, Wt, pb, ACC, tACC, ch) in ((0, WUG, 2 * bi, ACG[bi], tACG[bi], j),
                                                                   (1, WUV, 2 * bi + 1, ACV[bi], tACV[bi], 22 + j)):
                                for c in range(8):
                                    MM(PSB[pb][:, :], Wt[:, c, jj * 128:(jj + 1) * 128], XN[:, c, :], c == 0, c == 7,
                                       [tWs[which], tXNs], [tPS[pb]])
                                w0 = VL[:, C_FCW + ch:C_FCW + ch + 1]
                                w1 = VL[:, C_FCW + 44 + ch:C_FCW + 44 + ch + 1]
                                w2 = VL[:, C_FCW + 88 + ch:C_FCW + 88 + ch + 1]
                                bb = VL[:, C_FCB + ch:C_FCB + ch + 1]
                                tl = jj * 2 + which
                                ACT(ACC[:, :], PSB[pb][:, :], AF.Identity, [tPS[pb], tVL], [tACC], bias=bb, scale=w2)
                                STT(dve, ACC[:, 1:TT], PSB[pb][:, 0:TT - 1], w1, ACC[:, 1:TT], ALU.mult, ALU.add,
                                    [tPS[pb], tACC, tVL], [tACC])
                                STT(dve, ACC[:, 2:TT], PSB[pb][:, 0:TT - 2], w0, ACC[:, 2:TT], ALU.mult, ALU.add,
                                    [tPS[pb], tACC, tVL], [tACC])
                                STT(dve, ACC[:, 0:2], TAIL[:, tl, 0:2], w0, ACC[:, 0:2], ALU.mult, ALU.add,
                                    [tTAIL, tACC, tVL], [tACC])
                                STT(dve, ACC[:, 0:1], TAIL[:, tl, 1:2], w1, ACC[:, 0:1], ALU.mult, ALU.add,
                                    [tTAIL, tACC, tVL], [tACC])
                                CP(act, TAILn[:, tl, 0:2], PSB[pb][:, TT - 2:TT], [tPS[pb]], [tTAILn])
                            ACT(SG[bi][:, :], ACG[bi][:, :], AF.Silu, [tACG[bi]], [tSG[bi]])
                            TTn(dve, AT[:, jj, :], SG[bi][:, :], ACV[bi][:, :], ALU.mult, [tSG[bi], tACV[bi]], [tAT])
                        for m in range(8):
                            pb = 4 + (m % 2)
                            for jj in range(11):
                                MM(PSB[pb][:, :], WD[:, jj, m * 128:(m + 1) * 128], AT[:, jj, :], jj == 0, jj == 10,
                                   [tWs[2], tAT], [tPS[pb]])
                            TTn(dve, HT[:, m, :], PSB[pb][:, :], HT[:, m, :], ALU.add, [tPS[pb], tHT], [tHT])
                        if half == 0:
                            k.dma(sp, hv_a[:, :, cs], HT[:], [tHT], [tH[t]], d_h[t % nhb])
                            continue
                        rmsnorm([HT[:, c, :] for c in range(8)], [128] * 8,
                                [VL[:, C_GPLE + c:C_GPLE + c + 1] for c in range(8)], 1024.0,
                                [XN3[:, c, :] for c in range(8)], SQ, tSQ, RS, tRS, RSTD, tRSTD, 7, [tHT], [tXN3])
                        for m in range(8):
                            pg, pp = (4, 5) if m % 2 == 0 else (6, 7)
                            for c in range(8):
                                MM(PSB[pg][:, :], WG[:, c, m * 128:(m + 1) * 128], XN3[:, c, :], c == 0, c == 7,
                                   [tWs[3], tXN3], [tPS[pg]])
                            for c in range(2):
                                MM(PSB[pp][:, :], WP[:, c, m * 128:(m + 1) * 128], PTb[i][:, c, :], c == 0, c == 1,
                                   [tWs[4], tPTb[i]], [tPS[pp]])
                            ACT(SGT[:, :], PSB[pg][:, :], AF.Sigmoid, [tPS[pg]], [tSGT])
                            TTn(dve, TMP[:, :], PSB[pp][:, :], SGT[:, :], ALU.mult, [tPS[pp], tSGT], [tTMP])
                            TTn(pool, HT[:, m, :], HT[:, m, :], TMP[:, :], ALU.add, [tHT, tTMP], [tHT])
                        if last:
                            rmsnorm([HT[:, c, :] for c in range(8)], [128] * 8,
                                    [VL[:, C_GFIN + c:C_GFIN + c + 1] for c in range(8)], 1024.0,
                                    [HT[:, c, :] for c in range(8)], SQ, tSQ, RS, tRS, RSTD, tRSTD, 7, [tHT], [tHT])
                            k.dma(sp, hv_o[:, :, cs], HT[:], [tHT], [tNONE], d_h[t % nhb])
                        else:
                            k.dma(sp, hv_a[:, :, cs], HT[:], [tHT], [tH[t]], d_h[t % nhb])
                    k.barrier()
                if stop_after == f"p{7 + half}_{l}":
                    return nc
        k.barrier()
    return nc


def _vec_table(inp, l):
    V = np.zeros((128, NV), np.float32)

    def chunks(v, n, c0, rows=128):
        v = np.asarray(v, np.float32).reshape(n, rows)
        V[:rows, c0:c0 + n] = v.T

    chunks(inp["g_mix"][l], 8, C_GMIX)
    chunks(inp["g_ffn"][l], 8, C_GFFN)
    chunks(inp["g_ple"][l], 8, C_GPLE)
    gq = np.asarray(inp["g_qc"][l], np.float32)
    V[:, C_GQC] = gq[0:128]
    V[0:64, C_GQC + 1] = gq[128:192]
    V[:, C_GKVC] = np.asarray(inp["g_kvc"][l], np.float32)
    chunks(inp["g_out"][l], 16, C_GOUT64, rows=64)
    chunks(np.asarray(inp["g_out"][l])[512:], 4, C_GOUTL)
    V[0:4, C_BF] = np.asarray(inp["b_f"][l], np.float32)
    lcw = np.asarray(inp["lru_conv_w"][l], np.float32)
    for kk in range(4):
        chunks(lcw[kk], 4, C_LCW + kk * 4)
    chunks(inp["lru_conv_b"][l], 4, C_LCB)
    chunks(inp["b_r"][l], 4, C_BR)
    chunks(inp["b_i"][l], 4, C_BI)
    chunks(inp["lru_lambda"][l], 4, C_LAM)
    fcw = np.asarray(inp["ffn_conv_w"][l], np.float32)
    for kk in range(3):
        chunks(fcw[kk], 44, C_FCW + kk * 44)
    chunks(inp["ffn_conv_b"][l], 44, C_FCB)
    chunks(inp["g_final"], 8, C_GFIN)
    return V


def _consts():
    ident = np.eye(128, dtype=np.float32)
    kk = np.arange(128)[:, None]
    qq = np.arange(128)[None, :]
    cmask = np.where(kk > qq, -30000.0, 0.0).astype(np.float32)
    sel = np.zeros((128, 8, 70), np.float32)
    for h in range(4):
        sel[96, h, 64:67] = 8.0
        sel[0 + h, h, 67] = 8.0
        sel[32 + h, h, 68] = 8.0
        sel[64 + h, h, 69] = 8.0
        sel[0 + h, 4 + h, 64] = -1.0
        sel[32 + h, 4 + h, 65] = -1.0
        sel[64 + h, 4 + h, 66] = -1.0
        sel[96, 4 + h, 67:70] = 1.0
    freq = np.zeros((128, 2), np.float32)
    half = 16
    f = (np.float32(10000.0) ** (-np.arange(half, dtype=np.float32) / np.float32(half))).astype(np.float32)
    freq[64:80, 0] = f
    freq[80:96, 0] = f
    freq[64:80, 1] = -2.0 * math.pi
    freq[80:96, 1] = 2.0 * math.pi
    return ident, cmask, sel, freq


_NC_CACHE = {}


def make_in_maps(inputs, cores):
    f32 = lambda a: np.ascontiguousarray(np.asarray(a), dtype=np.float32)
    ident, cmask, sel, freq = _consts()
    vec = np.stack([_vec_table(inputs, l) for l in range(DEPTH)])
    shared = {
        "w_in": f32(inputs["w_in"]), "w_uq": f32(inputs["w_uq"]), "w_ukv": f32(inputs["w_ukv"]),
        "w_r": f32(inputs["w_r"]), "w_i": f32(inputs["w_i"]), "w_o": f32(inputs["w_o"]),
        "w_up": f32(inputs["w_up"]), "w_down": f32(inputs["w_down"]),
        "w_ple_gate": f32(inputs["w_ple_gate"]), "w_ple_proj": f32(inputs["w_ple_proj"]),
        "vecs": vec, "cident": ident, "cmask": cmask, "csel": sel, "cfreq": freq,
    }
    x = np.asarray(inputs["x"])
    p = np.asarray(inputs["p"])
    pos = np.asarray(inputs["positions"])
    maps = []
    for b in cores:
        m = dict(shared)
        m["xT"] = np.ascontiguousarray(x[b].T, dtype=np.float32)
        m["pT"] = np.ascontiguousarray(np.transpose(p[:, b], (0, 2, 1)), dtype=np.float32)
        m["pos"] = np.ascontiguousarray(np.broadcast_to(pos[b].astype(np.int32)[None, :], (128, S)))
        maps.append(m)
    return maps


def kernel(**inputs):
    if "nc" not in _NC_CACHE:
        _NC_CACHE["nc"] = build_nc()
    nc = _NC_CACHE["nc"]
    B = np.asarray(inputs["x"]).shape[0]
    maps = make_in_maps(inputs, list(range(B)))
    res = run_bass_kernel_spmd(nc, maps, core_ids=list(range(B)))
    out = np.stack([np.ascontiguousarray(r["outT"].T) for r in res.results], axis=0)
    return out.astype(np.float32)
```

```python
import contextlib
import math
import numpy as np
import concourse.bass as bass
import concourse.mybir as mybir
from concourse.bass_utils import run_bass_kernel_spmd

F32 = mybir.dt.float32
BF16 = mybir.dt.bfloat16
I32 = mybir.dt.int32
AF = mybir.ActivationFunctionType
ALU = mybir.AluOpType

S = 4096
D = 1024
TT = 512
NT = S // TT
DEPTH = 2
EPS = 1e-6
NV = 272

C_GMIX, C_GFFN, C_GPLE = 0, 8, 16
C_GQC, C_GKVC = 24, 26
C_GOUT64 = 27
C_GOUTL = 43
C_BF = 47
C_LCW = 48
C_LCB, C_BR, C_BI, C_LAM = 64, 68, 72, 76
C_FCW = 80
C_FCB = 212
C_GFIN = 256


class _Tile:
    __slots__ = ("name", "w", "r")

    def __init__(self, name):
        self.name = name
        self.w = None
        self.r = {}


class _Eng:
    def __init__(self, name, h, k, is_pe=False):
        self.name = name
        self.h = h
        self.k = k
        self.is_pe = is_pe
        self.cnt = 0
        self.waited = {}


class _DSem:
    def __init__(self, k):
        self.k = k
        self.cnt = 0


class Ctx:
    def __init__(self, nc, stack):
        self.nc = nc
        self.stack = stack
        self.sems = []
        self.dsems = []
        self.free_dsems = []
        self.live_dsems = []
        self.pe = _Eng("pe", nc.tensor, self._sem("s_pe"), True)
        self.act = _Eng("act", nc.scalar, self._sem("s_act"))
        self.dve = _Eng("dve", nc.vector, self._sem("s_dve"))
        self.pool = _Eng("pool", nc.gpsimd, self._sem("s_pool"))
        self.sp = _Eng("sp", nc.sync, self._sem("s_sp"))
        self.engs = [self.pe, self.act, self.dve, self.pool, self.sp]

    def _sem(self, name):
        name = f"{name}_{len(self.sems)}"
        h = self.stack.enter_context(self.nc.semaphore(name))
        self.sems.append(h)
        return len(self.sems) - 1

    def dsem(self, name):
        if self.free_dsems:
            d = self.free_dsems.pop()
        else:
            d = _DSem(self._sem("d_" + name))
            self.dsems.append(d)
        self.live_dsems.append(d)
        return d

    def release_dsems(self, keep=()):
        keepk = {d.k for d in keep}
        for d in self.live_dsems:
            if d.k not in keepk:
                self.free_dsems.append(d)
        self.live_dsems = [d for d in self.live_dsems if d.k in keepk]

    def _add(self, need, eng, st, kind):
        if st is None:
            return
        k, val, src = st
        if src is eng:
            if eng.is_pe or kind != "raw":
                return
        if need.get(k, 0) < val:
            need[k] = val

    def _emit_waits(self, eng, reads, writes):
        need = {}
        for t in reads:
            self._add(need, eng, t.w, "raw")
        for t in writes:
            self._add(need, eng, t.w, "waw")
            for st in t.r.values():
                self._add(need, eng, st, "war")
        for k, v in need.items():
            if eng.waited.get(k, 0) < v:
                eng.h.wait_ge(self.sems[k], v)
                eng.waited[k] = v

    def _stamp(self, st, reads, writes):
        for t in reads:
            old = t.r.get(st[0])
            if old is None or old[1] < st[1]:
                t.r[st[0]] = st
        for t in writes:
            t.w = st
            t.r = {}

    def op(self, eng, fn, reads=(), writes=(), inc=True):
        self._emit_waits(eng, reads, writes)
        ins = fn(eng.h)
        if inc:
            eng.cnt += 1
            ins.then_inc(self.sems[eng.k], 1)
            st = (eng.k, eng.cnt, eng)
        else:
            st = (eng.k, eng.cnt + 1, eng)
        self._stamp(st, reads, writes)
        return ins

    def dma(self, q, out, in_, reads, writes, ds):
        need = {}
        for t in reads:
            if t.w is not None:
                k, val, _ = t.w
                need[k] = max(need.get(k, 0), val)
        for t in writes:
            if t.w is not None:
                k, val, _ = t.w
                need[k] = max(need.get(k, 0), val)
            for (k, val, _) in t.r.values():
                need[k] = max(need.get(k, 0), val)
        for k, v in need.items():
            if q.waited.get(k, 0) < v:
                q.h.wait_ge(self.sems[k], v)
                q.waited[k] = v
        ins = q.h.dma_start(out=out, in_=in_)
        ds.cnt += 16
        ins.then_inc(self.sems[ds.k], 16)
        self._stamp((ds.k, ds.cnt, None), reads, writes)
        return ins

    def barrier(self):
        for e in self.engs:
            for f in self.engs:
                if f is not e and f.cnt > 0 and e.waited.get(f.k, 0) < f.cnt:
                    e.h.wait_ge(self.sems[f.k], f.cnt)
                    e.waited[f.k] = f.cnt
            for d in self.dsems:
                if d.cnt > 0 and e.waited.get(d.k, 0) < d.cnt:
                    e.h.wait_ge(self.sems[d.k], d.cnt)
                    e.waited[d.k] = d.cnt


def build_nc(stop_after=None, debug=False):
    nc = bass.Bass("TRN2", target_bir_lowering=False)
    EI = dict(kind="ExternalInput")
    xT = nc.dram_tensor("xT", [D, S], F32, **EI).ap()
    pT = nc.dram_tensor("pT", [DEPTH, 256, S], F32, **EI).ap()
    posd = nc.dram_tensor("pos", [128, S], I32, **EI).ap()
    w_in = nc.dram_tensor("w_in", [DEPTH, D, 2148], F32, **EI).ap()
    w_uq = nc.dram_tensor("w_uq", [DEPTH, 192, 384], F32, **EI).ap()
    w_ukv = nc.dram_tensor("w_ukv", [DEPTH, 128, 512], F32, **EI).ap()
    w_r = nc.dram_tensor("w_r", [DEPTH, 8, 64, 64], F32, **EI).ap()
    w_i = nc.dram_tensor("w_i", [DEPTH, 8, 64, 64], F32, **EI).ap()
    w_o = nc.dram_tensor("w_o", [DEPTH, D, D], F32, **EI).ap()
    w_up = nc.dram_tensor("w_up", [DEPTH, D, 5632], F32, **EI).ap()
    w_down = nc.dram_tensor("w_down", [DEPTH, 2816, D], F32, **EI).ap()
    w_pg = nc.dram_tensor("w_ple_gate", [DEPTH, D, D], F32, **EI).ap()
    w_pp = nc.dram_tensor("w_ple_proj", [DEPTH, 256, D], F32, **EI).ap()
    vecs = nc.dram_tensor("vecs", [DEPTH, 128, NV], F32, **EI).ap()
    cident = nc.dram_tensor("cident", [128, 128], F32, **EI).ap()
    cmaskd = nc.dram_tensor("cmask", [128, 128], F32, **EI).ap()
    cseld = nc.dram_tensor("csel", [128, 8, 70], F32, **EI).ap()
    cfreq = nc.dram_tensor("cfreq", [128, 2], F32, **EI).ap()
    outT = nc.dram_tensor("outT", [D, S], F32, kind="ExternalOutput").ap()
    dk = dict(kind="ExternalOutput") if debug else {}
    hA = nc.dram_tensor("hA", [D, S], F32, **dk).ap()
    xnT = nc.dram_tensor("xnT", [D, S], BF16, **dk).ap()
    omixT = nc.dram_tensor("omixT", [D, S], BF16, **dk).ap()
    ropeT = nc.dram_tensor("ropeT", [2, 128, S], F32, **dk).ap()

    hv_x = xT.rearrange("(c p) t -> p c t", p=128)
    hv_a = hA.rearrange("(c p) t -> p c t", p=128)
    hv_o = outT.rearrange("(c p) t -> p c t", p=128)
    xnv = xnT.rearrange("(c p) t -> p c t", p=128)
    omv = omixT.rearrange("(c p) t -> p c t", p=128)

    with contextlib.ExitStack() as gstack:
        k = Ctx(nc, gstack)
        pe, act, dve, pool, sp = k.pe, k.act, k.dve, k.pool, k.sp

        uid = [0]

        def sb(stack, name, shape, dt):
            uid[0] += 1
            return stack.enter_context(nc.sbuf_tensor(f"{name}_{uid[0]}", shape, dt))

        def MM(out, lhsT, rhs, start, stop, reads, writes, inc=None):
            if inc is None:
                inc = stop
            k.op(pe, lambda e: e.matmul(out, lhsT=lhsT, rhs=rhs, start=start, stop=stop),
                 reads, writes, inc)

        def ACT(out, in_, func, reads, writes, bias=None, scale=None, eng=None):
            kw = {}
            if bias is not None:
                kw["bias"] = bias
            if scale is not None:
                kw["scale"] = scale
            k.op(act, lambda e: e.activation(out=out, in_=in_, func=func, **kw), reads, writes)

        def TS(eng, out, in0, s1, s2, op0, op1, reads, writes):
            if op1 is None:
                k.op(eng, lambda e: e.tensor_scalar(out=out, in0=in0, scalar1=s1, scalar2=None, op0=op0),
                     reads, writes)
            else:
                k.op(eng, lambda e: e.tensor_scalar(out=out, in0=in0, scalar1=s1, scalar2=s2, op0=op0, op1=op1),
                     reads, writes)

        def STT(eng, out, in0, scalar, in1, op0, op1, reads, writes):
            k.op(eng, lambda e: e.scalar_tensor_tensor(out=out, in0=in0, scalar=scalar, in1=in1, op0=op0, op1=op1),
                 reads, writes)

        def TTn(eng, out, in0, in1, op, reads, writes):
            k.op(eng, lambda e: e.tensor_tensor(out=out, in0=in0, in1=in1, op=op), reads, writes)

        def CP(eng, out, in_, reads, writes):
            if eng is act:
                k.op(act, lambda e: e.activation(out=out, in_=in_, func=AF.Copy), reads, writes)
            else:
                k.op(eng, lambda e: e.tensor_copy(out=out, in_=in_), reads, writes)

        def MEMSET(eng, ap, val, writes):
            k.op(eng, lambda e: e.memset(ap, val), (), writes)

        def RECIP(out, in_, reads, writes):
            k.op(dve, lambda e: e.reciprocal(out=out, in_=in_), reads, writes)

        ONES = sb(gstack, "ONES", [128, 128], BF16)
        IDENT = sb(gstack, "IDENT", [128, 128], BF16)
        CMASK = sb(gstack, "CMASK", [128, 128], BF16)
        CST = sb(gstack, "CST", [128, 4], F32)
        VL = sb(gstack, "VL", [128, NV], F32)
        tONES, tIDENT, tCMASK, tCST, tVL = (_Tile(n) for n in ("ONES", "IDENT", "CMASK", "CST", "VL"))
        PSALL = gstack.enter_context(nc.psum_tensor("psall", [128, 8, 512], F32))
        PSB = [PSALL[:, i, :] for i in range(8)]
        tPS = [_Tile(f"ps{i}") for i in range(8)]
        d_const = k.dsem("const")
        d_const2 = k.dsem("const2")
        d_vl = k.dsem("vl")
        d_w = [k.dsem(f"w{i}") for i in range(12)]
        keep_ds = [d_const, d_const2, d_vl] + d_w

        MEMSET(dve, ONES[:], 1.0, [tONES])
        MEMSET(dve, CST[:, 0:1], EPS, [tCST])
        MEMSET(dve, CST[:, 1:2], 1.0, [tCST])
        MEMSET(dve, CST[:, 2:3], 0.0, [tCST])
        k.dma(pool, IDENT[:], cident[:, :], [], [tIDENT], d_const)
        k.dma(pool, CMASK[:], cmaskd[:, :], [], [tCMASK], d_const2)

        tH = [_Tile(f"h{t}") for t in range(NT)]
        tXN = [_Tile(f"xn{t}") for t in range(NT)]
        tOM = [[_Tile(f"om{g}_{t}") for t in range(NT)] for g in range(3)]
        tROPE = _Tile("rope")
        tNONE = _Tile("ext")

        def cols(t):
            return slice(t * TT, (t + 1) * TT)

        def rmsnorm(srcs, rows, gcols, nfeat, outs, SQ, tSQ, RS, tRS, RSTD, tRSTD, psn, src_tiles, out_tiles,
                    sq_eng=None):
            C = len(srcs)
            if sq_eng is not None:
                ACT(sq_eng[1], sq_eng[0], AF.Square, src_tiles, [tSQ])
            else:
                for c in range(C):
                    ACT(SQ[0:rows[c], c, :], srcs[c], AF.Square, src_tiles, [tSQ])
            for c in range(C):
                MM(PSB[psn][:, :], ONES[0:rows[c], :], SQ[0:rows[c], c, :], c == 0, c == C - 1,
                   [tSQ, tONES], [tPS[psn]])
            ACT(RS[:, :], PSB[psn][:, :], AF.Sqrt, [tPS[psn], tCST], [tRS], bias=CST[:, 0:1], scale=1.0 / nfeat)
            RECIP(RSTD[:, :], RS[:, :], [tRS], [tRSTD])
            for c in range(C):
                STT(dve, outs[c], srcs[c], gcols[c], RSTD[0:rows[c], :], ALU.mult, ALU.mult,
                    src_tiles + [tRSTD, tVL], out_tiles)

        with contextlib.ExitStack() as st:
            POSI = sb(st, "POSI", [128, S], I32)
            POSF = sb(st, "POSF", [128, S], F32)
            ANG = sb(st, "ANG", [128, S], F32)
            YT = sb(st, "YT", [128, S], F32)
            FRQ = sb(st, "FRQ", [128, 2], F32)
            CSB = sb(st, "CSB", [128, S], F32)
            CSB2 = sb(st, "CSB2", [128, S], F32)
            tCSB, tCSB2 = _Tile("CSB"), _Tile("CSB2")
            tPOSI, tPOSF, tANG, tYT, tFRQ = (_Tile(n) for n in ("POSI", "POSF", "ANG", "YT", "FRQ"))
            d_p = k.dsem("pro")
            R = slice(64, 96)
            k.dma(sp, POSI[R, :], posd[R, :], [], [tPOSI], d_p)
            d_p2 = k.dsem("pro2")
            k.dma(sp, FRQ[:], cfreq[:, :], [], [tFRQ], d_p2)
            CP(dve, POSF[R, :], POSI[R, :], [tPOSI], [tPOSF])
            TS(dve, ANG[R, :], POSF[R, :], FRQ[R, 0:1], None, ALU.mult, None, [tPOSF, tFRQ], [tANG])
            TS(dve, ANG[R, :], ANG[R, :], 1.0 / (2 * math.pi), None, ALU.mult, None, [tANG], [tANG])

            def reduce_turns(shift):
                TS(dve, YT[R, :], ANG[R, :], shift, None, ALU.add, None, [tANG], [tYT])
                CP(dve, POSI[R, :], YT[R, :], [tYT], [tPOSI])
                CP(dve, POSF[R, :], POSI[R, :], [tPOSI], [tPOSF])
                TTn(dve, YT[R, :], YT[R, :], POSF[R, :], ALU.subtract, [tYT, tPOSF], [tYT])
                TS(dve, POSF[R, :], YT[R, :], 0.5, None, ALU.is_gt, None, [tYT], [tPOSF])
                TTn(dve, YT[R, :], YT[R, :], POSF[R, :], ALU.subtract, [tYT, tPOSF], [tYT])
                TS(dve, POSF[R, :], YT[R, :], -0.5, None, ALU.is_lt, None, [tYT], [tPOSF])
                TTn(dve, YT[R, :], YT[R, :], POSF[R, :], ALU.add, [tYT, tPOSF], [tYT])

            reduce_turns(0.25)
            ACT(CSB[R, :], YT[R, :], AF.Sin, [tYT], [tCSB], scale=2 * math.pi)
            k.dma(sp, ropeT[0, R, :], CSB[R, :], [tCSB], [tROPE], d_p)
            reduce_turns(0.0)
            ACT(CSB2[R, :], YT[R, :], AF.Sin, [tYT, tFRQ], [tCSB2], scale=FRQ[R, 1:2])
            k.dma(sp, ropeT[1, R, :], CSB2[R, :], [tCSB2], [tROPE], d_p)
            k.barrier()
            k.release_dsems(keep_ds)
        if stop_after == "rope":
            k.barrier()
            return nc

        def attention(l, QT, tQT, KT, tKT, VA, tVA, kd, scale, grp, gcol0):
            with contextlib.ExitStack() as st:
                PT = [sb(st, f"PT{i}", [128, 512], BF16) for i in range(3)]
                tPT = [_Tile(f"PT{i}") for i in range(3)]
                RC = sb(st, "RC", [128, 512], F32)
                OHb = [sb(st, f"OH{i}", [128, 4, 512], F32) for i in range(2)]
                tOHb = [_Tile("OH0"), _Tile("OH1")]
                SQ = sb(st, "SQa", [128, 4, 512], BF16)
                RS = sb(st, "RSa", [128, 512], F32)
                RSTD = sb(st, "RSTDa", [128, 512], F32)
                OMX = [sb(st, f"OMXa{i}", [128, 4, 512], BF16) for i in range(2)]
                tRC, tSQ, tRS, tRSTD = (_Tile(n) for n in ("RC", "SQa", "RSa", "RSTDa"))
                tOMX = [_Tile("OMXa0"), _Tile("OMXa1")]
                d_st = [k.dsem("ast0"), k.dsem("ast1")]
                tiles = [(j, h, i) for j in range(NT) for h in range(4) for i in range(4 * j + 4)]
                LOOK = 2
                deferred = []

                def emit_score(idx):
                    j, h, i = tiles[idx]
                    r = i - 4 * j
                    c0 = 128 * r if r >= 0 else 0
                    pb = idx % 3
                    ksl = slice(i * 128, (i + 1) * 128)
                    q0 = j * TT
                    if r >= 0:
                        MM(PSB[pb][:, c0:c0 + 128], KT[0:kd, h, ksl], QT[0:kd, h, q0 + c0:q0 + c0 + 128],
                           True, False, [tKT, tQT], [tPS[pb]], inc=False)
                        MM(PSB[pb][:, c0:c0 + 128], IDENT[:, :], CMASK[:, :], False, True,
                           [tIDENT, tCMASK], [tPS[pb]], inc=(c0 + 128 == 512))
                        if c0 + 128 < 512:
                            MM(PSB[pb][:, c0 + 128:512], KT[0:kd, h, ksl], QT[0:kd, h, q0 + c0 + 128:q0 + 512],
                               True, True, [tKT, tQT], [tPS[pb]])
                    else:
                        MM(PSB[pb][:, :], KT[0:kd, h, ksl], QT[0:kd, h, q0:q0 + 512], True, True,
                           [tKT, tQT], [tPS[pb]])

                def emit_norm(j):
                    om = OMX[j % 2]
                    tom = tOMX[j % 2]
                    oh, toh = OHb[j % 2], tOHb[j % 2]
                    rmsnorm([oh[0:64, h, :] for h in range(4)], [64] * 4,
                            [VL[0:64, C_GOUT64 + gcol0 + h:C_GOUT64 + gcol0 + h + 1] for h in range(4)], 256.0,
                            [om[0:64, h, :] for h in range(4)], SQ, tSQ, RS, tRS, RSTD, tRSTD, 5, [toh], [tom])
                    r0 = 256 * grp
                    k.dma(sp, omixT[r0:r0 + 256, cols(j)].rearrange("(h p) t -> p h t", p=64), om[0:64, :, :],
                          [tom], [tOM[grp][j]], d_st[j % 2])

                def emit_exp_pv(idx, step):
                    j, h, i = tiles[idx]
                    nk = 4 * j + 4
                    r = i - 4 * j
                    c0 = 128 * r if r >= 0 else 0
                    pb = idx % 3
                    ob = 3 + (h % 2)
                    ACT(PT[pb][:, c0:512], PSB[pb][:, c0:512], AF.Exp, [tPS[pb]], [tPT[pb]], scale=scale)
                    MM(PSB[ob][:, c0:512], VA[:, i, h, :], PT[pb][:, c0:512], i == 0, i == nk - 1,
                       [tVA, tPT[pb]], [tPS[ob]])
                    if i == nk - 1:
                        oh, toh = OHb[j % 2], tOHb[j % 2]
                        RECIP(RC[64:128, :], PSB[ob][64:128, :], [tPS[ob]], [tRC])
                        TTn(dve, oh[0:64, h, :], PSB[ob][0:64, :], RC[64:128, :], ALU.mult, [tPS[ob], tRC], [toh])
                        if h == 3:
                            deferred.append((step + 6, lambda jj=j: emit_norm(jj)))

                nt_ = len(tiles)
                for step in range(nt_ + LOOK):
                    if step < nt_:
                        emit_score(step)
                    if step >= LOOK:
                        emit_exp_pv(step - LOOK, step)
                    while deferred and deferred[0][0] <= step:
                        deferred.pop(0)[1]()
                while deferred:
                    deferred.pop(0)[1]()
                k.barrier()
                k.release_dsems(keep_ds)

        for l in range(DEPTH):
            hsrc = hv_x if l == 0 else hv_a
            tHs = [tNONE] * NT if l == 0 else tH
            k.dma(sp, VL[:], vecs[l, :, :], [], [tVL], d_vl)
            win = w_in[l].rearrange("(c p) n -> p c n", p=128)

            with contextlib.ExitStack() as st:
                QM = sb(st, "QM", [128, 4, S], BF16)
                KM = sb(st, "KM", [128, 4, S], BF16)
                VA = sb(st, "VA", [128, 32, 4, 128], BF16)
                tQM, tKM, tVA = _Tile("QM"), _Tile("KM"), _Tile("VA")
                MEMSET(pool, VA[:, :, :, 64:128], 1.0, [tVA])
                with contextlib.ExitStack() as s1:
                    HTb1 = [sb(s1, f"HT1{i}", [128, 8, TT], F32) for i in range(2)]
                    tHTb1 = [_Tile("HT10"), _Tile("HT11")]
                    d_hb = [k.dsem("p1h0"), k.dsem("p1h1")]
                    XNb1 = [sb(s1, f"XN1{i}", [128, 8, TT], BF16) for i in range(2)]
                    tXNb1 = [_Tile("XN10"), _Tile("XN11")]
                    d_xsb = [k.dsem("p1xs0"), k.dsem("p1xs1")]
                    SQ = sb(s1, "SQ", [128, 8, TT], BF16)
                    SQB = sb(s1, "SQB", [128, 2, TT], BF16)
                    RSB = sb(s1, "RSB", [128, TT], F32)
                    RSTDB = sb(s1, "RSTDB", [128, TT], F32)
                    tSQB, tRSB, tRSTDB = _Tile("SQB"), _Tile("RSB"), _Tile("RSTDB")
                    RS = sb(s1, "RS", [128, TT], F32)
                    RSTD = sb(s1, "RSTD", [128, TT], F32)
                    QCN = sb(s1, "QCN", [128, 2, TT], BF16)
                    KVN = sb(s1, "KVN", [128, TT], BF16)
                    T1 = sb(s1, "T1", [128, TT], F32)
                    T2 = sb(s1, "T2", [128, TT], F32)
                    RC_ = sb(s1, "ROPEC", [128, TT], F32)
                    RS_ = sb(s1, "ROPES", [128, TT], F32)
                    WA = sb(s1, "WA", [128, 8, 320], BF16)
                    WKA = sb(s1, "WKA", [128, 8, 96], BF16)
                    WKB = sb(s1, "WKB", [128, 8, 96], BF16)
                    WUQ = sb(s1, "WUQ", [128, 2, 384], BF16)
                    WUQB = sb(s1, "WUQB", [128, 2, 4, 96], BF16)
                    WUKV = sb(s1, "WUKV", [128, 512], BF16)
                    WV = sb(s1, "WV", [128, 4, 64], BF16)
                    (tSQ, tRS, tRSTD, tQCN, tKVN, tT1, tT2, tRC_, tRS_, tW) = (
                        _Tile(n) for n in ("SQ", "RS", "RSTD", "QCN", "KVN", "T1", "T2", "RC_", "RS_", "W1"))
                    d_r, d_r2 = k.dsem("p1r"), k.dsem("p1r2")
                    MEMSET(pool, WKA[:], 0.0, [tW])
                    MEMSET(pool, WKB[:], 0.0, [tW])
                    MEMSET(pool, WUQB[:], 0.0, [tW])
                    MEMSET(pool, WUQ[:], 0.0, [tW])
                    k.dma(pool, WA[:], win[:, :, 0:320], [], [tW], d_w[0])
                    k.dma(pool, WKA[:, :, 64:96], win[:, :, 320:352], [], [tW], d_w[0])
                    k.dma(pool, WKB[:, :, 64:80], win[:, :, 336:352], [], [tW], d_w[0])
                    k.dma(pool, WKB[:, :, 80:96], win[:, :, 320:336], [], [tW], d_w[0])
                    k.dma(pool, WUQ[:, 0, :], w_uq[l, 0:128, :], [], [tW], d_w[0])
                    k.dma(pool, WUQ[0:64, 1, :], w_uq[l, 128:192, :], [], [tW], d_w[0])
                    uq4 = w_uq[l].rearrange("r (h d) -> r h d", h=4)
                    k.dma(pool, WUQB[:, 0, :, 64:80], uq4[0:128, :, 80:96], [], [tW], d_w[0])
                    k.dma(pool, WUQB[:, 0, :, 80:96], uq4[0:128, :, 64:80], [], [tW], d_w[0])
                    k.dma(pool, WUQB[0:64, 1, :, 64:80], uq4[128:192, :, 80:96], [], [tW], d_w[0])
                    k.dma(pool, WUQB[0:64, 1, :, 80:96], uq4[128:192, :, 64:80], [], [tW], d_w[0])
                    k.dma(pool, WUKV[:], w_ukv[l, :, :], [], [tW], d_w[0])
                    k.dma(pool, WV[:], w_ukv[l].rearrange("r (h d) -> r h d", h=4)[:, :, 64:128], [], [tW], d_w[0])
                    def ldh1(t):
                        k.dma(sp, HTb1[t % 2][:], hsrc[:, :, cols(t)], [tHs[t]], [tHTb1[t % 2]], d_hb[t % 2])

                    def normA(t):
                        HT, tHT = HTb1[t % 2], tHTb1[t % 2]
                        rmsnorm([HT[:, c, :] for c in range(8)], [128] * 8,
                                [VL[:, C_GMIX + c:C_GMIX + c + 1] for c in range(8)], 1024.0,
                                [XNb1[t % 2][:, c, :] for c in range(8)], SQ, tSQ, RS, tRS, RSTD, tRSTD, 7, [tHT],
                                [tXNb1[t % 2]], sq_eng=(HT[:, :, :], SQ[:, :, :]))
                        k.dma(sp, xnv[:, :, cols(t)], XNb1[t % 2][:], [tXNb1[t % 2]], [tXN[t]], d_xsb[t % 2])
                    ldh1(0)
                    ldh1(1)
                    normA(0)
                    for t in range(NT):
                        cs = cols(t)
                        XN, tXNs = XNb1[t % 2], tXNb1[t % 2]
                        if t >= 1 and t + 1 < NT:
                            ldh1(t + 1)
                        k.dma(sp, RC_[64:96, :], ropeT[0, 64:96, cs], [tROPE], [tRC_], d_r)
                        k.dma(sp, RS_[64:96, :], ropeT[1, 64:96, cs], [tROPE], [tRS_], d_r2)
                        specs = [(0, 128, WA, slice(0, 128)), (1, 64, WA, slice(128, 192)), (2, 128, WA, slice(192, 320)),
                                 (3, 96, WKA, slice(0, 96)), (4, 96, WKB, slice(0, 96))]
                        for (pb, m, Wt, csl) in specs:
                            for c in range(8):
                                MM(PSB[pb][0:m, :], Wt[:, c, csl], XN[:, c, :], c == 0, c == 7, [tW, tXNs], [tPS[pb]])
                        if t + 1 < NT:
                            normA(t + 1)
                        rmsnorm([PSB[0][:, :], PSB[1][0:64, :]], [128, 64],
                                [VL[:, C_GQC:C_GQC + 1], VL[0:64, C_GQC + 1:C_GQC + 2]], 192.0,
                                [QCN[:, 0, :], QCN[0:64, 1, :]], SQB, tSQB, RSB, tRSB, RSTDB, tRSTDB, 7,
                                [tPS[0], tPS[1]], [tQCN])
                        rmsnorm([PSB[2][:, :]], [128], [VL[:, C_GKVC:C_GKVC + 1]], 128.0, [KVN[:, :]],
                                SQB, tSQB, RSB, tRSB, RSTDB, tRSTDB, 7, [tPS[2]], [tKVN])
                        TTn(dve, T1[64:96, :], PSB[3][64:96, :], RC_[64:96, :], ALU.mult, [tPS[3], tRC_], [tT1])
                        TTn(dve, T2[64:96, :], PSB[4][64:96, :], RS_[64:96, :], ALU.mult, [tPS[4], tRS_], [tT2])
                        TTn(dve, T1[64:96, :], T1[64:96, :], T2[64:96, :], ALU.add, [tT1, tT2], [tT1])
                        for h in range(4):
                            CP(act if h % 2 == 0 else pool, KM[64:96, h, cs], T1[64:96, :], [tT1], [tKM])
                        for h in range(4):
                            pa, pb2 = (0, 1) if h % 2 == 0 else (5, 6)
                            hs = slice(h * 96, (h + 1) * 96)
                            MM(PSB[pa][0:96, :], WUQ[:, 0, hs], QCN[:, 0, :], True, False, [tW, tQCN], [tPS[pa]])
                            MM(PSB[pa][0:96, :], WUQ[0:64, 1, hs], QCN[0:64, 1, :], False, True, [tW, tQCN], [tPS[pa]])
                            MM(PSB[pb2][0:96, :], WUQB[:, 0, h, :], QCN[:, 0, :], True, False, [tW, tQCN], [tPS[pb2]])
                            MM(PSB[pb2][0:96, :], WUQB[0:64, 1, h, :], QCN[0:64, 1, :], False, True, [tW, tQCN], [tPS[pb2]])
                            CP(act, QM[0:64, h, cs], PSB[pa][0:64, :], [tPS[pa]], [tQM])
                            TTn(dve, T1[64:96, :], PSB[pa][64:96, :], RC_[64:96, :], ALU.mult, [tPS[pa], tRC_], [tT1])
                            TTn(dve, T2[64:96, :], PSB[pb2][64:96, :], RS_[64:96, :], ALU.mult, [tPS[pb2], tRS_], [tT2])
                            TTn(dve, QM[64:96, h, cs], T1[64:96, :], T2[64:96, :], ALU.add, [tT1, tT2], [tQM])
                        for h in range(4):
                            pb = 2 + (h % 2)
                            MM(PSB[pb][0:64, :], WUKV[:, h * 128:h * 128 + 64], KVN[:, :], True, True, [tW, tKVN], [tPS[pb]])
                            CP(act, KM[0:64, h, cs], PSB[pb][0:64, :], [tPS[pb]], [tKM])
                        for s_ in range(4):
                            pb = 5 + (s_ % 2)
                            MM(PSB[pb][:, 0:256], KVN[:, s_ * 128:(s_ + 1) * 128], WV[:, :, :].rearrange("p h d -> p (h d)"),
                               True, True, [tKVN, tW], [tPS[pb]])
                            CP(dve, VA[:, t * 4 + s_, :, 0:64], PSB[pb][:, 0:256].rearrange("p (h d) -> p h d", h=4),
                               [tPS[pb]], [tVA])
                    k.barrier()
                    k.release_dsems(keep_ds)
                if stop_after == f"p1_{l}":
                    return nc
                attention(l, QM, tQM, KM, tKM, VA, tVA, 96, 96.0 ** -0.5, 0, 0)
            if stop_after == f"p2_{l}":
                return nc

            with contextlib.ExitStack() as st:
                FQ = sb(st, "FQ", [128, 4, S], BF16)
                FK = sb(st, "FK", [128, 4, S], BF16)
                FVA = sb(st, "FVA", [128, 32, 4, 128], BF16)
                tFQ, tFK, tFVA = _Tile("FQ"), _Tile("FK"), _Tile("FVA")
                MEMSET(pool, FVA[:, :, :, 64:128], 1.0, [tFVA])
                with contextlib.ExitStack() as s1:
                    XNb = [sb(s1, f"XNf{i}", [128, 8, TT], BF16) for i in range(3)]
                    tXNb = [_Tile("XNf0"), _Tile("XNf1"), _Tile("XNf2")]
                    d_x = [k.dsem("p3x0"), k.dsem("p3x1"), k.dsem("p3x2")]
                    CCb = [sb(s1, f"CC{i}", [128, TT], BF16) for i in range(2)]
                    tCCb = [_Tile("CC0"), _Tile("CC1")]
                    EX = sb(s1, "EX", [4, TT], F32)
                    NL = sb(s1, "NL", [4, TT], F32)
                    ONE4 = sb(s1, "ONE4", [4, TT], F32)
                    CF = sb(s1, "CF", [4, TT], F32)
                    R1 = sb(s1, "R1", [4, TT], F32)
                    CHI = sb(s1, "CHI", [4, TT], BF16)
                    CMID = sb(s1, "CMID", [4, TT], BF16)
                    CLO = sb(s1, "CLO", [4, TT], BF16)
                    CARRY = sb(s1, "CARRY", [4, 2], F32)
                    NEGB = sb(s1, "NEGB", [4, 2], F32)
                    WFQ = sb(s1, "WFQ", [128, 8, 4, 70], BF16)
                    WFK = sb(s1, "WFK", [128, 8, 4, 70], BF16)
                    WFV = sb(s1, "WFV", [128, 8, 256], BF16)
                    WF4 = sb(s1, "WF4", [128, 8, 4], BF16)
                    SEL = sb(s1, "SEL", [128, 8, 70], BF16)
                    (tEX, tNL, tONE4, tCF, tR1, tCHI, tCMID, tCLO, tCARRY, tNEGB, tW) = (
                        _Tile(n) for n in ("EX", "NL", "ONE4", "CF", "R1", "CHI", "CMID", "CLO", "CARRY", "NEGB", "W3"))
                    MEMSET(pool, WFQ[:], 0.0, [tW])
                    MEMSET(pool, WFK[:], 0.0, [tW])
                    for i_ in range(2):
                        MEMSET(dve, CCb[i_][:], 0.0, [tCCb[i_]])
                        MEMSET(dve, CCb[i_][96:97, :], 1.0, [tCCb[i_]])
                    MEMSET(dve, ONE4[:], 1.0, [tONE4])
                    MEMSET(dve, CARRY[:], 0.0, [tCARRY])
                    TS(dve, NEGB[0:4, 0:1], VL[0:4, C_BF:C_BF + 1], -1.0, None, ALU.mult, None, [tVL], [tNEGB])
                    for c in range(8):
                        k.dma(pool, WFQ[:, c, :, 0:64], win[:, c, 352:608].rearrange("p (h d) -> p h d", h=4), [], [tW], d_w[1])
                        k.dma(pool, WFK[:, c, :, 0:64], win[:, c, 608:864].rearrange("p (h d) -> p h d", h=4), [], [tW], d_w[1])
                    k.dma(pool, WFV[:], win[:, :, 864:1120], [], [tW], d_w[1])
                    k.dma(pool, WF4[:], win[:, :, 1120:1124], [], [tW], d_w[1])
                    k.dma(pool, SEL[:], cseld[:, :, :], [], [tW], d_w[1])
                    def ld3(t):
                        k.dma(sp, XNb[t % 3][:], xnv[:, :, cols(t)], [tXN[t]], [tXNb[t % 3]], d_x[t % 3])

                    def cchain(t):
                        XN, tXNs = XNb[t % 3], tXNb[t % 3]
                        CC, tCC = CCb[t % 2], tCCb[t % 2]
                        for c in range(8):
                            MM(PSB[7][0:4, :], WF4[:, c, :], XN[:, c, :], c == 0, c == 7, [tW, tXNs], [tPS[7]])
                        ACT(EX[:, :], PSB[7][0:4, :], AF.Exp, [tPS[7], tNEGB], [tEX], bias=NEGB[0:4, 0:1], scale=-1.0)
                        ACT(NL[:, :], EX[:, :], AF.Ln, [tEX, tCST], [tNL], bias=CST[0:4, 1:2], scale=1.0)
                        TS(dve, NL[:, :], NL[:, :], -1.0, None, ALU.mult, None, [tNL], [tNL])
                        k.op(dve, lambda e: e.tensor_tensor_scan(out=CF[:, :], data0=ONE4[:, :], data1=NL[:, :],
                                                                 initial=CARRY[0:4, 0:1], op0=ALU.mult, op1=ALU.add),
                             [tONE4, tNL, tCARRY], [tCF])
                        CP(dve, CARRY[0:4, 0:1], CF[:, TT - 1:TT], [tCF], [tCARRY])
                        CP(dve, CHI[:, :], CF[:, :], [tCF], [tCHI])
                        TTn(dve, R1[:, :], CF[:, :], CHI[:, :], ALU.subtract, [tCF, tCHI], [tR1])
                        CP(dve, CMID[:, :], R1[:, :], [tR1], [tCMID])
                        TTn(dve, R1[:, :], R1[:, :], CMID[:, :], ALU.subtract, [tR1, tCMID], [tR1])
                        CP(dve, CLO[:, :], R1[:, :], [tR1], [tCLO])
                        CP(pool, CC[0:4, :], CHI[:, :], [tCHI], [tCC])
                        CP(pool, CC[32:36, :], CMID[:, :], [tCMID], [tCC])
                        CP(pool, CC[64:68, :], CLO[:, :], [tCLO], [tCC])
                    ld3(0)
                    ld3(1)
                    cchain(0)
                    for t in range(NT):
                        cs = cols(t)
                        XN, tXNs = XNb[t % 3], tXNb[t % 3]
                        CC, tCC = CCb[t % 2], tCCb[t % 2]
                        if t + 2 < NT:
                            ld3(t + 2)
                        if t + 1 < NT:
                            cchain(t + 1)
                        for h in range(4):
                            for (qi, Wt, DST, tD) in ((0, WFQ, FQ, tFQ), (1, WFK, FK, tFK)):
                                pb = (2 * h + qi) % 4
                                for c in range(8):
                                    MM(PSB[pb][0:70, :], Wt[:, c, h, :], XN[:, c, :], c == 0, False, [tW, tXNs], [tPS[pb]])
                                MM(PSB[pb][0:70, :], SEL[:, qi * 4 + h, :], CC[:, :], False, True, [tW, tCC], [tPS[pb]])
                                CP(act if qi == 0 else dve, DST[0:70, h, cs], PSB[pb][0:70, :], [tPS[pb]], [tD])
                        for s_ in range(4):
                            pb = 4 + (s_ % 2)
                            for c in range(8):
                                MM(PSB[pb][:, 0:256], XN[:, c, s_ * 128:(s_ + 1) * 128], WFV[:, c, :], c == 0, c == 7,
                                   [tXNs, tW], [tPS[pb]])
                            CP(act if s_ % 2 else dve, FVA[:, t * 4 + s_, :, 0:64],
                               PSB[pb][:, 0:256].rearrange("p (h d) -> p h d", h=4), [tPS[pb]], [tFVA])
                    k.barrier()
                    k.release_dsems(keep_ds)
                if stop_after == f"p3_{l}":
                    return nc
                attention(l, FQ, tFQ, FK, tFK, FVA, tFVA, 70, 0.125, 1, 4)
            if stop_after == f"p4_{l}":
                return nc

            with contextlib.ExitStack() as s1:
                XNb = [sb(s1, f"XNl{i}", [128, 8, TT], BF16) for i in range(2)]
                tXNb = [_Tile("XNl0"), _Tile("XNl1")]
                d_x = [k.dsem("p5x0"), k.dsem("p5x1")]
                LXB = sb(s1, "LXB", [128, 4, TT + 3], F32)
                XC = sb(s1, "XC", [128, 4, TT], F32)
                XCB = sb(s1, "XCB", [128, 4, TT], BF16)
                RG = sb(s1, "RG", [128, 4, TT], F32)
                IG = sb(s1, "IG", [128, 4, TT], F32)
                AG = sb(s1, "AG", [128, 4, TT], F32)
                A2 = sb(s1, "A2", [128, 4, TT], F32)
                BX = sb(s1, "BX", [128, 4, TT], F32)
                GL = sb(s1, "GL", [128, 4, TT], F32)
                HS = sb(s1, "HS", [128, 4, TT], F32)
                OL = sb(s1, "OL", [128, 4, TT], F32)
                SQ = sb(s1, "SQl", [128, 4, TT], BF16)
                RS = sb(s1, "RSl", [128, TT], F32)
                RSTD = sb(s1, "RSTDl", [128, TT], F32)
                OMX = [sb(s1, f"OMXl{i}", [128, 4, TT], BF16) for i in range(2)]
                tOMX = [_Tile("OMXl0"), _Tile("OMXl1")]
                d_st = [k.dsem("lst0"), k.dsem("lst1")]
                CARH = sb(s1, "CARH", [128, 4], F32)
                COEF = sb(s1, "COEF", [128, 8], F32)
                WL = sb(s1, "WL", [128, 8, 1024], BF16)
                WRB = sb(s1, "WRB", [128, 4, 128], BF16)
                WIB = sb(s1, "WIB", [128, 4, 128], BF16)
                (tLXB, tXC, tXCB, tRG, tIG, tAG, tA2, tBX, tGL, tHS, tOL, tSQ, tRS, tRSTD, tCARH, tCOEF, tW) = (
                    _Tile(n) for n in ("LXB", "XC", "XCB", "RG", "IG", "AG", "A2", "BX", "GL", "HS", "OL", "SQl", "RSl",
                                       "RSTDl", "CARH", "COEF", "W5"))
                MEMSET(pool, WRB[:], 0.0, [tW])
                MEMSET(pool, WIB[:], 0.0, [tW])
                MEMSET(dve, LXB[:], 0.0, [tLXB])
                MEMSET(dve, CARH[:], 0.0, [tCARH])
                ACT(COEF[:, 0:4], VL[:, C_LAM:C_LAM + 4], AF.Exp, [tVL], [tCOEF], scale=-1.0)
                ACT(COEF[:, 0:4], COEF[:, 0:4], AF.Ln, [tCOEF, tCST], [tCOEF], bias=CST[:, 1:2], scale=1.0)
                TS(dve, COEF[:, 4:8], COEF[:, 0:4], -16.0, None, ALU.mult, None, [tCOEF], [tCOEF])
                TS(dve, COEF[:, 0:4], COEF[:, 0:4], -8.0, None, ALU.mult, None, [tCOEF], [tCOEF])
                k.dma(pool, WL[:], win[:, :, 1124:2148], [], [tW], d_w[2])
                for c in range(4):
                    for hb in range(2):
                        ps_ = slice(hb * 64, hb * 64 + 64)
                        k.dma(pool, WRB[ps_, c, ps_], w_r[l, 2 * c + hb, :, :], [], [tW], d_w[2])
                        k.dma(pool, WIB[ps_, c, ps_], w_i[l, 2 * c + hb, :, :], [], [tW], d_w[2])
                k.dma(sp, XNb[0][:], xnv[:, :, cols(0)], [tXN[0]], [tXNb[0]], d_x[0])
                for t in range(NT):
                    cs = cols(t)
                    XN, tXNs = XNb[t % 2], tXNb[t % 2]
                    if t + 1 < NT:
                        k.dma(sp, XNb[(t + 1) % 2][:], xnv[:, :, cols(t + 1)], [tXN[t + 1]], [tXNb[(t + 1) % 2]],
                              d_x[(t + 1) % 2])
                    tP03 = [tPS[0], tPS[1], tPS[2], tPS[3]]
                    tP47 = [tPS[4], tPS[5], tPS[6], tPS[7]]
                    for c in range(4):
                        for kc in range(8):
                            MM(PSB[c][:, :], WL[:, kc, c * 128:(c + 1) * 128], XN[:, kc, :], kc == 0, kc == 7,
                               [tW, tXNs], [tPS[c]])
                    CP(pool, LXB[:, :, 0:3], LXB[:, :, TT:TT + 3], [tLXB], [tLXB])
                    CP(act, LXB[:, :, 3:TT + 3], PSALL[:, 0:4, :], tP03, [tLXB])
                    tXCc = [_Tile(f"XC{c}") for c in range(4)] if t == 0 else tXCc
                    tXCBc = [_Tile(f"XCB{c}") for c in range(4)] if t == 0 else tXCBc
                    for c in range(4):
                        wcol = lambda kk: VL[:, C_LCW + kk * 4 + c:C_LCW + kk * 4 + c + 1]
                        ACT(XC[:, c, :], LXB[:, c, 3:TT + 3], AF.Identity, [tLXB, tVL], [tXCc[c]],
                            bias=VL[:, C_LCB + c:C_LCB + c + 1], scale=wcol(3))
                        for kk in range(3):
                            STT(dve, XC[:, c, :], LXB[:, c, kk:kk + TT], wcol(kk), XC[:, c, :], ALU.mult, ALU.add,
                                [tLXB, tXCc[c], tVL], [tXCc[c]])
                        CP(act, XCB[:, c, :], XC[:, c, :], [tXCc[c]], [tXCBc[c]])
                    for c in range(4):
                        MM(PSB[c][:, :], WRB[:, c, :], XCB[:, c, :], True, True, [tW, tXCBc[c]], [tPS[c]])
                    for c in range(4):
                        MM(PSB[4 + c][:, :], WIB[:, c, :], XCB[:, c, :], True, True, [tW, tXCBc[c]], [tPS[4 + c]])
                    for c in range(4):
                        ACT(RG[:, c, :], PSB[c][:, :], AF.Sigmoid, [tPS[c], tVL], [tRG], bias=VL[:, C_BR + c:C_BR + c + 1])
                    for c in range(4):
                        ACT(IG[:, c, :], PSB[4 + c][:, :], AF.Sigmoid, [tPS[4 + c], tVL], [tIG],
                            bias=VL[:, C_BI + c:C_BI + c + 1])
                    for c in range(4):
                        for kc in range(8):
                            MM(PSB[c][:, :], WL[:, kc, 512 + c * 128:512 + (c + 1) * 128], XN[:, kc, :], kc == 0, kc == 7,
                               [tW, tXNs], [tPS[c]])
                    TTn(dve, BX[:, :, :], IG[:, :, :], XC[:, :, :], ALU.mult, [tIG] + tXCc, [tBX])
                    for c in range(4):
                        ACT(AG[:, c, :], RG[:, c, :], AF.Exp, [tRG, tCOEF], [tAG], scale=COEF[:, c:c + 1])
                    for c in range(4):
                        ACT(A2[:, c, :], RG[:, c, :], AF.Exp, [tRG, tCOEF], [tA2], scale=COEF[:, 4 + c:5 + c])
                    ACT(A2[:, :, :], A2[:, :, :], AF.Sqrt, [tA2, tCST], [tA2], bias=CST[:, 1:2], scale=-1.0)
                    TTn(dve, BX[:, :, :], BX[:, :, :], A2[:, :, :], ALU.mult, [tBX, tA2], [tBX])
                    for c in range(4):
                        k.op(dve, lambda e: e.tensor_tensor_scan(out=HS[:, c, :], data0=AG[:, c, :], data1=BX[:, c, :],
                                                                 initial=CARH[:, c:c + 1], op0=ALU.mult, op1=ALU.add),
                             [tAG, tBX, tCARH], [tHS])
                        CP(dve, CARH[:, c:c + 1], HS[:, c, TT - 1:TT], [tHS], [tCARH])
                    ACT(GL[:, :, :], PSALL[:, 0:4, :], AF.Gelu_apprx_tanh, tP03, [tGL])
                    TTn(dve, OL[:, :, :], HS[:, :, :], GL[:, :, :], ALU.mult, [tHS, tGL], [tOL])
                    om, tom = OMX[t % 2], tOMX[t % 2]
                    rmsnorm([OL[:, c, :] for c in range(4)], [128] * 4,
                            [VL[:, C_GOUTL + c:C_GOUTL + c + 1] for c in range(4)], 512.0,
                            [om[:, c, :] for c in range(4)], SQ, tSQ, RS, tRS, RSTD, tRSTD, 0, [tOL], [tom],
                            sq_eng=(OL[:, :, :], SQ[:, :, :]))
                    k.dma(sp, omv[:, 4:8, cs], om[:, :, :], [tom], [tOM[2][t]], d_st[t % 2])
                k.barrier()
                k.release_dsems(keep_ds)
            if stop_after == f"p5_{l}":
                return nc

            with contextlib.ExitStack() as s1:
                WO = sb(s1, "WO", [128, 8, 1024], BF16)
                OMb = [sb(s1, f"OMt{i}", [128, 8, TT], BF16) for i in range(2)]
                HTb = [sb(s1, f"HT6{i}", [128, 8, TT], F32) for i in range(3)]
                XN2 = sb(s1, "XN2", [128, 8, TT], BF16)
                SQ = sb(s1, "SQ6", [128, 8, TT], BF16)
                RS = sb(s1, "RS6", [128, TT], F32)
                RSTD = sb(s1, "RSTD6", [128, TT], F32)
                tOMb = [_Tile("OMt0"), _Tile("OMt1")]
                tHTb = [_Tile("HT60"), _Tile("HT61"), _Tile("HT62")]
                tXN2, tSQ, tRS, tRSTD, tW = (_Tile(n) for n in ("XN2", "SQ6", "RS6", "RSTD6", "W6"))
                d_o = [k.dsem("p6o0"), k.dsem("p6o1")]
                d_h = [k.dsem("p6h0"), k.dsem("p6h1"), k.dsem("p6h2")]
                d_xs = k.dsem("p6xs")
                k.dma(pool, WO[:], w_o[l].rearrange("(c p) n -> p c n", p=128), [], [tW], d_w[3])

                def ld6(t):
                    i = t % 2
                    ih = t % 3
                    k.dma(sp, OMb[i][:], omv[:, :, cols(t)], [tOM[0][t], tOM[1][t], tOM[2][t]], [tOMb[i]], d_o[i])
                    k.dma(sp, HTb[ih][:], hsrc[:, :, cols(t)], [tHs[t]], [tHTb[ih]], d_h[ih])

                def norm6(t):
                    HT, tHT = HTb[t % 3], tHTb[t % 3]
                    rmsnorm([HT[:, c, :] for c in range(8)], [128] * 8,
                            [VL[:, C_GFFN + c:C_GFFN + c + 1] for c in range(8)], 1024.0,
                            [XN2[:, c, :] for c in range(8)], SQ, tSQ, RS, tRS, RSTD, tRSTD, 7, [tHT], [tXN2],
                            sq_eng=(HT[:, :, :], SQ[:, :, :]))
                    k.dma(sp, xnv[:, :, cols(t)], XN2[:], [tXN2], [tXN[t]], d_xs)
                ld6(0)
                for t in range(NT + 1):
                    if t < NT:
                        i = t % 2
                        cs = cols(t)
                        if t + 1 < NT:
                            ld6(t + 1)
                        HT, tHT = HTb[t % 3], tHTb[t % 3]
                        for m in range(8):
                            pb = m % 4
                            for c in range(8):
                                MM(PSB[pb][:, :], WO[:, c, m * 128:(m + 1) * 128], OMb[i][:, c, :], c == 0, c == 7,
                                   [tW, tOMb[i]], [tPS[pb]])
                            TTn(dve, HT[:, m, :], PSB[pb][:, :], HT[:, m, :], ALU.add, [tPS[pb], tHT], [tHT])
                        k.dma(sp, hv_a[:, :, cs], HT[:], [tHT], [tH[t]], d_h[t % 3])
                    if t >= 1:
                        norm6(t - 1)
                k.barrier()
                k.release_dsems(keep_ds)
            if stop_after == f"p6_{l}":
                return nc

            wup = w_up[l].rearrange("(c p) n -> p c n", p=128)
            wdn = w_down[l].rearrange("(j p) n -> p j n", p=128)
            last = (l == DEPTH - 1)
            for half in range(2):
                with contextlib.ExitStack() as s1:
                    WUG = sb(s1, "WUG", [128, 8, 11 * 128], BF16)
                    WUV = sb(s1, "WUV", [128, 8, 11 * 128], BF16)
                    WD = sb(s1, "WD", [128, 11, 1024], BF16)
                    XNb = [sb(s1, f"XN7{i}", [128, 8, TT], BF16) for i in range(2)]
                    tXNb = [_Tile("XN70"), _Tile("XN71")]
                    nhb = 2
                    HTb = [sb(s1, f"HT7{i}", [128, 8, TT], F32) for i in range(nhb)]
                    tHTb = [_Tile(f"HT7{i}") for i in range(nhb)]
                    ATb = [sb(s1, f"AT{i}", [128, 11, TT], BF16) for i in range(2)]
                    tATb = [_Tile("AT0"), _Tile("AT1")]
                    ACG = [sb(s1, f"ACG{i}", [128, TT], F32) for i in range(2)]
                    ACV = [sb(s1, f"ACV{i}", [128, TT], F32) for i in range(2)]
                    SG = [sb(s1, f"SG{i}", [128, TT], F32) for i in range(2)]
                    TAILb = [sb(s1, f"TAIL{i}", [128, 22, 2], F32) for i in range(2)]
                    tTAILb = [_Tile("TAIL0"), _Tile("TAIL1")]
                    tW = _Tile("W7")
                    tWs = [_Tile("W7a"), _Tile("W7b"), _Tile("W7c"), _Tile("W7d"), _Tile("W7e")]
                    tACG = [_Tile("ACG0"), _Tile("ACG1")]
                    tACV = [_Tile("ACV0"), _Tile("ACV1")]
                    tSG = [_Tile("SG0"), _Tile("SG1")]
                    d_x = [k.dsem(f"p7x0_{half}"), k.dsem(f"p7x1_{half}")]
                    d_h = [k.dsem(f"p7h{i}_{half}") for i in range(nhb)]
                    j0 = half * 11
                    k.dma(pool, WUG[:], wup[:, :, j0 * 128:(j0 + 11) * 128], [], [tWs[0]], d_w[4 + half])
                    k.dma(pool, WUV[:], wup[:, :, 2816 + j0 * 128:2816 + (j0 + 11) * 128], [], [tWs[1]], d_w[7 + half])
                    k.dma(pool, WD[:], wdn[:, j0:j0 + 11, :], [], [tWs[2]], d_w[9 + half])
                    MEMSET(dve, TAILb[0][:], 0.0, [tTAILb[0]])
                    if half == 1:
                        WG = sb(s1, "WG", [128, 8, 1024], BF16)
                        WP = sb(s1, "WP", [128, 2, 1024], BF16)
                        XN3 = sb(s1, "XN3", [128, 8, TT], BF16)
                        SQ = sb(s1, "SQ8", [128, 8, TT], BF16)
                        RS = sb(s1, "RS8", [128, TT], F32)
                        RSTD = sb(s1, "RSTD8", [128, TT], F32)
                        PTb = [sb(s1, f"PTp{i}", [128, 2, TT], BF16) for i in range(2)]
                        SGT = sb(s1, "SGT", [128, TT], F32)
                        TMP = sb(s1, "TMP", [128, TT], F32)
                        tXN3, tSQ, tRS, tRSTD, tSGT, tTMP = (_Tile(n) for n in ("XN3", "SQ8", "RS8", "RSTD8", "SGT", "TMP"))
                        tPTb = [_Tile("PTp0"), _Tile("PTp1")]
                        d_p = [k.dsem("p8p0"), k.dsem("p8p1")]
                        k.dma(pool, WG[:], w_pg[l].rearrange("(c p) n -> p c n", p=128), [], [tWs[3]], d_w[6])
                        k.dma(pool, WP[:], w_pp[l].rearrange("(c p) n -> p c n", p=128), [], [tWs[4]], d_w[11])
                        ptv = pT[l].rearrange("(c p) t -> p c t", p=128)

                    def ld7(t):
                        i = t % 2
                        k.dma(sp, XNb[i][:], xnv[:, :, cols(t)], [tXN[t]], [tXNb[i]], d_x[i])
                        ih = t % nhb
                        k.dma(sp, HTb[ih][:], hv_a[:, :, cols(t)], [tH[t]], [tHTb[ih]], d_h[ih])
                        if half == 1:
                            k.dma(pool, PTb[i][:], ptv[:, :, cols(t)], [], [tPTb[i]], d_p[i])
                    ld7(0)
                    cidx = 0
                    for t in range(NT):
                        i = t % 2
                        cs = cols(t)
                        if t + 1 < NT and nhb == 2:
                            ld7(t + 1)
                        XN, tXNs = XNb[i], tXNb[i]
                        HT, tHT = HTb[t % nhb], tHTb[t % nhb]
                        AT, tAT = ATb[i], tATb[i]
                        TAIL, tTAIL = TAILb[i], tTAILb[i]
                        TAILn, tTAILn = TAILb[1 - i], tTAILb[1 - i]
                        for jj in range(11):
                            j = j0 + jj
                            bi = cidx % 2
                            cidx += 1
                            for (which, Wt, pb, ACC, tACC, ch) in ((0, WUG, 2 * bi, ACG[bi], tACG[bi], j),
                                                                   (1, WUV, 2 * bi + 1, ACV[bi], tACV[bi], 22 + j)):
                                for c in range(8):
                                    MM(PSB[pb][:, :], Wt[:, c, jj * 128:(jj + 1) * 128], XN[:, c, :], c == 0, c == 7,
                                       [tWs[which], tXNs], [tPS[pb]])
                                w0 = VL[:, C_FCW + ch:C_FCW + ch + 1]
                                w1 = VL[:, C_FCW + 44 + ch:C_FCW + 44 + ch + 1]
                                w2 = VL[:, C_FCW + 88 + ch:C_FCW + 88 + ch + 1]
                                bb = VL[:, C_FCB + ch:C_FCB + ch + 1]
                                tl = jj * 2 + which
                                ACT(ACC[:, :], PSB[pb][:, :], AF.Identity, [tPS[pb], tVL], [tACC], bias=bb, scale=w2)
                                STT(dve, ACC[:, 1:TT], PSB[pb][:, 0:TT - 1], w1, ACC[:, 1:TT], ALU.mult, ALU.add,
                                    [tPS[pb], tACC, tVL], [tACC])
                                STT(dve, ACC[:, 2:TT], PSB[pb][:, 0:TT - 2], w0, ACC[:, 2:TT], ALU.mult, ALU.add,
                                    [tPS[pb], tACC, tVL], [tACC])
                                STT(dve, ACC[:, 0:2], TAIL[:, tl, 0:2], w0, ACC[:, 0:2], ALU.mult, ALU.add,
                                    [tTAIL, tACC, tVL], [tACC])
                                STT(dve, ACC[:, 0:1], TAIL[:, tl, 1:2], w1, ACC[:, 0:1], ALU.mult, ALU.add,
                                    [tTAIL, tACC, tVL], [tACC])
                                CP(act, TAILn[:, tl, 0:2], PSB[pb][:, TT - 2:TT], [tPS[pb]], [tTAILn])
                            ACT(SG[bi][:, :], ACG[bi][:, :], AF.Silu, [tACG[bi]], [tSG[bi]])
                            TTn(pool, AT[:, jj, :], SG[bi][:, :], ACV[bi][:, :], ALU.mult, [tSG[bi], tACV[bi]], [tAT])
                        for m in range(8):
                            pb = 4 + (m % 2)
                            for jj in range(11):
                                MM(PSB[pb][:, :], WD[:, jj, m * 128:(m + 1) * 128], AT[:, jj, :], jj == 0, jj == 10,
                                   [tWs[2], tAT], [tPS[pb]])
                            TTn(dve, HT[:, m, :], PSB[pb][:, :], HT[:, m, :], ALU.add, [tPS[pb], tHT], [tHT])
                        if half == 0:
                            k.dma(sp, hv_a[:, :, cs], HT[:], [tHT], [tH[t]], d_h[t % nhb])
                            continue
                        rmsnorm([HT[:, c, :] for c in range(8)], [128] * 8,
                                [VL[:, C_GPLE + c:C_GPLE + c + 1] for c in range(8)], 1024.0,
                                [XN3[:, c, :] for c in range(8)], SQ, tSQ, RS, tRS, RSTD, tRSTD, 7, [tHT], [tXN3])
                        for m in range(8):
                            pg, pp = (4, 5) if m % 2 == 0 else (6, 7)
                            for c in range(8):
                                MM(PSB[pg][:, :], WG[:, c, m * 128:(m + 1) * 128], XN3[:, c, :], c == 0, c == 7,
                                   [tWs[3], tXN3], [tPS[pg]])
                            for c in range(2):
                                MM(PSB[pp][:, :], WP[:, c, m * 128:(m + 1) * 128], PTb[i][:, c, :], c == 0, c == 1,
                                   [tWs[4], tPTb[i]], [tPS[pp]])
                            ACT(SGT[:, :], PSB[pg][:, :], AF.Sigmoid, [tPS[pg]], [tSGT])
                            TTn(dve, TMP[:, :], PSB[pp][:, :], SGT[:, :], ALU.mult, [tPS[pp], tSGT], [tTMP])
                            TTn(pool, HT[:, m, :], HT[:, m, :], TMP[:, :], ALU.add, [tHT, tTMP], [tHT])
                        if last:
                            rmsnorm([HT[:, c, :] for c in range(8)], [128] * 8,
                                    [VL[:, C_GFIN + c:C_GFIN + c + 1] for c in range(8)], 1024.0,
                                    [HT[:, c, :] for c in range(8)], SQ, tSQ, RS, tRS, RSTD, tRSTD, 7, [tHT], [tHT])
                            k.dma(sp, hv_o[:, :, cs], HT[:], [tHT], [tNONE], d_h[t % nhb])
                        else:
                            k.dma(sp, hv_a[:, :, cs], HT[:], [tHT], [tH[t]], d_h[t % nhb])
                    k.barrier()
                    k.release_dsems(keep_ds)
                if stop_after == f"p{7 + half}_{l}":
                    return nc
        k.barrier()
    return nc


def _vec_table(inp, l):
    V = np.zeros((128, NV), np.float32)

    def chunks(v, n, c0, rows=128):
        v = np.asarray(v, np.float32).reshape(n, rows)
        V[:rows, c0:c0 + n] = v.T

    chunks(inp["g_mix"][l], 8, C_GMIX)
    chunks(inp["g_ffn"][l], 8, C_GFFN)
    chunks(inp["g_ple"][l], 8, C_GPLE)
    gq = np.asarray(inp["g_qc"][l], np.float32)
    V[:, C_GQC] = gq[0:128]
    V[0:64, C_GQC + 1] = gq[128:192]
    V[:, C_GKVC] = np.asarray(inp["g_kvc"][l], np.float32)
    chunks(inp["g_out"][l], 16, C_GOUT64, rows=64)
    chunks(np.asarray(inp["g_out"][l])[512:], 4, C_GOUTL)
    V[0:4, C_BF] = np.asarray(inp["b_f"][l], np.float32)
    lcw = np.asarray(inp["lru_conv_w"][l], np.float32)
    for kk in range(4):
        chunks(lcw[kk], 4, C_LCW + kk * 4)
    chunks(inp["lru_conv_b"][l], 4, C_LCB)
    chunks(inp["b_r"][l], 4, C_BR)
    chunks(inp["b_i"][l], 4, C_BI)
    chunks(inp["lru_lambda"][l], 4, C_LAM)
    fcw = np.asarray(inp["ffn_conv_w"][l], np.float32)
    for kk in range(3):
        chunks(fcw[kk], 44, C_FCW + kk * 44)
    chunks(inp["ffn_conv_b"][l], 44, C_FCB)
    chunks(inp["g_final"], 8, C_GFIN)
    return V


def _consts():
    ident = np.eye(128, dtype=np.float32)
    kk = np.arange(128)[:, None]
    qq = np.arange(128)[None, :]
    cmask = np.where(kk > qq, -30000.0, 0.0).astype(np.float32)
    sel = np.zeros((128, 8, 70), np.float32)
    for h in range(4):
        sel[96, h, 64:67] = 8.0
        sel[0 + h, h, 67] = 8.0
        sel[32 + h, h, 68] = 8.0
        sel[64 + h, h, 69] = 8.0
        sel[0 + h, 4 + h, 64] = -1.0
        sel[32 + h, 4 + h, 65] = -1.0
        sel[64 + h, 4 + h, 66] = -1.0
        sel[96, 4 + h, 67:70] = 1.0
    freq = np.zeros((128, 2), np.float32)
    half = 16
    f = (np.float32(10000.0) ** (-np.arange(half, dtype=np.float32) / np.float32(half))).astype(np.float32)
    freq[64:80, 0] = f
    freq[80:96, 0] = f
    freq[64:80, 1] = -2.0 * math.pi
    freq[80:96, 1] = 2.0 * math.pi
    return ident, cmask, sel, freq


_NC_CACHE = {}


def make_in_maps(inputs, cores):
    f32 = lambda a: np.ascontiguousarray(np.asarray(a), dtype=np.float32)
    ident, cmask, sel, freq = _consts()
    vec = np.stack([_vec_table(inputs, l) for l in range(DEPTH)])
    shared = {
        "w_in": f32(inputs["w_in"]), "w_uq": f32(inputs["w_uq"]), "w_ukv": f32(inputs["w_ukv"]),
        "w_r": f32(inputs["w_r"]), "w_i": f32(inputs["w_i"]), "w_o": f32(inputs["w_o"]),
        "w_up": f32(inputs["w_up"]), "w_down": f32(inputs["w_down"]),
        "w_ple_gate": f32(inputs["w_ple_gate"]), "w_ple_proj": f32(inputs["w_ple_proj"]),
        "vecs": vec, "cident": ident, "cmask": cmask, "csel": sel, "cfreq": freq,
    }
    x = np.asarray(inputs["x"])
    p = np.asarray(inputs["p"])
    pos = np.asarray(inputs["positions"])
    maps = []
    for b in cores:
        m = dict(shared)
        m["xT"] = np.ascontiguousarray(x[b].T, dtype=np.float32)
        m["pT"] = np.ascontiguousarray(np.transpose(p[:, b], (0, 2, 1)), dtype=np.float32)
        m["pos"] = np.ascontiguousarray(np.broadcast_to(pos[b].astype(np.int32)[None, :], (128, S)))
        maps.append(m)
    return maps


def kernel(**inputs):
    if "nc" not in _NC_CACHE:
        _NC_CACHE["nc"] = build_nc()
    nc = _NC_CACHE["nc"]
    B = np.asarray(inputs["x"]).shape[0]
    maps = make_in_maps(inputs, list(range(B)))
    res = run_bass_kernel_spmd(nc, maps, core_ids=list(range(B)))
    out = np.stack([np.ascontiguousarray(r["outT"].T) for r in res.results], axis=0)
    return out.astype(np.float32)
```

```python
import contextlib
import math
import numpy as np
import concourse.bass as bass
import concourse.mybir as mybir
from concourse.bass_utils import run_bass_kernel_spmd

F32 = mybir.dt.float32
BF16 = mybir.dt.bfloat16
I32 = mybir.dt.int32
AF = mybir.ActivationFunctionType
ALU = mybir.AluOpType

S = 4096
D = 1024
TT = 512
NT = S // TT
DEPTH = 2
EPS = 1e-6
NV = 272

C_GMIX, C_GFFN, C_GPLE = 0, 8, 16
C_GQC, C_GKVC = 24, 26
C_GOUT64 = 27
C_GOUTL = 43
C_BF = 47
C_LCW = 48
C_LCB, C_BR, C_BI, C_LAM = 64, 68, 72, 76
C_FCW = 80
C_FCB = 212
C_GFIN = 256


class _Tile:
    __slots__ = ("name", "w", "r")

    def __init__(self, name):
        self.name = name
        self.w = None
        self.r = {}


class _Eng:
    def __init__(self, name, h, k, is_pe=False):
        self.name = name
        self.h = h
        self.k = k
        self.is_pe = is_pe
        self.cnt = 0
        self.waited = {}


class _DSem:
    def __init__(self, k):
        self.k = k
        self.cnt = 0


class Ctx:
    def __init__(self, nc, stack):
        self.nc = nc
        self.stack = stack
        self.sems = []
        self.dsems = []
        self.free_dsems = []
        self.live_dsems = []
        self.pe = _Eng("pe", nc.tensor, self._sem("s_pe"), True)
        self.act = _Eng("act", nc.scalar, self._sem("s_act"))
        self.dve = _Eng("dve", nc.vector, self._sem("s_dve"))
        self.pool = _Eng("pool", nc.gpsimd, self._sem("s_pool"))
        self.sp = _Eng("sp", nc.sync, self._sem("s_sp"))
        self.engs = [self.pe, self.act, self.dve, self.pool, self.sp]

    def _sem(self, name):
        name = f"{name}_{len(self.sems)}"
        h = self.stack.enter_context(self.nc.semaphore(name))
        self.sems.append(h)
        return len(self.sems) - 1

    def dsem(self, name):
        if self.free_dsems:
            d = self.free_dsems.pop()
        else:
            d = _DSem(self._sem("d_" + name))
            self.dsems.append(d)
        self.live_dsems.append(d)
        return d

    def release_dsems(self, keep=()):
        keepk = {d.k for d in keep}
        for d in self.live_dsems:
            if d.k not in keepk:
                self.free_dsems.append(d)
        self.live_dsems = [d for d in self.live_dsems if d.k in keepk]

    def _add(self, need, eng, st, kind):
        if st is None:
            return
        k, val, src = st
        if src is eng:
            if eng.is_pe or kind != "raw":
                return
        if need.get(k, 0) < val:
            need[k] = val

    def _emit_waits(self, eng, reads, writes):
        need = {}
        for t in reads:
            self._add(need, eng, t.w, "raw")
        for t in writes:
            self._add(need, eng, t.w, "waw")
            for st in t.r.values():
                self._add(need, eng, st, "war")
        for k, v in need.items():
            if eng.waited.get(k, 0) < v:
                eng.h.wait_ge(self.sems[k], v)
                eng.waited[k] = v

    def _stamp(self, st, reads, writes):
        for t in reads:
            old = t.r.get(st[0])
            if old is None or old[1] < st[1]:
                t.r[st[0]] = st
        for t in writes:
            t.w = st
            t.r = {}

    def op(self, eng, fn, reads=(), writes=(), inc=True):
        self._emit_waits(eng, reads, writes)
        ins = fn(eng.h)
        if inc:
            eng.cnt += 1
            ins.then_inc(self.sems[eng.k], 1)
            st = (eng.k, eng.cnt, eng)
        else:
            st = (eng.k, eng.cnt + 1, eng)
        self._stamp(st, reads, writes)
        return ins

    def dma(self, q, out, in_, reads, writes, ds):
        need = {}
        for t in reads:
            if t.w is not None:
                k, val, _ = t.w
                need[k] = max(need.get(k, 0), val)
        for t in writes:
            if t.w is not None:
                k, val, _ = t.w
                need[k] = max(need.get(k, 0), val)
            for (k, val, _) in t.r.values():
                need[k] = max(need.get(k, 0), val)
        for k, v in need.items():
            if q.waited.get(k, 0) < v:
                q.h.wait_ge(self.sems[k], v)
                q.waited[k] = v
        ins = q.h.dma_start(out=out, in_=in_)
        ds.cnt += 16
        ins.then_inc(self.sems[ds.k], 16)
        self._stamp((ds.k, ds.cnt, None), reads, writes)
        return ins

    def barrier(self):
        for e in self.engs:
            for f in self.engs:
                if f is not e and f.cnt > 0 and e.waited.get(f.k, 0) < f.cnt:
                    e.h.wait_ge(self.sems[f.k], f.cnt)
                    e.waited[f.k] = f.cnt
            for d in self.dsems:
                if d.cnt > 0 and e.waited.get(d.k, 0) < d.cnt:
                    e.h.wait_ge(self.sems[d.k], d.cnt)
                    e.waited[d.k] = d.cnt


def build_nc(stop_after=None, debug=False):
    nc = bass.Bass("TRN2", target_bir_lowering=False)
    EI = dict(kind="ExternalInput")
    xT = nc.dram_tensor("xT", [D, S], F32, **EI).ap()
    pT = nc.dram_tensor("pT", [DEPTH, 256, S], F32, **EI).ap()
    posd = nc.dram_tensor("pos", [128, S], I32, **EI).ap()
    w_in = nc.dram_tensor("w_in", [DEPTH, D, 2148], F32, **EI).ap()
    w_uq = nc.dram_tensor("w_uq", [DEPTH, 192, 384], F32, **EI).ap()
    w_ukv = nc.dram_tensor("w_ukv", [DEPTH, 128, 512], F32, **EI).ap()
    w_r = nc.dram_tensor("w_r", [DEPTH, 8, 64, 64], F32, **EI).ap()
    w_i = nc.dram_tensor("w_i", [DEPTH, 8, 64, 64], F32, **EI).ap()
    w_o = nc.dram_tensor("w_o", [DEPTH, D, D], F32, **EI).ap()
    w_up = nc.dram_tensor("w_up", [DEPTH, D, 5632], F32, **EI).ap()
    w_down = nc.dram_tensor("w_down", [DEPTH, 2816, D], F32, **EI).ap()
    w_pg = nc.dram_tensor("w_ple_gate", [DEPTH, D, D], F32, **EI).ap()
    w_pp = nc.dram_tensor("w_ple_proj", [DEPTH, 256, D], F32, **EI).ap()
    vecs = nc.dram_tensor("vecs", [DEPTH, 128, NV], F32, **EI).ap()
    cident = nc.dram_tensor("cident", [128, 128], F32, **EI).ap()
    cmaskd = nc.dram_tensor("cmask", [128, 128], F32, **EI).ap()
    cseld = nc.dram_tensor("csel", [128, 8, 70], F32, **EI).ap()
    cfreq = nc.dram_tensor("cfreq", [128, 2], F32, **EI).ap()
    outT = nc.dram_tensor("outT", [D, S], F32, kind="ExternalOutput").ap()
    dk = dict(kind="ExternalOutput") if debug else {}
    hA = nc.dram_tensor("hA", [D, S], F32, **dk).ap()
    xnT = nc.dram_tensor("xnT", [D, S], BF16, **dk).ap()
    omixT = nc.dram_tensor("omixT", [D, S], BF16, **dk).ap()
    ropeT = nc.dram_tensor("ropeT", [2, 128, S], F32, **dk).ap()

    hv_x = xT.rearrange("(c p) t -> p c t", p=128)
    hv_a = hA.rearrange("(c p) t -> p c t", p=128)
    hv_o = outT.rearrange("(c p) t -> p c t", p=128)
    xnv = xnT.rearrange("(c p) t -> p c t", p=128)
    omv = omixT.rearrange("(c p) t -> p c t", p=128)

    with contextlib.ExitStack() as gstack:
        k = Ctx(nc, gstack)
        pe, act, dve, pool, sp = k.pe, k.act, k.dve, k.pool, k.sp

        uid = [0]

        def sb(stack, name, shape, dt):
            uid[0] += 1
            return stack.enter_context(nc.sbuf_tensor(f"{name}_{uid[0]}", shape, dt))

        def MM(out, lhsT, rhs, start, stop, reads, writes, inc=None):
            if inc is None:
                inc = stop
            k.op(pe, lambda e: e.matmul(out, lhsT=lhsT, rhs=rhs, start=start, stop=stop),
                 reads, writes, inc)

        def ACT(out, in_, func, reads, writes, bias=None, scale=None, eng=None):
            kw = {}
            if bias is not None:
                kw["bias"] = bias
            if scale is not None:
                kw["scale"] = scale
            k.op(act, lambda e: e.activation(out=out, in_=in_, func=func, **kw), reads, writes)

        def TS(eng, out, in0, s1, s2, op0, op1, reads, writes):
            if op1 is None:
                k.op(eng, lambda e: e.tensor_scalar(out=out, in0=in0, scalar1=s1, scalar2=None, op0=op0),
                     reads, writes)
            else:
                k.op(eng, lambda e: e.tensor_scalar(out=out, in0=in0, scalar1=s1, scalar2=s2, op0=op0, op1=op1),
                     reads, writes)

        def STT(eng, out, in0, scalar, in1, op0, op1, reads, writes):
            k.op(eng, lambda e: e.scalar_tensor_tensor(out=out, in0=in0, scalar=scalar, in1=in1, op0=op0, op1=op1),
                 reads, writes)

        def TTn(eng, out, in0, in1, op, reads, writes):
            k.op(eng, lambda e: e.tensor_tensor(out=out, in0=in0, in1=in1, op=op), reads, writes)

        def CP(eng, out, in_, reads, writes):
            if eng is act:
                k.op(act, lambda e: e.activation(out=out, in_=in_, func=AF.Copy), reads, writes)
            else:
                k.op(eng, lambda e: e.tensor_copy(out=out, in_=in_), reads, writes)

        def MEMSET(eng, ap, val, writes):
            k.op(eng, lambda e: e.memset(ap, val), (), writes)

        def RECIP(out, in_, reads, writes):
            k.op(dve, lambda e: e.reciprocal(out=out, in_=in_), reads, writes)

        ONES = sb(gstack, "ONES", [128, 128], BF16)
        IDENT = sb(gstack, "IDENT", [128, 128], BF16)
        CMASK = sb(gstack, "CMASK", [128, 128], BF16)
        CST = sb(gstack, "CST", [128, 4], F32)
        VL = sb(gstack, "VL", [128, NV], F32)
        tONES, tIDENT, tCMASK, tCST, tVL = (_Tile(n) for n in ("ONES", "IDENT", "CMASK", "CST", "VL"))
        PSALL = gstack.enter_context(nc.psum_tensor("psall", [128, 8, 512], F32))
        PSB = [PSALL[:, i, :] for i in range(8)]
        tPS = [_Tile(f"ps{i}") for i in range(8)]
        d_const = k.dsem("const")
        d_const2 = k.dsem("const2")
        d_vl = k.dsem("vl")
        d_w = [k.dsem(f"w{i}") for i in range(12)]
        keep_ds = [d_const, d_const2, d_vl] + d_w

        MEMSET(dve, ONES[:], 1.0, [tONES])
        MEMSET(dve, CST[:, 0:1], EPS, [tCST])
        MEMSET(dve, CST[:, 1:2], 1.0, [tCST])
        MEMSET(dve, CST[:, 2:3], 0.0, [tCST])
        k.dma(pool, IDENT[:], cident[:, :], [], [tIDENT], d_const)
        k.dma(pool, CMASK[:], cmaskd[:, :], [], [tCMASK], d_const2)

        tH = [_Tile(f"h{t}") for t in range(NT)]
        tXN = [_Tile(f"xn{t}") for t in range(NT)]
        tOM = [[_Tile(f"om{g}_{t}") for t in range(NT)] for g in range(3)]
        tROPE = _Tile("rope")
        tNONE = _Tile("ext")

        def cols(t):
            return slice(t * TT, (t + 1) * TT)

        def rmsnorm(srcs, rows, gcols, nfeat, outs, SQ, tSQ, RS, tRS, RSTD, tRSTD, psn, src_tiles, out_tiles,
                    sq_eng=None):
            C = len(srcs)
            if sq_eng is not None:
                ACT(sq_eng[1], sq_eng[0], AF.Square, src_tiles, [tSQ])
            else:
                for c in range(C):
                    ACT(SQ[0:rows[c], c, :], srcs[c], AF.Square, src_tiles, [tSQ])
            for c in range(C):
                MM(PSB[psn][:, :], ONES[0:rows[c], :], SQ[0:rows[c], c, :], c == 0, c == C - 1,
                   [tSQ, tONES], [tPS[psn]])
            ACT(RS[:, :], PSB[psn][:, :], AF.Sqrt, [tPS[psn], tCST], [tRS], bias=CST[:, 0:1], scale=1.0 / nfeat)
            RECIP(RSTD[:, :], RS[:, :], [tRS], [tRSTD])
            for c in range(C):
                STT(dve, outs[c], srcs[c], gcols[c], RSTD[0:rows[c], :], ALU.mult, ALU.mult,
                    src_tiles + [tRSTD, tVL], out_tiles)

        with contextlib.ExitStack() as st:
            POSI = sb(st, "POSI", [128, S], I32)
            POSF = sb(st, "POSF", [128, S], F32)
            ANG = sb(st, "ANG", [128, S], F32)
            YT = sb(st, "YT", [128, S], F32)
            FRQ = sb(st, "FRQ", [128, 2], F32)
            CSB = sb(st, "CSB", [128, S], F32)
            CSB2 = sb(st, "CSB2", [128, S], F32)
            tCSB, tCSB2 = _Tile("CSB"), _Tile("CSB2")
            tPOSI, tPOSF, tANG, tYT, tFRQ = (_Tile(n) for n in ("POSI", "POSF", "ANG", "YT", "FRQ"))
            d_p = k.dsem("pro")
            R = slice(64, 96)
            k.dma(sp, POSI[R, :], posd[R, :], [], [tPOSI], d_p)
            d_p2 = k.dsem("pro2")
            k.dma(sp, FRQ[:], cfreq[:, :], [], [tFRQ], d_p2)
            CP(dve, POSF[R, :], POSI[R, :], [tPOSI], [tPOSF])
            TS(dve, ANG[R, :], POSF[R, :], FRQ[R, 0:1], None, ALU.mult, None, [tPOSF, tFRQ], [tANG])
            TS(dve, ANG[R, :], ANG[R, :], 1.0 / (2 * math.pi), None, ALU.mult, None, [tANG], [tANG])

            def reduce_turns(shift):
                TS(dve, YT[R, :], ANG[R, :], shift, None, ALU.add, None, [tANG], [tYT])
                CP(dve, POSI[R, :], YT[R, :], [tYT], [tPOSI])
                CP(dve, POSF[R, :], POSI[R, :], [tPOSI], [tPOSF])
                TTn(dve, YT[R, :], YT[R, :], POSF[R, :], ALU.subtract, [tYT, tPOSF], [tYT])
                TS(dve, POSF[R, :], YT[R, :], 0.5, None, ALU.is_gt, None, [tYT], [tPOSF])
                TTn(dve, YT[R, :], YT[R, :], POSF[R, :], ALU.subtract, [tYT, tPOSF], [tYT])
                TS(dve, POSF[R, :], YT[R, :], -0.5, None, ALU.is_lt, None, [tYT], [tPOSF])
                TTn(dve, YT[R, :], YT[R, :], POSF[R, :], ALU.add, [tYT, tPOSF], [tYT])

            reduce_turns(0.25)
            ACT(CSB[R, :], YT[R, :], AF.Sin, [tYT], [tCSB], scale=2 * math.pi)
            k.dma(sp, ropeT[0, R, :], CSB[R, :], [tCSB], [tROPE], d_p)
            reduce_turns(0.0)
            ACT(CSB2[R, :], YT[R, :], AF.Sin, [tYT, tFRQ], [tCSB2], scale=FRQ[R, 1:2])
            k.dma(sp, ropeT[1, R, :], CSB2[R, :], [tCSB2], [tROPE], d_p)
            k.barrier()
            k.release_dsems(keep_ds)
        if stop_after == "rope":
            k.barrier()
            return nc

        def attention(l, QT, tQT, KT, tKT, VA, tVA, kd, scale, grp, gcol0):
            with contextlib.ExitStack() as st:
                PT = [sb(st, f"PT{i}", [128, 512], BF16) for i in range(3)]
                tPT = [_Tile(f"PT{i}") for i in range(3)]
                RC = sb(st, "RC", [128, 512], F32)
                OHb = [sb(st, f"OH{i}", [128, 4, 512], F32) for i in range(2)]
                tOHb = [_Tile("OH0"), _Tile("OH1")]
                SQ = sb(st, "SQa", [128, 4, 512], BF16)
                RS = sb(st, "RSa", [128, 512], F32)
                RSTD = sb(st, "RSTDa", [128, 512], F32)
                OMX = [sb(st, f"OMXa{i}", [128, 4, 512], BF16) for i in range(2)]
                tRC, tSQ, tRS, tRSTD = (_Tile(n) for n in ("RC", "SQa", "RSa", "RSTDa"))
                tOMX = [_Tile("OMXa0"), _Tile("OMXa1")]
                d_st = [k.dsem("ast0"), k.dsem("ast1")]
                tiles = [(j, h, i) for j in range(NT) for h in range(4) for i in range(4 * j + 4)]
                LOOK = 2
                deferred = []

                def emit_score(idx):
                    j, h, i = tiles[idx]
                    r = i - 4 * j
                    c0 = 128 * r if r >= 0 else 0
                    pb = idx % 3
                    ksl = slice(i * 128, (i + 1) * 128)
                    q0 = j * TT
                    if r >= 0:
                        MM(PSB[pb][:, c0:c0 + 128], KT[0:kd, h, ksl], QT[0:kd, h, q0 + c0:q0 + c0 + 128],
                           True, False, [tKT, tQT], [tPS[pb]], inc=False)
                        MM(PSB[pb][:, c0:c0 + 128], IDENT[:, :], CMASK[:, :], False, True,
                           [tIDENT, tCMASK], [tPS[pb]], inc=(c0 + 128 == 512))
                        if c0 + 128 < 512:
                            MM(PSB[pb][:, c0 + 128:512], KT[0:kd, h, ksl], QT[0:kd, h, q0 + c0 + 128:q0 + 512],
                               True, True, [tKT, tQT], [tPS[pb]])
                    else:
                        MM(PSB[pb][:, :], KT[0:kd, h, ksl], QT[0:kd, h, q0:q0 + 512], True, True,
                           [tKT, tQT], [tPS[pb]])

                def emit_norm(j):
                    om = OMX[j % 2]
                    tom = tOMX[j % 2]
                    oh, toh = OHb[j % 2], tOHb[j % 2]
                    rmsnorm([oh[0:64, h, :] for h in range(4)], [64] * 4,
                            [VL[0:64, C_GOUT64 + gcol0 + h:C_GOUT64 + gcol0 + h + 1] for h in range(4)], 256.0,
                            [om[0:64, h, :] for h in range(4)], SQ, tSQ, RS, tRS, RSTD, tRSTD, 5, [toh], [tom])
                    r0 = 256 * grp
                    k.dma(sp, omixT[r0:r0 + 256, cols(j)].rearrange("(h p) t -> p h t", p=64), om[0:64, :, :],
                          [tom], [tOM[grp][j]], d_st[j % 2])

                def emit_exp_pv(idx, step):
                    j, h, i = tiles[idx]
                    nk = 4 * j + 4
                    r = i - 4 * j
                    c0 = 128 * r if r >= 0 else 0
                    pb = idx % 3
                    ob = 3 + (h % 2)
                    ACT(PT[pb][:, c0:512], PSB[pb][:, c0:512], AF.Exp, [tPS[pb]], [tPT[pb]], scale=scale)
                    MM(PSB[ob][:, c0:512], VA[:, i, h, :], PT[pb][:, c0:512], i == 0, i == nk - 1,
                       [tVA, tPT[pb]], [tPS[ob]])
                    if i == nk - 1:
                        oh, toh = OHb[j % 2], tOHb[j % 2]
                        RECIP(RC[64:128, :], PSB[ob][64:128, :], [tPS[ob]], [tRC])
                        TTn(dve, oh[0:64, h, :], PSB[ob][0:64, :], RC[64:128, :], ALU.mult, [tPS[ob], tRC], [toh])
                        if h == 3:
                            deferred.append((step + 6, lambda jj=j: emit_norm(jj)))

                nt_ = len(tiles)
                for step in range(nt_ + LOOK):
                    if step < nt_:
                        emit_score(step)
                    if step >= LOOK:
                        emit_exp_pv(step - LOOK, step)
                    while deferred and deferred[0][0] <= step:
                        deferred.pop(0)[1]()
                while deferred:
                    deferred.pop(0)[1]()
                k.barrier()
                k.release_dsems(keep_ds)

        for l in range(DEPTH):
            hsrc = hv_x if l == 0 else hv_a
            tHs = [tNONE] * NT if l == 0 else tH
            k.dma(sp, VL[:], vecs[l, :, :], [], [tVL], d_vl)
            win = w_in[l].rearrange("(c p) n -> p c n", p=128)

            with contextlib.ExitStack() as st:
                QM = sb(st, "QM", [128, 4, S], BF16)
                KM = sb(st, "KM", [128, 4, S], BF16)
                VA = sb(st, "VA", [128, 32, 4, 128], BF16)
                tQM, tKM, tVA = _Tile("QM"), _Tile("KM"), _Tile("VA")
                MEMSET(pool, VA[:, :, :, 64:128], 1.0, [tVA])
                with contextlib.ExitStack() as s1:
                    HTb1 = [sb(s1, f"HT1{i}", [128, 8, TT], F32) for i in range(2)]
                    tHTb1 = [_Tile("HT10"), _Tile("HT11")]
                    d_hb = [k.dsem("p1h0"), k.dsem("p1h1")]
                    XNb1 = [sb(s1, f"XN1{i}", [128, 8, TT], BF16) for i in range(2)]
                    tXNb1 = [_Tile("XN10"), _Tile("XN11")]
                    d_xsb = [k.dsem("p1xs0"), k.dsem("p1xs1")]
                    SQ = sb(s1, "SQ", [128, 8, TT], BF16)
                    SQB = sb(s1, "SQB", [128, 2, TT], BF16)
                    RSB = sb(s1, "RSB", [128, TT], F32)
                    RSTDB = sb(s1, "RSTDB", [128, TT], F32)
                    tSQB, tRSB, tRSTDB = _Tile("SQB"), _Tile("RSB"), _Tile("RSTDB")
                    RS = sb(s1, "RS", [128, TT], F32)
                    RSTD = sb(s1, "RSTD", [128, TT], F32)
                    QCN = sb(s1, "QCN", [128, 2, TT], BF16)
                    KVN = sb(s1, "KVN", [128, TT], BF16)
                    T1 = sb(s1, "T1", [128, TT], F32)
                    T2 = sb(s1, "T2", [128, TT], F32)
                    RC_ = sb(s1, "ROPEC", [128, TT], F32)
                    RS_ = sb(s1, "ROPES", [128, TT], F32)
                    WA = sb(s1, "WA", [128, 8, 320], BF16)
                    WKA = sb(s1, "WKA", [128, 8, 96], BF16)
                    WKB = sb(s1, "WKB", [128, 8, 96], BF16)
                    WUQ = sb(s1, "WUQ", [128, 2, 384], BF16)
                    WUQB = sb(s1, "WUQB", [128, 2, 4, 96], BF16)
                    WUKV = sb(s1, "WUKV", [128, 512], BF16)
                    WV = sb(s1, "WV", [128, 4, 64], BF16)
                    (tSQ, tRS, tRSTD, tQCN, tKVN, tT1, tT2, tRC_, tRS_, tW) = (
                        _Tile(n) for n in ("SQ", "RS", "RSTD", "QCN", "KVN", "T1", "T2", "RC_", "RS_", "W1"))
                    d_r, d_r2 = k.dsem("p1r"), k.dsem("p1r2")
                    MEMSET(pool, WKA[:], 0.0, [tW])
                    MEMSET(pool, WKB[:], 0.0, [tW])
                    MEMSET(pool, WUQB[:], 0.0, [tW])
                    MEMSET(pool, WUQ[:], 0.0, [tW])
                    k.dma(pool, WA[:], win[:, :, 0:320], [], [tW], d_w[0])
                    k.dma(pool, WKA[:, :, 64:96], win[:, :, 320:352], [], [tW], d_w[0])
                    k.dma(pool, WKB[:, :, 64:80], win[:, :, 336:352], [], [tW], d_w[0])
                    k.dma(pool, WKB[:, :, 80:96], win[:, :, 320:336], [], [tW], d_w[0])
                    k.dma(pool, WUQ[:, 0, :], w_uq[l, 0:128, :], [], [tW], d_w[0])
                    k.dma(pool, WUQ[0:64, 1, :], w_uq[l, 128:192, :], [], [tW], d_w[0])
                    uq4 = w_uq[l].rearrange("r (h d) -> r h d", h=4)
                    k.dma(pool, WUQB[:, 0, :, 64:80], uq4[0:128, :, 80:96], [], [tW], d_w[0])
                    k.dma(pool, WUQB[:, 0, :, 80:96], uq4[0:128, :, 64:80], [], [tW], d_w[0])
                    k.dma(pool, WUQB[0:64, 1, :, 64:80], uq4[128:192, :, 80:96], [], [tW], d_w[0])
                    k.dma(pool, WUQB[0:64, 1, :, 80:96], uq4[128:192, :, 64:80], [], [tW], d_w[0])
                    k.dma(pool, WUKV[:], w_ukv[l, :, :], [], [tW], d_w[0])
                    k.dma(pool, WV[:], w_ukv[l].rearrange("r (h d) -> r h d", h=4)[:, :, 64:128], [], [tW], d_w[0])
                    def ldh1(t):
                        k.dma(sp, HTb1[t % 2][:], hsrc[:, :, cols(t)], [tHs[t]], [tHTb1[t % 2]], d_hb[t % 2])

                    def normA(t):
                        HT, tHT = HTb1[t % 2], tHTb1[t % 2]
                        rmsnorm([HT[:, c, :] for c in range(8)], [128] * 8,
                                [VL[:, C_GMIX + c:C_GMIX + c + 1] for c in range(8)], 1024.0,
                                [XNb1[t % 2][:, c, :] for c in range(8)], SQ, tSQ, RS, tRS, RSTD, tRSTD, 7, [tHT],
                                [tXNb1[t % 2]], sq_eng=(HT[:, :, :], SQ[:, :, :]))
                        k.dma(sp, xnv[:, :, cols(t)], XNb1[t % 2][:], [tXNb1[t % 2]], [tXN[t]], d_xsb[t % 2])
                    ldh1(0)
                    ldh1(1)
                    normA(0)
                    for t in range(NT):
                        cs = cols(t)
                        XN, tXNs = XNb1[t % 2], tXNb1[t % 2]
                        if t >= 1 and t + 1 < NT:
                            ldh1(t + 1)
                        k.dma(sp, RC_[64:96, :], ropeT[0, 64:96, cs], [tROPE], [tRC_], d_r)
                        k.dma(sp, RS_[64:96, :], ropeT[1, 64:96, cs], [tROPE], [tRS_], d_r2)
                        specs = [(0, 128, WA, slice(0, 128)), (1, 64, WA, slice(128, 192)), (2, 128, WA, slice(192, 320)),
                                 (3, 96, WKA, slice(0, 96)), (4, 96, WKB, slice(0, 96))]
                        for (pb, m, Wt, csl) in specs:
                            for c in range(8):
                                MM(PSB[pb][0:m, :], Wt[:, c, csl], XN[:, c, :], c == 0, c == 7, [tW, tXNs], [tPS[pb]])
                        if t + 1 < NT:
                            normA(t + 1)
                        rmsnorm([PSB[0][:, :], PSB[1][0:64, :]], [128, 64],
                                [VL[:, C_GQC:C_GQC + 1], VL[0:64, C_GQC + 1:C_GQC + 2]], 192.0,
                                [QCN[:, 0, :], QCN[0:64, 1, :]], SQB, tSQB, RSB, tRSB, RSTDB, tRSTDB, 7,
                                [tPS[0], tPS[1]], [tQCN])
                        rmsnorm([PSB[2][:, :]], [128], [VL[:, C_GKVC:C_GKVC + 1]], 128.0, [KVN[:, :]],
                                SQB, tSQB, RSB, tRSB, RSTDB, tRSTDB, 7, [tPS[2]], [tKVN])
                        TTn(dve, T1[64:96, :], PSB[3][64:96, :], RC_[64:96, :], ALU.mult, [tPS[3], tRC_], [tT1])
                        TTn(dve, T2[64:96, :], PSB[4][64:96, :], RS_[64:96, :], ALU.mult, [tPS[4], tRS_], [tT2])
                        TTn(dve, T1[64:96, :], T1[64:96, :], T2[64:96, :], ALU.add, [tT1, tT2], [tT1])
                        for h in range(4):
                            CP(act if h % 2 == 0 else pool, KM[64:96, h, cs], T1[64:96, :], [tT1], [tKM])
                        for h in range(4):
                            pa, pb2 = (0, 1) if h % 2 == 0 else (5, 6)
                            hs = slice(h * 96, (h + 1) * 96)
                            MM(PSB[pa][0:96, :], WUQ[:, 0, hs], QCN[:, 0, :], True, False, [tW, tQCN], [tPS[pa]])
                            MM(PSB[pa][0:96, :], WUQ[0:64, 1, hs], QCN[0:64, 1, :], False, True, [tW, tQCN], [tPS[pa]])
                            MM(PSB[pb2][0:96, :], WUQB[:, 0, h, :], QCN[:, 0, :], True, False, [tW, tQCN], [tPS[pb2]])
                            MM(PSB[pb2][0:96, :], WUQB[0:64, 1, h, :], QCN[0:64, 1, :], False, True, [tW, tQCN], [tPS[pb2]])
                            CP(act, QM[0:64, h, cs], PSB[pa][0:64, :], [tPS[pa]], [tQM])
                            TTn(dve, T1[64:96, :], PSB[pa][64:96, :], RC_[64:96, :], ALU.mult, [tPS[pa], tRC_], [tT1])
                            TTn(dve, T2[64:96, :], PSB[pb2][64:96, :], RS_[64:96, :], ALU.mult, [tPS[pb2], tRS_], [tT2])
                            TTn(dve, QM[64:96, h, cs], T1[64:96, :], T2[64:96, :], ALU.add, [tT1, tT2], [tQM])
                        for h in range(4):
                            pb = 2 + (h % 2)
                            MM(PSB[pb][0:64, :], WUKV[:, h * 128:h * 128 + 64], KVN[:, :], True, True, [tW, tKVN], [tPS[pb]])
                            CP(act, KM[0:64, h, cs], PSB[pb][0:64, :], [tPS[pb]], [tKM])
                        for s_ in range(4):
                            pb = 5 + (s_ % 2)
                            MM(PSB[pb][:, 0:256], KVN[:, s_ * 128:(s_ + 1) * 128], WV[:, :, :].rearrange("p h d -> p (h d)"),
                               True, True, [tKVN, tW], [tPS[pb]])
                            CP(dve, VA[:, t * 4 + s_, :, 0:64], PSB[pb][:, 0:256].rearrange("p (h d) -> p h d", h=4),
                               [tPS[pb]], [tVA])
                    k.barrier()
                    k.release_dsems(keep_ds)
                if stop_after == f"p1_{l}":
                    return nc
                attention(l, QM, tQM, KM, tKM, VA, tVA, 96, 96.0 ** -0.5, 0, 0)
            if stop_after == f"p2_{l}":
                return nc

            with contextlib.ExitStack() as st:
                FQ = sb(st, "FQ", [128, 4, S], BF16)
                FK = sb(st, "FK", [128, 4, S], BF16)
                FVA = sb(st, "FVA", [128, 32, 4, 128], BF16)
                tFQ, tFK, tFVA = _Tile("FQ"), _Tile("FK"), _Tile("FVA")
                MEMSET(pool, FVA[:, :, :, 64:128], 1.0, [tFVA])
                with contextlib.ExitStack() as s1:
                    XNb = [sb(s1, f"XNf{i}", [128, 8, TT], BF16) for i in range(3)]
                    tXNb = [_Tile("XNf0"), _Tile("XNf1"), _Tile("XNf2")]
                    d_x = [k.dsem("p3x0"), k.dsem("p3x1"), k.dsem("p3x2")]
                    CCb = [sb(s1, f"CC{i}", [128, TT], BF16) for i in range(2)]
                    tCCb = [_Tile("CC0"), _Tile("CC1")]
                    EX = sb(s1, "EX", [4, TT], F32)
                    NL = sb(s1, "NL", [4, TT], F32)
                    ONE4 = sb(s1, "ONE4", [4, TT], F32)
                    CF = sb(s1, "CF", [4, TT], F32)
                    R1 = sb(s1, "R1", [4, TT], F32)
                    CHI = sb(s1, "CHI", [4, TT], BF16)
                    CMID = sb(s1, "CMID", [4, TT], BF16)
                    CLO = sb(s1, "CLO", [4, TT], BF16)
                    CARRY = sb(s1, "CARRY", [4, 2], F32)
                    NEGB = sb(s1, "NEGB", [4, 2], F32)
                    WFQ = sb(s1, "WFQ", [128, 8, 4, 70], BF16)
                    WFK = sb(s1, "WFK", [128, 8, 4, 70], BF16)
                    WFV = sb(s1, "WFV", [128, 8, 256], BF16)
                    WF4 = sb(s1, "WF4", [128, 8, 4], BF16)
                    SEL = sb(s1, "SEL", [128, 8, 70], BF16)
                    (tEX, tNL, tONE4, tCF, tR1, tCHI, tCMID, tCLO, tCARRY, tNEGB, tW) = (
                        _Tile(n) for n in ("EX", "NL", "ONE4", "CF", "R1", "CHI", "CMID", "CLO", "CARRY", "NEGB", "W3"))
                    MEMSET(pool, WFQ[:], 0.0, [tW])
                    MEMSET(pool, WFK[:], 0.0, [tW])
                    for i_ in range(2):
                        MEMSET(dve, CCb[i_][:], 0.0, [tCCb[i_]])
                        MEMSET(dve, CCb[i_][96:97, :], 1.0, [tCCb[i_]])
                    MEMSET(dve, ONE4[:], 1.0, [tONE4])
                    MEMSET(dve, CARRY[:], 0.0, [tCARRY])
                    TS(dve, NEGB[0:4, 0:1], VL[0:4, C_BF:C_BF + 1], -1.0, None, ALU.mult, None, [tVL], [tNEGB])
                    for c in range(8):
                        k.dma(pool, WFQ[:, c, :, 0:64], win[:, c, 352:608].rearrange("p (h d) -> p h d", h=4), [], [tW], d_w[1])
                        k.dma(pool, WFK[:, c, :, 0:64], win[:, c, 608:864].rearrange("p (h d) -> p h d", h=4), [], [tW], d_w[1])
                    k.dma(pool, WFV[:], win[:, :, 864:1120], [], [tW], d_w[1])
                    k.dma(pool, WF4[:], win[:, :, 1120:1124], [], [tW], d_w[1])
                    k.dma(pool, SEL[:], cseld[:, :, :], [], [tW], d_w[1])
                    def ld3(t):
                        k.dma(sp, XNb[t % 3][:], xnv[:, :, cols(t)], [tXN[t]], [tXNb[t % 3]], d_x[t % 3])

                    def cchain(t):
                        XN, tXNs = XNb[t % 3], tXNb[t % 3]
                        CC, tCC = CCb[t % 2], tCCb[t % 2]
                        for c in range(8):
                            MM(PSB[7][0:4, :], WF4[:, c, :], XN[:, c, :], c == 0, c == 7, [tW, tXNs], [tPS[7]])
                        ACT(EX[:, :], PSB[7][0:4, :], AF.Exp, [tPS[7], tNEGB], [tEX], bias=NEGB[0:4, 0:1], scale=-1.0)
                        ACT(NL[:, :], EX[:, :], AF.Ln, [tEX, tCST], [tNL], bias=CST[0:4, 1:2], scale=1.0)
                        TS(dve, NL[:, :], NL[:, :], -1.0, None, ALU.mult, None, [tNL], [tNL])
                        k.op(dve, lambda e: e.tensor_tensor_scan(out=CF[:, :], data0=ONE4[:, :], data1=NL[:, :],
                                                                 initial=CARRY[0:4, 0:1], op0=ALU.mult, op1=ALU.add),
                             [tONE4, tNL, tCARRY], [tCF])
                        CP(dve, CARRY[0:4, 0:1], CF[:, TT - 1:TT], [tCF], [tCARRY])
                        CP(dve, CHI[:, :], CF[:, :], [tCF], [tCHI])
                        TTn(dve, R1[:, :], CF[:, :], CHI[:, :], ALU.subtract, [tCF, tCHI], [tR1])
                        CP(dve, CMID[:, :], R1[:, :], [tR1], [tCMID])
                        TTn(dve, R1[:, :], R1[:, :], CMID[:, :], ALU.subtract, [tR1, tCMID], [tR1])
                        CP(dve, CLO[:, :], R1[:, :], [tR1], [tCLO])
                        CP(pool, CC[0:4, :], CHI[:, :], [tCHI], [tCC])
                        CP(pool, CC[32:36, :], CMID[:, :], [tCMID], [tCC])
                        CP(pool, CC[64:68, :], CLO[:, :], [tCLO], [tCC])
                    ld3(0)
                    ld3(1)
                    cchain(0)
                    for t in range(NT):
                        cs = cols(t)
                        XN, tXNs = XNb[t % 3], tXNb[t % 3]
                        CC, tCC = CCb[t % 2], tCCb[t % 2]
                        if t + 2 < NT:
                            ld3(t + 2)
                        if t + 1 < NT:
                            cchain(t + 1)
                        for h in range(4):
                            for (qi, Wt, DST, tD) in ((0, WFQ, FQ, tFQ), (1, WFK, FK, tFK)):
                                pb = (2 * h + qi) % 4
                                for c in range(8):
                                    MM(PSB[pb][0:70, :], Wt[:, c, h, :], XN[:, c, :], c == 0, False, [tW, tXNs], [tPS[pb]])
                                MM(PSB[pb][0:70, :], SEL[:, qi * 4 + h, :], CC[:, :], False, True, [tW, tCC], [tPS[pb]])
                                CP(act if qi == 0 else dve, DST[0:70, h, cs], PSB[pb][0:70, :], [tPS[pb]], [tD])
                        for s_ in range(4):
                            pb = 4 + (s_ % 2)
                            for c in range(8):
                                MM(PSB[pb][:, 0:256], XN[:, c, s_ * 128:(s_ + 1) * 128], WFV[:, c, :], c == 0, c == 7,
                                   [tXNs, tW], [tPS[pb]])
                            CP(act if s_ % 2 else dve, FVA[:, t * 4 + s_, :, 0:64],
                               PSB[pb][:, 0:256].rearrange("p (h d) -> p h d", h=4), [tPS[pb]], [tFVA])
                    k.barrier()
                    k.release_dsems(keep_ds)
                if stop_after == f"p3_{l}":
                    return nc
                attention(l, FQ, tFQ, FK, tFK, FVA, tFVA, 70, 0.125, 1, 4)
            if stop_after == f"p4_{l}":
                return nc

            with contextlib.ExitStack() as s1:
                XNb = [sb(s1, f"XNl{i}", [128, 8, TT], BF16) for i in range(2)]
                tXNb = [_Tile("XNl0"), _Tile("XNl1")]
                d_x = [k.dsem("p5x0"), k.dsem("p5x1")]
                names5 = ("LXB", "XC", "XCB", "RG", "IG", "AG", "A2", "HS", "SQl", "RSl", "RSTDl")
                shapes5 = {"LXB": ([128, 4, TT + 3], F32), "XC": ([128, 4, TT], F32), "XCB": ([128, 4, TT], BF16),
                           "RG": ([128, 4, TT], F32), "IG": ([128, 4, TT], F32), "AG": ([128, 4, TT], F32),
                           "A2": ([128, 4, TT], F32), "HS": ([128, 4, TT], F32), "SQl": ([128, 4, TT], BF16),
                           "RSl": ([128, TT], F32), "RSTDl": ([128, TT], F32)}
                B5 = [{n: sb(s1, f"{n}{p}", shapes5[n][0], shapes5[n][1]) for n in names5} for p in range(2)]
                T5 = [{n: _Tile(f"{n}{p}") for n in names5} for p in range(2)]
                tXC5 = [[_Tile(f"XC{p}_{c}") for c in range(4)] for p in range(2)]
                tXCB5 = [[_Tile(f"XCB{p}_{c}") for c in range(4)] for p in range(2)]
                OMX = [sb(s1, f"OMXl{i}", [128, 4, TT], BF16) for i in range(2)]
                tOMX = [_Tile("OMXl0"), _Tile("OMXl1")]
                d_st = [k.dsem("lst0"), k.dsem("lst1")]
                CARH = sb(s1, "CARH", [128, 4], F32)
                COEF = sb(s1, "COEF", [128, 8], F32)
                WL = sb(s1, "WL", [128, 8, 1024], BF16)
                WRB = sb(s1, "WRB", [128, 4, 128], BF16)
                WIB = sb(s1, "WIB", [128, 4, 128], BF16)
                tCARH, tCOEF, tW = _Tile("CARH"), _Tile("COEF"), _Tile("W5")
                MEMSET(pool, WRB[:], 0.0, [tW])
                MEMSET(pool, WIB[:], 0.0, [tW])
                MEMSET(dve, B5[0]["LXB"][:], 0.0, [T5[0]["LXB"]])
                MEMSET(dve, B5[1]["LXB"][:], 0.0, [T5[1]["LXB"]])
                MEMSET(dve, CARH[:], 0.0, [tCARH])
                ACT(COEF[:, 0:4], VL[:, C_LAM:C_LAM + 4], AF.Exp, [tVL], [tCOEF], scale=-1.0)
                ACT(COEF[:, 0:4], COEF[:, 0:4], AF.Ln, [tCOEF, tCST], [tCOEF], bias=CST[:, 1:2], scale=1.0)
                TS(dve, COEF[:, 4:8], COEF[:, 0:4], -16.0, None, ALU.mult, None, [tCOEF], [tCOEF])
                TS(dve, COEF[:, 0:4], COEF[:, 0:4], -8.0, None, ALU.mult, None, [tCOEF], [tCOEF])
                k.dma(pool, WL[:], win[:, :, 1124:2148], [], [tW], d_w[2])
                for c in range(4):
                    for hb in range(2):
                        ps_ = slice(hb * 64, hb * 64 + 64)
                        k.dma(pool, WRB[ps_, c, ps_], w_r[l, 2 * c + hb, :, :], [], [tW], d_w[2])
                        k.dma(pool, WIB[ps_, c, ps_], w_i[l, 2 * c + hb, :, :], [], [tW], d_w[2])
                def ld5(t):
                    k.dma(sp, XNb[t % 2][:], xnv[:, :, cols(t)], [tXN[t]], [tXNb[t % 2]], d_x[t % 2])

                def lru_tile(t):
                    p = t % 2
                    b0_ = 4 * p
                    cs = cols(t)
                    XN, tXNs = XNb[p], tXNb[p]
                    Bp, Tp, Bo, To = B5[p], T5[p], B5[1 - p], T5[1 - p]
                    LXB, XC, XCB, RG, IG, AG, A2, HS = (Bp[n] for n in ("LXB", "XC", "XCB", "RG", "IG", "AG", "A2", "HS"))
                    tLXB, tRG, tIG, tAG, tA2, tHS = (Tp[n] for n in ("LXB", "RG", "IG", "AG", "A2", "HS"))
                    tXCc, tXCBc = tXC5[p], tXCB5[p]
                    tPB = [tPS[b0_ + c] for c in range(4)]
                    for c in range(4):
                        for kc in range(8):
                            MM(PSB[b0_ + c][:, :], WL[:, kc, c * 128:(c + 1) * 128], XN[:, kc, :], kc == 0, kc == 7,
                               [tW, tXNs], [tPS[b0_ + c]])
                        yield
                    CP(pool, LXB[:, :, 0:3], Bo["LXB"][:, :, TT:TT + 3], [To["LXB"]], [tLXB])
                    CP(act, LXB[:, :, 3:TT + 3], PSALL[:, b0_:b0_ + 4, :], tPB, [tLXB])
                    yield
                    for c in range(4):
                        wcol = lambda kk: VL[:, C_LCW + kk * 4 + c:C_LCW + kk * 4 + c + 1]
                        ACT(XC[:, c, :], LXB[:, c, 3:TT + 3], AF.Identity, [tLXB, tVL], [tXCc[c]],
                            bias=VL[:, C_LCB + c:C_LCB + c + 1], scale=wcol(3))
                        yield
                        for kk in range(3):
                            STT(dve, XC[:, c, :], LXB[:, c, kk:kk + TT], wcol(kk), XC[:, c, :], ALU.mult, ALU.add,
                                [tLXB, tXCc[c], tVL], [tXCc[c]])
                        yield
                        CP(act, XCB[:, c, :], XC[:, c, :], [tXCc[c]], [tXCBc[c]])
                        yield
                    for c in range(4):
                        MM(PSB[b0_ + c][:, :], WRB[:, c, :], XCB[:, c, :], True, True, [tW, tXCBc[c]], [tPS[b0_ + c]])
                    yield
                    for c in range(4):
                        ACT(RG[:, c, :], PSB[b0_ + c][:, :], AF.Sigmoid, [tPS[b0_ + c], tVL], [tRG],
                            bias=VL[:, C_BR + c:C_BR + c + 1])
                    yield
                    for c in range(4):
                        MM(PSB[b0_ + c][:, :], WIB[:, c, :], XCB[:, c, :], True, True, [tW, tXCBc[c]], [tPS[b0_ + c]])
                    yield
                    for c in range(4):
                        ACT(IG[:, c, :], PSB[b0_ + c][:, :], AF.Sigmoid, [tPS[b0_ + c], tVL], [tIG],
                            bias=VL[:, C_BI + c:C_BI + c + 1])
                    yield
                    for c in range(4):
                        for kc in range(8):
                            MM(PSB[b0_ + c][:, :], WL[:, kc, 512 + c * 128:512 + (c + 1) * 128], XN[:, kc, :],
                               kc == 0, kc == 7, [tW, tXNs], [tPS[b0_ + c]])
                        yield
                    if t + 2 < NT:
                        ld5(t + 2)
                    TTn(dve, IG[:, :, :], IG[:, :, :], XC[:, :, :], ALU.mult, [tIG] + tXCc, [tIG])
                    yield
                    for c in range(4):
                        ACT(AG[:, c, :], RG[:, c, :], AF.Exp, [tRG, tCOEF], [tAG], scale=COEF[:, c:c + 1])
                    yield
                    for c in range(4):
                        ACT(A2[:, c, :], RG[:, c, :], AF.Exp, [tRG, tCOEF], [tA2], scale=COEF[:, 4 + c:5 + c])
                    yield
                    ACT(A2[:, :, :], A2[:, :, :], AF.Sqrt, [tA2, tCST], [tA2], bias=CST[:, 1:2], scale=-1.0)
                    yield
                    TTn(dve, IG[:, :, :], IG[:, :, :], A2[:, :, :], ALU.mult, [tIG, tA2], [tIG])
                    yield
                    for c in range(4):
                        k.op(dve, lambda e: e.tensor_tensor_scan(out=HS[:, c, :], data0=AG[:, c, :], data1=IG[:, c, :],
                                                                 initial=CARH[:, c:c + 1], op0=ALU.mult, op1=ALU.add),
                             [tAG, tIG, tCARH], [tHS])
                        CP(dve, CARH[:, c:c + 1], HS[:, c, TT - 1:TT], [tHS], [tCARH])
                        yield
                    ACT(RG[:, :, :], PSALL[:, b0_:b0_ + 4, :], AF.Gelu_apprx_tanh, tPB + [tRG], [tRG])
                    yield
                    TTn(dve, HS[:, :, :], HS[:, :, :], RG[:, :, :], ALU.mult, [tHS, tRG], [tHS])
                    yield
                    om, tom = OMX[p], tOMX[p]
                    rmsnorm([HS[:, c, :] for c in range(4)], [128] * 4,
                            [VL[:, C_GOUTL + c:C_GOUTL + c + 1] for c in range(4)], 512.0,
                            [om[:, c, :] for c in range(4)], Bp["SQl"], Tp["SQl"], Bp["RSl"], Tp["RSl"], Bp["RSTDl"],
                            Tp["RSTDl"], b0_, [tHS], [tom], sq_eng=(HS[:, :, :], Bp["SQl"][:, :, :]))
                    yield
                    k.dma(sp, omv[:, 4:8, cs], om[:, :, :], [tom], [tOM[2][t]], d_st[p])

                ld5(0)
                ld5(1)
                gens = [lru_tile(t) for t in range(NT)]
                active = []
                gi = 0
                while gi < len(gens) and len(active) < 2:
                    active.append(gens[gi])
                    gi += 1
                while active:
                    for g in list(active):
                        try:
                            next(g)
                        except StopIteration:
                            active.remove(g)
                            if gi < len(gens):
                                active.append(gens[gi])
                                gi += 1
                k.barrier()
                k.release_dsems(keep_ds)
            if stop_after == f"p5_{l}":
                return nc

            with contextlib.ExitStack() as s1:
                WO = sb(s1, "WO", [128, 8, 1024], BF16)
                OMb = [sb(s1, f"OMt{i}", [128, 8, TT], BF16) for i in range(2)]
                HTb = [sb(s1, f"HT6{i}", [128, 8, TT], F32) for i in range(3)]
                XN2 = sb(s1, "XN2", [128, 8, TT], BF16)
                SQ = sb(s1, "SQ6", [128, 8, TT], BF16)
                RS = sb(s1, "RS6", [128, TT], F32)
                RSTD = sb(s1, "RSTD6", [128, TT], F32)
                tOMb = [_Tile("OMt0"), _Tile("OMt1")]
                tHTb = [_Tile("HT60"), _Tile("HT61"), _Tile("HT62")]
                tXN2, tSQ, tRS, tRSTD, tW = (_Tile(n) for n in ("XN2", "SQ6", "RS6", "RSTD6", "W6"))
                d_o = [k.dsem("p6o0"), k.dsem("p6o1")]
                d_h = [k.dsem("p6h0"), k.dsem("p6h1"), k.dsem("p6h2")]
                d_xs = k.dsem("p6xs")
                k.dma(pool, WO[:], w_o[l].rearrange("(c p) n -> p c n", p=128), [], [tW], d_w[3])

                def ld6(t):
                    i = t % 2
                    ih = t % 3
                    k.dma(sp, OMb[i][:], omv[:, :, cols(t)], [tOM[0][t], tOM[1][t], tOM[2][t]], [tOMb[i]], d_o[i])
                    k.dma(sp, HTb[ih][:], hsrc[:, :, cols(t)], [tHs[t]], [tHTb[ih]], d_h[ih])

                def norm6(t):
                    HT, tHT = HTb[t % 3], tHTb[t % 3]
                    rmsnorm([HT[:, c, :] for c in range(8)], [128] * 8,
                            [VL[:, C_GFFN + c:C_GFFN + c + 1] for c in range(8)], 1024.0,
                            [XN2[:, c, :] for c in range(8)], SQ, tSQ, RS, tRS, RSTD, tRSTD, 7, [tHT], [tXN2],
                            sq_eng=(HT[:, :, :], SQ[:, :, :]))
                    k.dma(sp, xnv[:, :, cols(t)], XN2[:], [tXN2], [tXN[t]], d_xs)
                ld6(0)
                for t in range(NT + 1):
                    if t < NT:
                        i = t % 2
                        cs = cols(t)
                        if t + 1 < NT:
                            ld6(t + 1)
                        HT, tHT = HTb[t % 3], tHTb[t % 3]
                        for m in range(8):
                            pb = m % 4
                            for c in range(8):
                                MM(PSB[pb][:, :], WO[:, c, m * 128:(m + 1) * 128], OMb[i][:, c, :], c == 0, c == 7,
                                   [tW, tOMb[i]], [tPS[pb]])
                            TTn(dve, HT[:, m, :], PSB[pb][:, :], HT[:, m, :], ALU.add, [tPS[pb], tHT], [tHT])
                        k.dma(sp, hv_a[:, :, cs], HT[:], [tHT], [tH[t]], d_h[t % 3])
                    if t >= 1:
                        norm6(t - 1)
                k.barrier()
                k.release_dsems(keep_ds)
            if stop_after == f"p6_{l}":
                return nc

            wup = w_up[l].rearrange("(c p) n -> p c n", p=128)
            wdn = w_down[l].rearrange("(j p) n -> p j n", p=128)
            last = (l == DEPTH - 1)
            for half in range(2):
                with contextlib.ExitStack() as s1:
                    WUG = sb(s1, "WUG", [128, 8, 11 * 128], BF16)
                    WUV = sb(s1, "WUV", [128, 8, 11 * 128], BF16)
                    WD = sb(s1, "WD", [128, 11, 1024], BF16)
                    XNb = [sb(s1, f"XN7{i}", [128, 8, TT], BF16) for i in range(2)]
                    tXNb = [_Tile("XN70"), _Tile("XN71")]
                    nhb = 2
                    HTb = [sb(s1, f"HT7{i}", [128, 8, TT], F32) for i in range(nhb)]
                    tHTb = [_Tile(f"HT7{i}") for i in range(nhb)]
                    ATb = [sb(s1, f"AT{i}", [128, 11, TT], BF16) for i in range(2)]
                    tATb = [_Tile("AT0"), _Tile("AT1")]
                    ACG = [sb(s1, f"ACG{i}", [128, TT], F32) for i in range(2)]
                    ACV = [sb(s1, f"ACV{i}", [128, TT], F32) for i in range(2)]
                    SG = [sb(s1, f"SG{i}", [128, TT], F32) for i in range(2)]
                    TAILb = [sb(s1, f"TAIL{i}", [128, 22, 2], F32) for i in range(2)]
                    tTAILb = [_Tile("TAIL0"), _Tile("TAIL1")]
                    tW = _Tile("W7")
                    tWs = [_Tile("W7a"), _Tile("W7b"), _Tile("W7c"), _Tile("W7d"), _Tile("W7e")]
                    tACG = [_Tile("ACG0"), _Tile("ACG1")]
                    tACV = [_Tile("ACV0"), _Tile("ACV1")]
                    tSG = [_Tile("SG0"), _Tile("SG1")]
                    d_x = [k.dsem(f"p7x0_{half}"), k.dsem(f"p7x1_{half}")]
                    d_h = [k.dsem(f"p7h{i}_{half}") for i in range(nhb)]
                    j0 = half * 11
                    k.dma(pool, WUG[:], wup[:, :, j0 * 128:(j0 + 11) * 128], [], [tWs[0]], d_w[4 + half])
                    k.dma(pool, WUV[:], wup[:, :, 2816 + j0 * 128:2816 + (j0 + 11) * 128], [], [tWs[1]], d_w[7 + half])
                    k.dma(pool, WD[:], wdn[:, j0:j0 + 11, :], [], [tWs[2]], d_w[9 + half])
                    MEMSET(dve, TAILb[0][:], 0.0, [tTAILb[0]])
                    if half == 1:
                        WG = sb(s1, "WG", [128, 8, 1024], BF16)
                        WP = sb(s1, "WP", [128, 2, 1024], BF16)
                        XN3 = sb(s1, "XN3", [128, 8, TT], BF16)
                        SQ = sb(s1, "SQ8", [128, 8, TT], BF16)
                        RS = sb(s1, "RS8", [128, TT], F32)
                        RSTD = sb(s1, "RSTD8", [128, TT], F32)
                        PTb = [sb(s1, f"PTp{i}", [128, 2, TT], BF16) for i in range(2)]
                        SGT = sb(s1, "SGT", [128, TT], F32)
                        TMP = sb(s1, "TMP", [128, TT], F32)
                        tXN3, tSQ, tRS, tRSTD, tSGT, tTMP = (_Tile(n) for n in ("XN3", "SQ8", "RS8", "RSTD8", "SGT", "TMP"))
                        tPTb = [_Tile("PTp0"), _Tile("PTp1")]
                        d_p = [k.dsem("p8p0"), k.dsem("p8p1")]
                        k.dma(pool, WG[:], w_pg[l].rearrange("(c p) n -> p c n", p=128), [], [tWs[3]], d_w[6])
                        k.dma(pool, WP[:], w_pp[l].rearrange("(c p) n -> p c n", p=128), [], [tWs[4]], d_w[11])
                        ptv = pT[l].rearrange("(c p) t -> p c t", p=128)

                    def ld7(t):
                        i = t % 2
                        k.dma(sp, XNb[i][:], xnv[:, :, cols(t)], [tXN[t]], [tXNb[i]], d_x[i])
                        ih = t % nhb
                        k.dma(sp, HTb[ih][:], hv_a[:, :, cols(t)], [tH[t]], [tHTb[ih]], d_h[ih])
                        if half == 1:
                            k.dma(pool, PTb[i][:], ptv[:, :, cols(t)], [], [tPTb[i]], d_p[i])
                    ld7(0)
                    cidx = 0
                    for t in range(NT):
                        i = t % 2
                        cs = cols(t)
                        if t + 1 < NT and nhb == 2:
                            ld7(t + 1)
                        XN, tXNs = XNb[i], tXNb[i]
                        HT, tHT = HTb[t % nhb], tHTb[t % nhb]
                        AT, tAT = ATb[i], tATb[i]
                        TAIL, tTAIL = TAILb[i], tTAILb[i]
                        TAILn, tTAILn = TAILb[1 - i], tTAILb[1 - i]
                        for jj in range(11):
                            j = j0 + jj
                            bi = cidx % 2
                            cidx += 1
                            for (which, Wt, pb, ACC, tACC, ch) in ((0, WUG, 2 * bi, ACG[bi], tACG[bi], j),
                                                                   (1, WUV, 2 * bi + 1, ACV[bi], tACV[bi], 22 + j)):
                                for c in range(8):
                                    MM(PSB[pb][:, :], Wt[:, c, jj * 128:(jj + 1) * 128], XN[:, c, :], c == 0, c == 7,
                                       [tWs[which], tXNs], [tPS[pb]])
                                w0 = VL[:, C_FCW + ch:C_FCW + ch + 1]
                                w1 = VL[:, C_FCW + 44 + ch:C_FCW + 44 + ch + 1]
                                w2 = VL[:, C_FCW + 88 + ch:C_FCW + 88 + ch + 1]
                                bb = VL[:, C_FCB + ch:C_FCB + ch + 1]
                                tl = jj * 2 + which
                                ACT(ACC[:, :], PSB[pb][:, :], AF.Identity, [tPS[pb], tVL], [tACC], bias=bb, scale=w2)
                                STT(dve, ACC[:, 1:TT], PSB[pb][:, 0:TT - 1], w1, ACC[:, 1:TT], ALU.mult, ALU.add,
                                    [tPS[pb], tACC, tVL], [tACC])
                                STT(dve, ACC[:, 2:TT], PSB[pb][:, 0:TT - 2], w0, ACC[:, 2:TT], ALU.mult, ALU.add,
                                    [tPS[pb], tACC, tVL], [tACC])
                                STT(dve, ACC[:, 0:2], TAIL[:, tl, 0:2], w0, ACC[:, 0:2], ALU.mult, ALU.add,
                                    [tTAIL, tACC, tVL], [tACC])
                                STT(dve, ACC[:, 0:1], TAIL[:, tl, 1:2], w1, ACC[:, 0:1], ALU.mult, ALU.add,
                                    [tTAIL, tACC, tVL], [tACC])
                                CP(act, TAILn[:, tl, 0:2], PSB[pb][:, TT - 2:TT], [tPS[pb]], [tTAILn])
                            ACT(SG[bi][:, :], ACG[bi][:, :], AF.Silu, [tACG[bi]], [tSG[bi]])
                            TTn(pool, AT[:, jj, :], SG[bi][:, :], ACV[bi][:, :], ALU.mult, [tSG[bi], tACV[bi]], [tAT])
                        for m in range(8):
                            pb = 4 + (m % 2)
                            for jj in range(11):
                                MM(PSB[pb][:, :], WD[:, jj, m * 128:(m + 1) * 128], AT[:, jj, :], jj == 0, jj == 10,
                                   [tWs[2], tAT], [tPS[pb]])
                            TTn(dve, HT[:, m, :], PSB[pb][:, :], HT[:, m, :], ALU.add, [tPS[pb], tHT], [tHT])
                        if half == 0:
                            k.dma(sp, hv_a[:, :, cs], HT[:], [tHT], [tH[t]], d_h[t % nhb])
                            continue
                        rmsnorm([HT[:, c, :] for c in range(8)], [128] * 8,
                                [VL[:, C_GPLE + c:C_GPLE + c + 1] for c in range(8)], 1024.0,
                                [XN3[:, c, :] for c in range(8)], SQ, tSQ, RS, tRS, RSTD, tRSTD, 7, [tHT], [tXN3])
                        for m in range(8):
                            pg, pp = (4, 5) if m % 2 == 0 else (6, 7)
                            for c in range(8):
                                MM(PSB[pg][:, :], WG[:, c, m * 128:(m + 1) * 128], XN3[:, c, :], c == 0, c == 7,
                                   [tWs[3], tXN3], [tPS[pg]])
                            for c in range(2):
                                MM(PSB[pp][:, :], WP[:, c, m * 128:(m + 1) * 128], PTb[i][:, c, :], c == 0, c == 1,
                                   [tWs[4], tPTb[i]], [tPS[pp]])
                            ACT(SGT[:, :], PSB[pg][:, :], AF.Sigmoid, [tPS[pg]], [tSGT])
                            TTn(dve, TMP[:, :], PSB[pp][:, :], SGT[:, :], ALU.mult, [tPS[pp], tSGT], [tTMP])
                            TTn(pool, HT[:, m, :], HT[:, m, :], TMP[:, :], ALU.add, [tHT, tTMP], [tHT])
                        if last:
                            rmsnorm([HT[:, c, :] for c in range(8)], [128] * 8,
                                    [VL[:, C_GFIN + c:C_GFIN + c + 1] for c in range(8)], 1024.0,
                                    [HT[:, c, :] for c in range(8)], SQ, tSQ, RS, tRS, RSTD, tRSTD, 7, [tHT], [tHT])
                            k.dma(sp, hv_o[:, :, cs], HT[:], [tHT], [tNONE], d_h[t % nhb])
                        else:
                            k.dma(sp, hv_a[:, :, cs], HT[:], [tHT], [tH[t]], d_h[t % nhb])
                    k.barrier()
                    k.release_dsems(keep_ds)
                if stop_after == f"p{7 + half}_{l}":
                    return nc
        k.barrier()
    return nc


def _vec_table(inp, l):
    V = np.zeros((128, NV), np.float32)

    def chunks(v, n, c0, rows=128):
        v = np.asarray(v, np.float32).reshape(n, rows)
        V[:rows, c0:c0 + n] = v.T

    chunks(inp["g_mix"][l], 8, C_GMIX)
    chunks(inp["g_ffn"][l], 8, C_GFFN)
    chunks(inp["g_ple"][l], 8, C_GPLE)
    gq = np.asarray(inp["g_qc"][l], np.float32)
    V[:, C_GQC] = gq[0:128]
    V[0:64, C_GQC + 1] = gq[128:192]
    V[:, C_GKVC] = np.asarray(inp["g_kvc"][l], np.float32)
    chunks(inp["g_out"][l], 16, C_GOUT64, rows=64)
    chunks(np.asarray(inp["g_out"][l])[512:], 4, C_GOUTL)
    V[0:4, C_BF] = np.asarray(inp["b_f"][l], np.float32)
    lcw = np.asarray(inp["lru_conv_w"][l], np.float32)
    for kk in range(4):
        chunks(lcw[kk], 4, C_LCW + kk * 4)
    chunks(inp["lru_conv_b"][l], 4, C_LCB)
    chunks(inp["b_r"][l], 4, C_BR)
    chunks(inp["b_i"][l], 4, C_BI)
    chunks(inp["lru_lambda"][l], 4, C_LAM)
    fcw = np.asarray(inp["ffn_conv_w"][l], np.float32)
    for kk in range(3):
        chunks(fcw[kk], 44, C_FCW + kk * 44)
    chunks(inp["ffn_conv_b"][l], 44, C_FCB)
    chunks(inp["g_final"], 8, C_GFIN)
    return V


def _consts():
    ident = np.eye(128, dtype=np.float32)
    kk = np.arange(128)[:, None]
    qq = np.arange(128)[None, :]
    cmask = np.where(kk > qq, -30000.0, 0.0).astype(np.float32)
    sel = np.zeros((128, 8, 70), np.float32)
    for h in range(4):
        sel[96, h, 64:67] = 8.0
        sel[0 + h, h, 67] = 8.0
        sel[32 + h, h, 68] = 8.0
        sel[64 + h, h, 69] = 8.0
        sel[0 + h, 4 + h, 64] = -1.0
        sel[32 + h, 4 + h, 65] = -1.0
        sel[64 + h, 4 + h, 66] = -1.0
        sel[96, 4 + h, 67:70] = 1.0
    freq = np.zeros((128, 2), np.float32)
    half = 16
    f = (np.float32(10000.0) ** (-np.arange(half, dtype=np.float32) / np.float32(half))).astype(np.float32)
    freq[64:80, 0] = f
    freq[80:96, 0] = f
    freq[64:80, 1] = -2.0 * math.pi
    freq[80:96, 1] = 2.0 * math.pi
    return ident, cmask, sel, freq


_NC_CACHE = {}


def make_in_maps(inputs, cores):
    f32 = lambda a: np.ascontiguousarray(np.asarray(a), dtype=np.float32)
    ident, cmask, sel, freq = _consts()
    vec = np.stack([_vec_table(inputs, l) for l in range(DEPTH)])
    shared = {
        "w_in": f32(inputs["w_in"]), "w_uq": f32(inputs["w_uq"]), "w_ukv": f32(inputs["w_ukv"]),
        "w_r": f32(inputs["w_r"]), "w_i": f32(inputs["w_i"]), "w_o": f32(inputs["w_o"]),
        "w_up": f32(inputs["w_up"]), "w_down": f32(inputs["w_down"]),
        "w_ple_gate": f32(inputs["w_ple_gate"]), "w_ple_proj": f32(inputs["w_ple_proj"]),
        "vecs": vec, "cident": ident, "cmask": cmask, "csel": sel, "cfreq": freq,
    }
    x = np.asarray(inputs["x"])
    p = np.asarray(inputs["p"])
    pos = np.asarray(inputs["positions"])
    maps = []
    for b in cores:
        m = dict(shared)
        m["xT"] = np.ascontiguousarray(x[b].T, dtype=np.float32)
        m["pT"] = np.ascontiguousarray(np.transpose(p[:, b], (0, 2, 1)), dtype=np.float32)
        m["pos"] = np.ascontiguousarray(np.broadcast_to(pos[b].astype(np.int32)[None, :], (128, S)))
        maps.append(m)
    return maps


def kernel(**inputs):
    if "nc" not in _NC_CACHE:
        _NC_CACHE["nc"] = build_nc()
    nc = _NC_CACHE["nc"]
    B = np.asarray(inputs["x"]).shape[0]
    maps = make_in_maps(inputs, list(range(B)))
    res = run_bass_kernel_spmd(nc, maps, core_ids=list(range(B)))
    out = np.stack([np.ascontiguousarray(r["outT"].T) for r in res.results], axis=0)
    return out.astype(np.float32)
```
